# Optimizing a Trainium2 kernel written in Bass

```python
import math
import jax
import jax.numpy as jnp
from jax import lax
import numpy as np

D_MODEL = 1024
BATCH = 32
SEQ = 2048
DEPTH = 2

GRID_W = 64
CTX_LEN = 256
F32 = jnp.float32
NORM_EPS = 1e-6
HG_HEADS = 4
HG_HEAD_DIM = 128
HG_WIDTH = HG_HEADS * HG_HEAD_DIM
HG_CHUNK = 32
HG_EXP_CLIP = 30.0
AT_HEADS = 8
AT_KV_HEADS = 2
AT_HEAD_DIM = 64
AT_GROUP = AT_HEADS // AT_KV_HEADS
AT_Q_BLOCK = 128
ROPE_THETA = 10000.0
HY_WIDTH = 512
HY_EMB_DIM = 33
HY_BANDS = (HY_EMB_DIM - 1) // 2
HY_FILTER_WIDTH = 64
HY_INNER = 2
HY_FAST_DECAY = 0.3
HY_SLOW_DECAY = 1.5
HY_TARGET = 1e-2
D_FF = 2816
SPLIT_SIZES = (HG_WIDTH, HG_WIDTH, HG_WIDTH, HG_WIDTH, HG_WIDTH,
               AT_HEADS * AT_HEAD_DIM, AT_KV_HEADS * AT_HEAD_DIM, AT_KV_HEADS * AT_HEAD_DIM,
               3 * HY_WIDTH, 3 * D_MODEL)
D_IN = sum(SPLIT_SIZES)
SPLIT_POINTS = tuple(sum(SPLIT_SIZES[:i + 1]) for i in range(len(SPLIT_SIZES) - 1))

kernel_name = "hybrid_dit_hgrn2_gqa_hyena_ctxprefix"


def rmsnorm(x, g):
    xf = x.astype(F32)
    y = xf * lax.rsqrt(jnp.mean(xf * xf, axis=-1, keepdims=True) + NORM_EPS)
    return (y * g.astype(F32)).astype(x.dtype)


def modulate(x, g, shift, scale):
    return rmsnorm(x, g) * (1 + scale) + shift


def dwconv3(x, w, b):
    xp = jnp.pad(x, ((0, 0), (1, 1), (0, 0)))
    return xp[:, :-2] * w[0] + xp[:, 1:-1] * w[1] + xp[:, 2:] * w[2] + b


def rev(a):
    return a[:, ::-1]


def hg_heads(z):
    return z.astype(F32).reshape(z.shape[:2] + (HG_HEADS, HG_HEAD_DIM))


def hgrn_decay(zf, lb):
    zf = hg_heads(zf)
    lb = lb.reshape(HG_HEADS, HG_HEAD_DIM)
    log_f = jax.nn.log_sigmoid(zf) + jnp.log1p(lb * jnp.exp(jnp.minimum(-zf, HG_EXP_CLIP)))
    k = (1.0 - lb) * jax.nn.sigmoid(-zf)
    return log_f, k


def hgrn_chunk_scan(q, log_f, k, v, s0):
    B, L, H, _ = q.shape
    n = L // HG_CHUNK
    mask = jnp.tril(jnp.ones((HG_CHUNK, HG_CHUNK), dtype=bool))[:, :, None]

    def to_chunks(a):
        return a.reshape(B, n, HG_CHUNK, H, a.shape[-1]).transpose(1, 0, 3, 2, 4)

    def step(S, inp):
        qc, lfc, kc, vc = inp
        G = jnp.cumsum(lfc, axis=2)
        diff = G[:, :, :, None, :] - G[:, :, None, :, :]
        decay = jnp.where(mask, jnp.exp(jnp.where(mask, diff, 0.0)), 0.0)
        A = jnp.einsum('bhtk,bhtsk,bhsk->bhts', qc, decay, kc)
        o = jnp.einsum('bhts,bhsv->bhtv', A, vc) + jnp.einsum('bhtk,bhkv->bhtv', qc * jnp.exp(G), S)
        G_last = G[:, :, -1:, :]
        S = jnp.exp(G_last[:, :, 0, :])[..., None] * S + jnp.einsum('bhsk,bhsv->bhkv', kc * jnp.exp(G_last - G), vc)
        return S, o

    S, o = lax.scan(step, s0, (to_chunks(q), to_chunks(log_f), to_chunks(k), to_chunks(v)))
    o = o.transpose(1, 0, 3, 2, 4).reshape(B, L, H, v.shape[-1])
    return o, S


def hgrn_final_state(log_f, k, v):
    rest = lax.cumsum(log_f, axis=1, reverse=True) - log_f
    return jnp.einsum('blhk,blhv->bhkv', k * jnp.exp(rest), v)


def hgrn_out(o, zg, g_norm, dtype):
    B, L = o.shape[:2]
    return (rmsnorm(o, g_norm).reshape(B, L, HG_WIDTH) * jax.nn.silu(zg.astype(F32))).astype(dtype)


def axial_rope(L):
    rows = L // GRID_W
    row = jnp.repeat(jnp.arange(rows), GRID_W).astype(F32)
    col = jnp.tile(jnp.arange(GRID_W), rows).astype(F32)
    n_freq = AT_HEAD_DIM // 4
    inv = ROPE_THETA ** (-jnp.arange(n_freq, dtype=F32) / n_freq)
    ang = jnp.concatenate([row[:, None] * inv, col[:, None] * inv], axis=-1)
    return jnp.cos(ang), jnp.sin(ang)


def apply_rope(x, cos, sin):
    xf = x.astype(F32).reshape(x.shape[:-1] + (AT_HEAD_DIM // 2, 2))
    x1, x2 = xf[..., 0], xf[..., 1]
    c = cos[None, :, None, :]
    s = sin[None, :, None, :]
    out = jnp.stack([x1 * c - x2 * s, x1 * s + x2 * c], axis=-1)
    return out.reshape(x.shape).astype(x.dtype)


def attn_heads(zq, zk, zv, q_g, k_g):
    B, L = zq.shape[:2]
    q = rmsnorm(zq.reshape(B, L, AT_HEADS, AT_HEAD_DIM), q_g)
    k = rmsnorm(zk.reshape(B, L, AT_KV_HEADS, AT_HEAD_DIM), k_g)
    v = zv.reshape(B, L, AT_KV_HEADS, AT_HEAD_DIM)
    return q, k, v


def gqa_softmax(q, k, v):
    s = jnp.einsum('bqhgd,bkhd->bhgqk', q, k).astype(F32) * (AT_HEAD_DIM ** -0.5)
    p = jax.nn.softmax(s, axis=-1).astype(v.dtype)
    return jnp.einsum('bhgqk,bkhd->bqhgd', p, v)


def attend_latent(q, k_all, v_all):
    B, L = q.shape[:2]
    nb = L // AT_Q_BLOCK
    qb = q.reshape(B, nb, AT_Q_BLOCK, AT_KV_HEADS, AT_GROUP, AT_HEAD_DIM).transpose(1, 0, 2, 3, 4, 5)
    out = lax.map(lambda qblk: gqa_softmax(qblk, k_all, v_all), qb)
    return out.transpose(1, 0, 2, 3, 4, 5).reshape(B, L, AT_HEADS * AT_HEAD_DIM)


def hyena_filters(L, w1, b1, wi, bi, freq, w_last):
    t = jnp.linspace(0.0, 1.0, L, dtype=F32)[:, None]
    w = 2.0 * math.pi * jnp.arange(L, dtype=F32)[:, None] / L
    f = jnp.linspace(1e-4, HY_BANDS - 1, HY_BANDS, dtype=F32)[None, :]
    z = jnp.concatenate([t, jnp.cos(f * w), -jnp.sin(f * w)], axis=-1)
    fr = freq.astype(F32)
    h = jnp.sin(fr * (z @ w1.astype(F32) + b1.astype(F32)))
    for j in range(HY_INNER):
        h = jnp.sin(fr * (h @ wi[j].astype(F32) + bi[j].astype(F32)))
    h = h @ w_last.astype(F32)
    max_decay = math.log(HY_TARGET) / HY_FAST_DECAY
    min_decay = math.log(HY_TARGET) / HY_SLOW_DECAY
    deltas = jnp.abs(jnp.linspace(min_decay, max_decay, HY_WIDTH, dtype=F32))
    decay = jnp.exp(-t * deltas)
    return h[:, :HY_WIDTH] * decay, h[:, HY_WIDTH:] * decay


def bidir_fftconv(u, h_f, h_b):
    L = u.shape[1]
    kern = jnp.concatenate([(h_f[0] + h_b[0])[None], h_f[1:], jnp.zeros_like(h_f[:1]), h_b[:0:-1]], axis=0)
    U = jnp.fft.rfft(u, n=2 * L, axis=1)
    K = jnp.fft.rfft(kern, axis=0)
    return jnp.fft.irfft(U * K[None], n=2 * L, axis=1)[:, :L]


def hyena(z, conv_w, conv_b, filt, d_bias):
    zc = dwconv3(z, conv_w, conv_b).astype(F32)
    x0, x1, v = jnp.split(zc, 3, axis=-1)
    h_f, h_b = filt
    u = v * x1
    y = bidir_fftconv(u, h_f, h_b) + u * d_bias.astype(F32)
    return (y * x0).astype(z.dtype)


def mixer(hx, hc, p, lb_f, lb_b, need_ctx):
    B, L, _ = hx.shape
    Lc = hc.shape[1]
    dt = hx.dtype
    px = jnp.split(hx @ p['w_in'], SPLIT_POINTS, axis=-1)
    pc = jnp.split(hc @ p['w_in'], SPLIT_POINTS, axis=-1)

    qx, vx, qc, vc = hg_heads(px[0]), hg_heads(px[3]), hg_heads(pc[0]), hg_heads(pc[3])
    lfx_f, kx_f = hgrn_decay(px[1], lb_f)
    lfx_b, kx_b = hgrn_decay(px[2], lb_b)
    lfc_f, kc_f = hgrn_decay(pc[1], lb_f)
    lfc_b, kc_b = hgrn_decay(pc[2], lb_b)
    if need_ctx:
        s0 = jnp.zeros((B, HG_HEADS, HG_HEAD_DIM, HG_HEAD_DIM), F32)
        oc_f, sf = hgrn_chunk_scan(qc, lfc_f, kc_f, vc, s0)
        oc_b, sb = hgrn_chunk_scan(rev(qc), rev(lfc_b), rev(kc_b), rev(vc), s0)
        a_c = hgrn_out(oc_f + rev(oc_b), pc[4], p['hg_norm'], dt)
    else:
        sf = hgrn_final_state(lfc_f, kc_f, vc)
        sb = hgrn_final_state(rev(lfc_b), rev(kc_b), rev(vc))
    ox_f, _ = hgrn_chunk_scan(qx, lfx_f, kx_f, vx, sf)
    ox_b, _ = hgrn_chunk_scan(rev(qx), rev(lfx_b), rev(kx_b), rev(vx), sb)
    a_x = hgrn_out(ox_f + rev(ox_b), px[4], p['hg_norm'], dt)

    cos, sin = axial_rope(L)
    aq_x, ak_x, av_x = attn_heads(px[5], px[6], px[7], p['q_norm'], p['k_norm'])
    aq_c, ak_c, av_c = attn_heads(pc[5], pc[6], pc[7], p['q_norm'], p['k_norm'])
    aq_x = apply_rope(aq_x, cos, sin)
    ak_x = apply_rope(ak_x, cos, sin)
    k_all = jnp.concatenate([ak_c, ak_x], axis=1)
    v_all = jnp.concatenate([av_c, av_x], axis=1)
    b_x = attend_latent(aq_x, k_all, v_all)

    filt_args = (p['hy_w1'], p['hy_b1'], p['hy_wi'], p['hy_bi'], p['hy_freq'], p['hy_w_last'])
    c_x = hyena(px[8], p['hy_conv_w'], p['hy_conv_b'], hyena_filters(L, *filt_args), p['hy_bias'])

    def merge(parts, a, b, c):
        g_a, g_b, g_c = jnp.split(parts[9], 3, axis=-1)
        m = (jax.nn.sigmoid(g_a) * (a @ p['w_oa']) + jax.nn.sigmoid(g_b) * (b @ p['w_ob'])
             + jax.nn.sigmoid(g_c) * (c @ p['w_oc']))
        return m @ p['w_out']

    yx = merge(px, a_x, b_x, c_x)
    if need_ctx:
        b_c = gqa_softmax(aq_c.reshape(B, Lc, AT_KV_HEADS, AT_GROUP, AT_HEAD_DIM), ak_c, av_c).reshape(B, Lc, AT_HEADS * AT_HEAD_DIM)
        c_c = hyena(pc[8], p['hy_conv_w'], p['hy_conv_b'], hyena_filters(Lc, *filt_args), p['hy_bias'])
        return yx, merge(pc, a_c, b_c, c_c)
    return yx, None


def conv_ffn(h, w_up, cw, cb, w_down):
    u = dwconv3(h @ w_up, cw, cb)
    a, b = jnp.split(u, 2, axis=-1)
    return (jax.nn.silu(a) * b) @ w_down


def setup_inputs(seed: int = 0) -> dict:
    key = jax.random.key(seed)
    D = D_MODEL
    specs = [
        ('x', (BATCH, SEQ, D), 1.0, 0.0),
        ('c', (BATCH, D), 1.0, 0.0),
        ('ctx', (BATCH, CTX_LEN, D), 1.0, 0.0),
        ('c_ctx', (D,), 1.0, 0.0),
        ('w_ada', (DEPTH, D, 6 * D), 0.5 * D ** -0.5, 0.0),
        ('b_ada', (DEPTH, 6 * D), 0.02, 0.0),
        ('g_pre_mix', (DEPTH, D), 0.02, 1.0),
        ('g_post_mix', (DEPTH, D), 0.02, 1.0),
        ('g_pre_ffn', (DEPTH, D), 0.02, 1.0),
        ('g_post_ffn', (DEPTH, D), 0.02, 1.0),
        ('w_in', (DEPTH, D, D_IN), D ** -0.5, 0.0),
        ('hg_lower_bounds', (DEPTH, 2, HG_WIDTH), 0.1, 0.0),
        ('hg_norm', (DEPTH, HG_HEAD_DIM), 0.02, 1.0),
        ('q_norm', (DEPTH, AT_HEAD_DIM), 0.02, 1.0),
        ('k_norm', (DEPTH, AT_HEAD_DIM), 0.02, 1.0),
        ('hy_conv_w', (DEPTH, 3, 3 * HY_WIDTH), 0.5, 0.0),
        ('hy_conv_b', (DEPTH, 3 * HY_WIDTH), 0.02, 0.0),
        ('hy_w1', (DEPTH, HY_EMB_DIM, HY_FILTER_WIDTH), HY_EMB_DIM ** -0.5, 0.0),
        ('hy_b1', (DEPTH, HY_FILTER_WIDTH), 0.1, 0.0),
        ('hy_wi', (DEPTH, HY_INNER, HY_FILTER_WIDTH, HY_FILTER_WIDTH), HY_FILTER_WIDTH ** -0.5, 0.0),
        ('hy_bi', (DEPTH, HY_INNER, HY_FILTER_WIDTH), 0.1, 0.0),
        ('hy_freq', (DEPTH, HY_FILTER_WIDTH), 0.02, 1.0),
        ('hy_w_last', (DEPTH, HY_FILTER_WIDTH, 2 * HY_WIDTH), 0.05 * HY_FILTER_WIDTH ** -0.5, 0.0),
        ('hy_bias', (DEPTH, HY_WIDTH), 1.0, 0.0),
        ('w_oa', (DEPTH, HG_WIDTH, D), HG_WIDTH ** -0.5, 0.0),
        ('w_ob', (DEPTH, AT_HEADS * AT_HEAD_DIM, D), (AT_HEADS * AT_HEAD_DIM) ** -0.5, 0.0),
        ('w_oc', (DEPTH, HY_WIDTH, D), HY_WIDTH ** -0.5, 0.0),
        ('w_out', (DEPTH, D, D), D ** -0.5, 0.0),
        ('w_up', (DEPTH, D, 2 * D_FF), D ** -0.5, 0.0),
        ('ffn_conv_w', (DEPTH, 3, 2 * D_FF), 0.5, 0.0),
        ('ffn_conv_b', (DEPTH, 2 * D_FF), 0.02, 0.0),
        ('w_down', (DEPTH, D_FF, D), D_FF ** -0.5, 0.0),
    ]
    keys = jax.random.split(key, len(specs))
    return {name: off + scale * jax.random.normal(k, shape, jnp.float32)
            for (name, shape, scale, off), k in zip(specs, keys)}


def reference(x, c, ctx, c_ctx, w_ada, b_ada, g_pre_mix, g_post_mix, g_pre_ffn, g_post_ffn,
              w_in, hg_lower_bounds, hg_norm, q_norm, k_norm, hy_conv_w, hy_conv_b, hy_w1, hy_b1,
              hy_wi, hy_bi, hy_freq, hy_w_last, hy_bias, w_oa, w_ob, w_oc, w_out,
              w_up, ffn_conv_w, ffn_conv_b, w_down):
    lbp = jax.nn.softmax(hg_lower_bounds.astype(F32), axis=0)
    lower = jnp.cumsum(lbp, axis=0) - lbp[0]
    src_x = jax.nn.silu(c)
    src_c = jax.nn.silu(c_ctx)
    for l in range(DEPTH):
        need_ctx = l < DEPTH - 1
        mx = jnp.split((src_x @ w_ada[l] + b_ada[l])[:, None, :], 6, axis=-1)
        mc = jnp.split((src_c @ w_ada[l] + b_ada[l])[None, None, :], 6, axis=-1)
        p = {'w_in': w_in[l], 'hg_norm': hg_norm[l], 'q_norm': q_norm[l], 'k_norm': k_norm[l],
             'hy_conv_w': hy_conv_w[l], 'hy_conv_b': hy_conv_b[l], 'hy_w1': hy_w1[l], 'hy_b1': hy_b1[l],
             'hy_wi': hy_wi[l], 'hy_bi': hy_bi[l], 'hy_freq': hy_freq[l], 'hy_w_last': hy_w_last[l],
             'hy_bias': hy_bias[l], 'w_oa': w_oa[l], 'w_ob': w_ob[l], 'w_oc': w_oc[l], 'w_out': w_out[l]}
        hx = modulate(x, g_pre_mix[l], mx[0], mx[1])
        hc = modulate(ctx, g_pre_mix[l], mc[0], mc[1])
        yx, yc = mixer(hx, hc, p, lower[l, 0], lower[l, 1], need_ctx)
        x = x + mx[2] * rmsnorm(yx, g_post_mix[l])
        fx = conv_ffn(modulate(x, g_pre_ffn[l], mx[3], mx[4]), w_up[l], ffn_conv_w[l], ffn_conv_b[l], w_down[l])
        x = x + mx[5] * rmsnorm(fx, g_post_ffn[l])
        if need_ctx:
            ctx = ctx + mc[2] * rmsnorm(yc, g_post_mix[l])
            fc = conv_ffn(modulate(ctx, g_pre_ffn[l], mc[3], mc[4]), w_up[l], ffn_conv_w[l], ffn_conv_b[l], w_down[l])
            ctx = ctx + mc[5] * rmsnorm(fc, g_post_ffn[l])
    return x
```

```python
import numpy as np
import concourse.bass as bass
import concourse.mybir as mybir
from concourse.bass_utils import run_bass_kernel_spmd
from contextlib import ExitStack

F32 = mybir.dt.float32
BF16 = mybir.dt.bfloat16
AF = mybir.ActivationFunctionType
ALU = mybir.AluOpType
AX = mybir.AxisListType


class _Op:
    __slots__ = ("eng", "fn", "deps", "signal", "idx", "is_dma", "dsem", "dtarget", "seq")

    def __init__(self, eng, fn, is_dma=False):
        self.eng = eng
        self.fn = fn
        self.deps = []
        self.signal = False
        self.idx = 0
        self.is_dma = is_dma
        self.dsem = None
        self.dtarget = 0


class Sched:
    ENG = ("pe", "act", "dve", "pool", "sp")

    def __init__(self, nc, es, ndma=10, dma_queues=("sp", "pool", "act")):
        self.nc = nc
        self.es = es
        self.ops = []
        self.last_w = {}
        self.readers = {}
        self.sem = {e: es.enter_context(nc.semaphore("s_" + e)) for e in self.ENG}
        self.dsem = {q: [es.enter_context(nc.semaphore("d_%s%d" % (q, i))) for i in range(ndma)]
                     for q in dma_queues}
        self.dcount = {q: 0 for q in dma_queues}
        self.ndma = ndma
        self.last_dma = {}
        self.last_eng = {}
        self.n = 0

    def sb(self, name, shape, dt):
        self.uid = getattr(self, "uid", 0) + 1
        return self.es.enter_context(self.nc.sbuf_tensor("sb%d_%s" % (self.uid, name), list(shape), dt))

    def ps(self, name, shape, dt):
        self.uid = getattr(self, "uid", 0) + 1
        return self.es.enter_context(self.nc.psum_tensor("ps%d_%s" % (self.uid, name), list(shape), dt))

    def _record(self, op, reads, writes):
        deps = {}
        for t in reads:
            w = self.last_w.get(t)
            if w is not None:
                deps[id(w)] = (w, "raw")
        for t in writes:
            w = self.last_w.get(t)
            if w is not None and id(w) not in deps:
                deps[id(w)] = (w, "waw")
            for r in self.readers.get(t, ()):
                if id(r) not in deps:
                    deps[id(r)] = (r, "war")
        for p, kind in deps.values():
            if p is op:
                continue
            if not p.is_dma and p.eng == op.eng and not op.is_dma:
                if kind != "raw" or op.eng == "pe":
                    continue
            op.deps.append(p)
            if not p.is_dma:
                p.signal = True
        for t in reads:
            lst = self.readers.setdefault(t, [])
            if not op.is_dma:
                lst[:] = [r for r in lst if r.is_dma or r.eng != op.eng]
            lst.append(op)
        for t in writes:
            self.last_w[t] = op
            self.readers[t] = []
        self.ops.append(op)
        if not op.is_dma:
            self.last_eng[op.eng] = op
        self.n += 1
        return op

    def op(self, eng, fn, reads=(), writes=()):
        return self._record(_Op(eng, fn), reads, writes)

    def dma(self, q, out, in_, reads=(), writes=(), **kw):
        o = _Op(q, lambda e: e.dma_start(out=out, in_=in_, **kw), is_dma=True)
        i = self.dcount[q]
        self.dcount[q] += 1
        slot = i % self.ndma
        o.dsem = (q, slot)
        o.dtarget = 16 * (i // self.ndma + 1)
        prev = self.last_dma.get(o.dsem)
        if prev is not None:
            o.deps.append(prev)
        self.last_dma[o.dsem] = o
        return self._record(o, reads, writes)

    def barrier(self):
        lasts = list(self.last_eng.values()) + list(self.last_dma.values())
        for p in lasts:
            if not p.is_dma:
                p.signal = True
        for e in self.ENG:
            o = _Op(e, None)
            o.deps = [p for p in lasts if p.is_dma or p.eng != e]
            self.ops.append(o)
        self.last_w = {}
        self.readers = {}

    def finish(self):
        self.flush()

    def eps_ap(self, like):
        return self.epsT[0:like.shape[0], 0:1]

    def pe(self, fn, r=(), w=()):
        return self.op("pe", fn, r, w)

    def act(self, fn, r=(), w=()):
        return self.op("act", fn, r, w)

    def dve(self, fn, r=(), w=()):
        return self.op("dve", fn, r, w)

    def pool(self, fn, r=(), w=()):
        return self.op("pool", fn, r, w)

    def flush(self):
        self.barrier()
        if not hasattr(self, "_cnt"):
            self._cnt = {e: 0 for e in self.ENG}
            self._known = {e: {} for e in self.ENG}
        per = {e: [] for e in self.ENG}
        for o in self.ops:
            if o.signal:
                self._cnt[o.eng] += 1
                o.idx = self._cnt[o.eng]
            per[o.eng].append(o)
        self.ops = []
        nc = self.nc
        with nc.Block() as block:
            def emit(engname):
                def body(e):
                    known = self._known[engname]
                    for o in per[engname]:
                        need = {}
                        for p in o.deps:
                            if p.is_dma:
                                key, val = p.dsem, p.dtarget
                            else:
                                key, val = p.eng, p.idx
                            if known.get(key, 0) >= val:
                                continue
                            if need.get(key, 0) < val:
                                need[key] = val
                        for key, val in need.items():
                            s = self.dsem[key[0]][key[1]] if isinstance(key, tuple) else self.sem[key]
                            e.wait_ge(s, val)
                            known[key] = val
                        if o.fn is None:
                            continue
                        ins = o.fn(e)
                        if o.is_dma:
                            ins.then_inc(self.dsem[o.dsem[0]][o.dsem[1]], 16)
                        elif o.signal:
                            ins.then_inc(self.sem[engname], 1)
                return body
            block.tensor(emit("pe"))
            block.scalar(emit("act"))
            block.vector(emit("dve"))
            block.gpsimd(emit("pool"))
            block.sync(emit("sp"))


D = 1024
L = 2048
LC = 256
T = L + LC
NT = T // 128
DIN = 7936
DFF = 2816
NCORES = 8
EPS = 1e-6
import math
import ml_dtypes
NPBF = ml_dtypes.bfloat16

C_HQ, C_ZF, C_ZB, C_HV, C_ZG, C_AQ, C_AK, C_AV, C_HY, C_GT = 0, 512, 1024, 1536, 2048, 2560, 3072, 3200, 3328, 4864

TG = [(0, 512), (512, 512), (1024, 512), (1536, 512), (2048, 256)]


def kmajor(w):
    k, n = w.shape
    return np.ascontiguousarray(w.reshape(k // 128, 128, n).transpose(1, 0, 2))


def rep128(v):
    return np.ascontiguousarray(np.broadcast_to(np.asarray(v)[None, :], (128, v.shape[-1])))


def host_constants():
    c = {}
    c["ident"] = np.eye(128, dtype=np.float32).astype(NPBF)
    c["ones"] = np.ones((128, 128), np.float32).astype(NPBF)
    s = np.arange(128)[:, None]
    t = np.arange(128)[None, :]
    same = (s // 32) == (t // 32)
    mf = (same & (s <= t)).astype(np.float32)
    mb = (same & (s >= t)).astype(np.float32)
    c["hgmask"] = np.stack([np.tile(mf, (1, 4)), np.tile(mb, (1, 4))]).astype(NPBF)
    rm = np.zeros((128, 4), np.float32)
    for j in range(4):
        rm[32 * j:32 * j + 32, j] = 1.0
    c["rowmask"] = rm
    rs = np.ones((128, 512), np.float32)
    rs[:, ::32] = 0.0
    c["resetmask"] = rs
    rows = L // 64
    row = np.repeat(np.arange(rows), 64).astype(np.float32)
    col = np.tile(np.arange(64), rows).astype(np.float32)
    inv = (10000.0 ** (-np.arange(16, dtype=np.float32) / 16)).astype(np.float32)
    ang = np.concatenate([row[:, None] * inv, col[:, None] * inv], axis=-1).astype(np.float32)
    c["cos8"] = np.tile(np.cos(ang).astype(np.float32), (1, 8))
    c["sin8"] = np.tile(np.sin(ang).astype(np.float32), (1, 8))
    for Ls, tag in ((L, "x"), (LC, "c")):
        tt = np.linspace(0.0, 1.0, Ls, dtype=np.float32)[:, None]
        w = (2.0 * math.pi * np.arange(Ls, dtype=np.float32)[:, None] / Ls).astype(np.float32)
        f = np.linspace(1e-4, 15, 16, dtype=np.float32)[None, :]
        z = np.concatenate([tt, np.cos(f * w), -np.sin(f * w)], axis=-1).astype(np.float32)
        c["zT" + tag] = np.ascontiguousarray(z.T)
        deltas = np.abs(np.linspace(math.log(1e-2) / 1.5, math.log(1e-2) / 0.3, 512, dtype=np.float32))
        c["decay" + tag] = np.exp(-tt * deltas[None, :]).astype(np.float32)
        N = 2 * Ls
        om = 2.0 * np.pi * (np.arange(Ls, dtype=np.float64) + 0.5) / N
        tj = np.arange(Ls, dtype=np.float64)
        ph = tj[:, None] * om[None, :]
        fwd = np.concatenate([np.cos(ph), -np.sin(ph)], axis=1)
        inv_m = np.concatenate([np.cos(ph).T, -np.sin(ph).T], axis=0) * (2.0 / N)
        c["fwd" + tag] = kmajor(fwd.astype(np.float32)).astype(NPBF)
        c["inv" + tag] = kmajor(inv_m.astype(np.float32)).astype(NPBF)
    return c


def host_layout(inp, ns, core):
    b0 = core * ns
    m = {}
    m["xin"] = np.ascontiguousarray(np.concatenate([inp["ctx"][b0:b0 + ns], inp["x"][b0:b0 + ns]], axis=1))
    cc = np.concatenate([inp["c"][b0:b0 + ns], inp["c_ctx"][None, :]], axis=0)
    m["cT"] = np.ascontiguousarray(cc.reshape(ns + 1, 8, 128).transpose(2, 1, 0))
    for nm in ("w_ada", "w_in", "w_oa", "w_ob", "w_oc", "w_out", "w_up", "w_down"):
        m[nm] = np.stack([kmajor(inp[nm][l]) for l in range(2)])
    m["b_ada"] = np.stack([rep128(inp["b_ada"][l]) for l in range(2)])
    m["g4"] = np.stack([np.stack([rep128(inp[nm][l]) for nm in ("g_pre_mix", "g_post_mix", "g_pre_ffn", "g_post_ffn")])
                        for l in range(2)])
    lb = inp["hg_lower_bounds"]
    m["lbT"] = np.ascontiguousarray(lb.reshape(2, 2, 4, 128).transpose(3, 0, 1, 2))
    m["hgn"] = np.ascontiguousarray(inp["hg_norm"].T)
    m["gq"] = np.stack([rep128(np.tile(inp["q_norm"][l], 8)) for l in range(2)])
    m["gk"] = np.stack([rep128(np.tile(inp["k_norm"][l], 2)) for l in range(2)])
    cw = inp["hy_conv_w"]
    m["hycw"] = np.ascontiguousarray(cw.reshape(2, 3, 12, 128).transpose(3, 0, 2, 1))
    m["hycb"] = np.ascontiguousarray(inp["hy_conv_b"].reshape(2, 12, 128).transpose(2, 0, 1))
    m["hybias"] = np.ascontiguousarray(inp["hy_bias"].reshape(2, 4, 128).transpose(2, 0, 1))
    m["hyw1"] = np.ascontiguousarray(inp["hy_w1"])
    m["hyb1"] = np.ascontiguousarray(inp["hy_b1"].T)
    m["hywi"] = np.ascontiguousarray(inp["hy_wi"])
    m["hybi"] = np.ascontiguousarray(inp["hy_bi"].transpose(2, 0, 1))
    m["hyfreq"] = np.ascontiguousarray(inp["hy_freq"].T)
    m["hywl"] = np.ascontiguousarray(inp["hy_w_last"])
    fw = inp["ffn_conv_w"]
    m["ffcw"] = np.ascontiguousarray(fw.reshape(2, 3, 44, 128).transpose(3, 0, 2, 1))
    m["ffcb"] = np.ascontiguousarray(inp["ffn_conv_b"].reshape(2, 44, 128).transpose(2, 0, 1))
    return m


class Rot:
    def __init__(self, S, name, n, shape, dt, psum=False):
        self.bufs = [(S.ps if psum else S.sb)("%s%d" % (name, i), shape, dt) for i in range(n)]
        self.name = name
        self.i = 0

    def next(self):
        j = self.i % len(self.bufs)
        self.i += 1
        return self.bufs[j], (self.name, j)


class Ctx:
    pass


def bc(ap, shape):
    return ap.to_broadcast(list(shape))


def declare(nc, ns, consts, layout):
    d = {}
    for k, v in list(consts.items()) + list(layout.items()):
        dt = BF16 if v.dtype == NPBF else F32
        d[k] = nc.dram_tensor(k, list(v.shape), dt, kind="ExternalInput").ap()
    d["out"] = nc.dram_tensor("out", [ns, L, D], F32, kind="ExternalOutput").ap()
    d["xs"] = nc.dram_tensor("xs", [ns, T, D], F32, kind="Internal").ap()
    d["modrep"] = nc.dram_tensor("modrep", [2, ns + 1, 128, 6 * D], F32, kind="Internal").ap()
    d["kspx"] = nc.dram_tensor("kspx", [2, 128, 32, 512], F32, kind="Internal").ap()
    d["kspc"] = nc.dram_tensor("kspc", [128, 4, 512], F32, kind="Internal").ap()
    for nm in ("aTs", "bTs", "cTs"):
        d[nm] = nc.dram_tensor(nm, [128, 4, T], BF16, kind="Internal").ap()
    return d


def rstd_ops(S, out, ssum, inv_n, rt, wt):
    S.act(lambda e: e.activation(out, ssum, AF.Sqrt, bias=S.eps_ap(out), scale=inv_n), rt, wt)
    S.dve(lambda e: e.reciprocal(out, out), wt, wt)


def load_cast(S, stg, dst, src, kc, n, wtoks, eng):
    cap = stg.bufs[0].shape[1]
    kstep = max(1, min(kc, cap // n))
    for k0 in range(0, kc, kstep):
        k1 = min(kc, k0 + kstep)
        buf, tok = stg.next()
        view = buf[:, 0:(k1 - k0) * n].rearrange("p (k n) -> p k n", k=k1 - k0)
        S.dma("sp", view, src[:, k0:k1, :], reads=[], writes=[tok])
        S.castn = getattr(S, "castn", 0) + 1
        if S.castn % 2 == 0:
            S.act(lambda e, view=view, k0=k0, k1=k1: e.copy(dst[:, k0:k1, :], view), [tok], wtoks)
        else:
            S.dve(lambda e, view=view, k0=k0, k1=k1: e.tensor_copy(dst[:, k0:k1, :], view), [tok], wtoks)


def phase_mod(S, nc, d, ns):
    with ExitStack() as es:
        S.es = es
        cT = S.sb("cT_sb", [128, 8, ns + 1], F32)
        sil = S.sb("sil", [128, 8, ns + 1], F32)
        srep = S.sb("srep", [128, 8, ns + 1, 128], BF16)
        onesf = S.sb("onesf", [128, 128], F32)
        stg = Rot(S, "stg", 2, [128, 4096], F32)
        wb = Rot(S, "wb", 2, [128, 8, 512], BF16)
        bb = Rot(S, "bb", 2, [128, 512], F32)
        gb = Rot(S, "gb", 2, [128, 512], F32)
        ob = Rot(S, "ob", 3, [128, 512], F32)
        pp = Rot(S, "pm", 3, [128, 512], F32, psum=True)
        S.dma("sp", cT[:], d["cT"], [], ["cT"])
        S.act(lambda e: e.activation(sil[:], cT[:], AF.Silu), ["cT"], ["sil"])
        S.dve(lambda e: e.memset(onesf[:], 1.0), [], ["onesf"])
        for k in range(8):
            for s in range(ns + 1):
                S.dve(lambda e, k=k, s=s: e.tensor_scalar(srep[:, k, s, :], onesf[:], sil[:, k, s:s + 1], None, ALU.mult),
                      ["sil", "onesf"], ["srep"])
        for l in range(2):
            for cb in range(12):
                sec, half = cb // 2, cb % 2
                w, wt = wb.next()
                load_cast(S, stg, w[:], d["w_ada"][l, :, :, cb * 512:(cb + 1) * 512], 8, 512, [wt], "pool")
                b, bt = bb.next()
                S.dma("sp", b[:], d["b_ada"][l, :, cb * 512:(cb + 1) * 512], [], [bt])
                g, gt = gb.next()
                if sec in (1, 2, 4, 5):
                    gi = {1: 0, 2: 1, 4: 2, 5: 3}[sec]
                    S.dma("sp", g[:], d["g4"][l, gi, :, half * 512:(half + 1) * 512], [], [gt])
                for s in range(ns + 1):
                    p, pt = pp.next()
                    for k in range(8):
                        S.pe(lambda e, p=p, k=k, s=s, w=w: e.matmul(p[:], srep[:, k, s, :], w[:, k, :], start=(k == 0), stop=(k == 7)),
                             ["srep", wt], [pt])
                    o, ot = ob.next()
                    if sec in (0, 3):
                        S.dve(lambda e, o=o, p=p, b=b: e.tensor_tensor(o[:], p[:], b[:], ALU.add), [pt, bt], [ot])
                    elif sec in (1, 4):
                        S.dve(lambda e, o=o, p=p, b=b: e.scalar_tensor_tensor(o[:], p[:], 1.0, b[:], ALU.add, ALU.add), [pt, bt], [ot])
                        S.dve(lambda e, o=o, g=g: e.tensor_tensor(o[:], o[:], g[:], ALU.mult), [ot, gt], [ot])
                    else:
                        S.dve(lambda e, o=o, p=p, b=b: e.tensor_tensor(o[:], p[:], b[:], ALU.add), [pt, bt], [ot])
                        S.dve(lambda e, o=o, g=g: e.tensor_tensor(o[:], o[:], g[:], ALU.mult), [ot, gt], [ot])
                    S.dma("sp", d["modrep"][l, s, :, cb * 512:(cb + 1) * 512], o[:], [ot], [])
        S.flush()


def phase_norm(S, nc, d, l, b, sub, src, tiles, hT, ident, ns):
    with ExitStack() as es:
        S.es = es
        vs = S.sb("vs", [128, 2, 2, D], F32)
        for who, s in ((0, b), (1, ns)):
            if who == 1 and 0 not in tiles:
                continue
            S.dma("sp", vs[:, who, 0, :], d["modrep"][l, s, :, (3 * sub) * D:(3 * sub + 1) * D], [], ["vs"])
            S.dma("sp", vs[:, who, 1, :], d["modrep"][l, s, :, (3 * sub + 1) * D:(3 * sub + 2) * D], [], ["vs"])
        xt = Rot(S, "xt", 2, [128, D], F32)
        junk = S.sb("junk", [128, D], F32)
        ss = Rot(S, "ss", 2, [128, 1], F32)
        rs = Rot(S, "rs", 2, [128, 1], F32)
        h1 = Rot(S, "h1", 2, [128, D], F32)
        hb = Rot(S, "hb", 2, [128, D], BF16)
        pt_ = Rot(S, "ptr", 2, [128, 8, 128], BF16, psum=True)
        for i in tiles:
            who = 1 if i < 2 else 0
            x, xtok = xt.next()
            S.dma("sp", x[:], src(i), [], [xtok])
            s1, s1t = ss.next()
            S.act(lambda e, x=x, s1=s1: e.activation(junk[:], x[:], AF.Square, accum_out=s1[:]), [xtok], ["junk", s1t])
            r1, r1t = rs.next()
            rstd_ops(S, r1[:], s1[:], 1.0 / D, [s1t], [r1t])
            h, ht = h1.next()
            S.dve(lambda e, h=h, x=x, r1=r1, who=who: e.scalar_tensor_tensor(h[:], x[:], r1[:], vs[:, who, 1, :], ALU.mult, ALU.mult),
                  [xtok, r1t, "vs"], [ht])
            hh, hht = hb.next()
            S.pool(lambda e, hh=hh, h=h, who=who: e.tensor_tensor(hh[:], h[:], vs[:, who, 0, :], ALU.add), [ht, "vs"], [hht])
            p, ptk = pt_.next()
            for k in range(8):
                S.pe(lambda e, p=p, hh=hh, k=k: e.transpose(p[:, k, :], hh[:, k * 128:(k + 1) * 128], ident[:]), [hht, "ident"], [ptk])
            S.act(lambda e, p=p, i=i: e.copy(hT[:, :, i * 128:(i + 1) * 128], p[:]), [ptk], [("hT", i)])
        S.flush()


def proj_fm(S, pp, w, wtok, hT, ncols_off, t0, n):
    p, pt = pp.next()
    for k in range(8):
        S.pe(lambda e, p=p, k=k: e.matmul(p[:, 0:n], w[:, k, ncols_off:ncols_off + 128], hT[:, k, t0:t0 + n],
                                          start=(k == 0), stop=(k == 7)),
             [wtok] + [("hT", i) for i in range(t0 // 128, (t0 + n) // 128)], [pt])
    return p, pt


def phase_hgrn(S, nc, d, l, hT, cst):
    ident, ones, hgmask, rowmask, resetmask = cst["ident"], cst["ones"], cst["hgmask"], cst["rowmask"], cst["resetmask"]
    with ExitStack() as es:
        S.es = es
        stg = Rot(S, "stg", 1, [128, 2048], F32)
        wq = S.sb("wq", [128, 8, 512], BF16)
        wz = S.sb("wz", [128, 8, 512], BF16)
        wv = wz
        vtok = S.sb("vtok", [128, NT, 512], BF16)
        kglT = S.sb("kglT", [128, 4, T], BF16)
        oT = S.sb("oT", [128, 4, T], F32)
        qeT = S.sb("qeT", [128, 4, T], BF16)
        keT = S.sb("keT", [128, 4, T], BF16)
        gl = S.sb("gl", [128, 4, T // 32], F32)
        lbv = S.sb("lbv", [128, 2, 2, 4], F32)
        lo = S.sb("lo", [128, 2, 4], F32)
        oml = S.sb("oml", [128, 2, 4], F32)
        gn = S.sb("gn", [128, 2], F32)
        Sf = S.sb("Sf", [128, 4, 128], F32)
        Sb = S.sb("Sb", [128, 4, 128], BF16)
        pp = Rot(S, "pp", 2, [128, 512], F32, psum=True)
        S.dma("sp", lbv[:], d["lbT"], [], ["lbv"])
        S.dma("sp", gn[:], d["hgn"], [], ["gn"])
        if l == 0:
            S.dve(lambda e: e.memset(lo[:], 0.0), [], ["lo"])
        else:
            S.dve(lambda e: e.tensor_tensor(lo[:], lbv[:, 1, :, :], lbv[:, 0, :, :], ALU.subtract), ["lbv"], ["lo"])
            S.act(lambda e: e.activation(lo[:], lo[:], AF.Sigmoid), ["lo"], ["lo"])
        S.dve(lambda e: e.tensor_scalar(oml[:], lo[:], -1.0, 1.0, ALU.mult, ALU.add), ["lo"], ["oml"])
        S.dve(lambda e: e.memset(oT[:], 0.0), [], ["oT"])
        load_cast(S, stg, wq[:], d["w_in"][l, :, :, C_HQ:C_HQ + 512], 8, 512, ["wq"], "pool")
        load_cast(S, stg, wv[:], d["w_in"][l, :, :, C_HV:C_HV + 512], 8, 512, ["wz"], "pool")
        for i in range(NT):
            p, pt = pp.next()
            for k in range(8):
                S.pe(lambda e, p=p, k=k, i=i: e.matmul(p[:], hT[:, k, i * 128:(i + 1) * 128], wv[:, k, :], start=(k == 0), stop=(k == 7)),
                     ["wz", ("hT", i)], [pt])
            S.act(lambda e, p=p, i=i: e.copy(vtok[:, i, :], p[:]), [pt], [("vtok", i)])
        tA = Rot(S, "tA", 2, [128, 512], F32)
        tB = Rot(S, "tB", 2, [128, 512], F32)
        tC = Rot(S, "tC", 2, [128, 512], F32)
        tK = Rot(S, "tK", 1, [128, 512], F32)
        tD = Rot(S, "tD", 1, [128, 512], F32)
        pkt = Rot(S, "pkt", 1, [128, 4, 128], BF16, psum=True)
        pA = Rot(S, "pA", 1, [128, 4, 128], F32, psum=True)
        po = Rot(S, "po", 2, [128, 4, 128], F32, psum=True)
        pu = Rot(S, "pu", 2, [128, 128], F32, psum=True)
        ktm = S.sb("ktm", [128, 4, 512], BF16)
        AT = S.sb("AT", [128, 4, 128], BF16)
        if l == 0 and not getattr(S, "_rep_h", False):
            S._rep_h = True
            print("hgrn sbuf remaining", nc.sbuf_bytes_remaining)
        for dr in range(2):
            load_cast(S, stg, wz[:], d["w_in"][l, :, :, (C_ZF, C_ZB)[dr]:(C_ZF, C_ZB)[dr] + 512], 8, 512, ["wz"], "pool")
            for h in range(4):
                for (t0, n) in TG:
                    nch = n // 32
                    p, pt = proj_fm(S, pp, wz, "wz", hT, h * 128, t0, n)
                    a, at = tA.next()
                    S.act(lambda e, a=a, p=p, n=n: e.activation(a[:, 0:n], p[:, 0:n], AF.Sigmoid), [pt], [at])
                    S.dve(lambda e, a=a, n=n, h=h, dr=dr: e.tensor_scalar(a[:, 0:n], a[:, 0:n], oml[:, dr, h:h + 1], lo[:, dr, h:h + 1], ALU.mult, ALU.add),
                          [at, "oml", "lo"], [at])
                    kk, kt = tK.next()
                    S.pool(lambda e, kk=kk, a=a, n=n: e.tensor_scalar(kk[:, 0:n], a[:, 0:n], -1.0, 1.0, ALU.mult, ALU.add), [at], [kt])
                    b_, bt = tB.next()
                    S.act(lambda e, b_=b_, a=a, n=n: e.activation(b_[:, 0:n], a[:, 0:n], AF.Ln), [at], [bt])
                    c_, ct = tC.next()
                    S.dve(lambda e, c_=c_, b_=b_, n=n: e.tensor_tensor_scan(c_[:, 0:n], resetmask[:, 0:n], b_[:, 0:n], 0.0, ALU.mult, ALU.add),
                          [bt, "resetmask"], [ct])
                    if dr == 1:
                        c3 = c_[:, 0:n].rearrange("p (c t) -> p c t", t=32)
                        b3 = b_[:, 0:n].rearrange("p (c t) -> p c t", t=32)
                        S.pool(lambda e, b_=b_, c_=c_, n=n: e.tensor_tensor(b_[:, 0:n], b_[:, 0:n], c_[:, 0:n], ALU.subtract), [bt, ct], [bt])
                        g_, gt_ = tD.next()
                        g3 = g_[:, 0:n].rearrange("p (c t) -> p c t", t=32)
                        S.dve(lambda e, c3=c3, b3=b3, g3=g3, nch=nch: e.tensor_tensor(g3, b3, bc(c3[:, :, 31:32], [128, nch, 32]), ALU.add), [bt, ct], [gt_])
                        c_, ct = g_, gt_
                    S.act(lambda e, a=a, c_=c_, n=n: e.activation(a[:, 0:n], c_[:, 0:n], AF.Exp), [ct, at], [at])
                    a3 = a[:, 0:n].rearrange("p (c t) -> p c t", t=32)
                    pos = 31 if dr == 0 else 0
                    S.pool(lambda e, a3=a3, h=h, t0=t0, nch=nch, pos=pos: e.tensor_copy(gl[:, h, t0 // 32:t0 // 32 + nch], a3[:, :, pos]), [at], ["gl"])
                    q, qt = proj_fm(S, pp, wq, "wq", hT, h * 128, t0, n)
                    S.dve(lambda e, q=q, a=a, h=h, t0=t0, n=n: e.tensor_tensor(qeT[:, h, t0:t0 + n], q[:, 0:n], a[:, 0:n], ALU.mult),
                          [qt, at], [("qeT", h)])
                    S.act(lambda e, b_=b_, c_=c_, n=n: e.activation(b_[:, 0:n], c_[:, 0:n], AF.Exp, scale=-1.0), [ct, bt], [bt])
                    S.pool(lambda e, b_=b_, kk=kk, h=h, t0=t0, n=n: e.tensor_tensor(keT[:, h, t0:t0 + n], kk[:, 0:n], b_[:, 0:n], ALU.mult),
                           [bt, kt], [("keT", h)])
                    S.dve(lambda e, h=h, t0=t0, n=n, nch=nch: e.tensor_tensor(kglT[:, h, t0:t0 + n].rearrange("p (c t) -> p c t", t=32),
                                                                             keT[:, h, t0:t0 + n].rearrange("p (c t) -> p c t", t=32),
                                                                             bc(gl[:, h, t0 // 32:t0 // 32 + nch].unsqueeze(2), [128, nch, 32]), ALU.mult),
                          [("keT", h), "gl"], [("kglT", h)])
            S.dve(lambda e: e.memset(Sf[:], 0.0), [], [("Sf", h) for h in range(4)])
            S.pool(lambda e: e.memset(Sb[:], 0.0), [], [("Sb", h) for h in range(4)])
            order = list(range(NT)) if dr == 0 else [1, 0] + list(range(NT - 1, 1, -1))
            jord = [0, 1, 2, 3] if dr == 0 else [3, 2, 1, 0]
            for i in order:
                tk = slice(i * 128, (i + 1) * 128)
                pk, pkt_ = pkt.next()
                for h in range(4):
                    S.pe(lambda e, pk=pk, h=h, tk=tk: e.transpose(pk[:, h, :], kglT[:, h, tk], ident[:]), [("kglT", h), "ident"], [pkt_])
                for j in range(4):
                    eng = S.act if j % 2 == 0 else S.dve
                    if j % 2 == 0:
                        S.dve(lambda e, pk=pk, j=j: e.tensor_scalar(ktm[:, j, :], pk[:].rearrange("p h k -> p (h k)"), rowmask[:, j:j + 1], None, ALU.mult),
                              [pkt_, "rowmask"], [("ktm", j)])
                    else:
                        S.dve(lambda e, pk=pk, j=j: e.tensor_scalar(ktm[:, j, :], pk[:].rearrange("p h k -> p (h k)"), rowmask[:, j:j + 1], None, ALU.mult),
                              [pkt_, "rowmask"], [("ktm", j)])
                pa, pat = pA.next()
                for h in range(4):
                    S.pe(lambda e, pa=pa, h=h, tk=tk: e.matmul(pa[:, h, :], keT[:, h, tk], qeT[:, h, tk], start=True, stop=True),
                         [("keT", h), ("qeT", h)], [pat])
                S.dve(lambda e, pa=pa, dr=dr: e.tensor_tensor(AT[:].rearrange("p h t -> p (h t)"), pa[:].rearrange("p h t -> p (h t)"), hgmask[:, dr, :], ALU.mult),
                      [pat, "hgmask"], ["AT"])
                o_, ot = po.next()
                for h in range(4):
                    S.pe(lambda e, o_=o_, h=h, i=i: e.matmul(o_[:, h, :], vtok[:, i, h * 128:(h + 1) * 128], AT[:, h, :], start=True, stop=False),
                         [("vtok", i), "AT"], [ot])
                    for jj, j in enumerate(jord):
                        c0 = i * 128 + 32 * j
                        ch = c0 // 32
                        S.pe(lambda e, o_=o_, h=h, j=j, c0=c0, jj=jj: e.matmul(o_[:, h, 32 * j:32 * j + 32], Sb[:, h, :], qeT[:, h, c0:c0 + 32],
                                                                               start=False, stop=(jj == 3)),
                             [("Sb", h), ("qeT", h)], [ot])
                        u, ut = pu.next()
                        S.pe(lambda e, u=u, h=h, j=j, i=i: e.matmul(u[:], ktm[:, j, h * 128:(h + 1) * 128], vtok[:, i, h * 128:(h + 1) * 128], start=True, stop=True),
                             [("ktm", j), ("vtok", i)], [ut])
                        S.dve(lambda e, u=u, h=h, ch=ch: e.scalar_tensor_tensor(Sf[:, h, :], Sf[:, h, :], gl[:, h, ch:ch + 1], u[:], ALU.mult, ALU.add),
                              [ut, ("Sf", h), "gl"], [("Sf", h)])
                        S.act(lambda e, h=h: e.copy(Sb[:, h, :], Sf[:, h, :]), [("Sf", h)], [("Sb", h)])
                S.dve(lambda e, o_=o_, tk=tk: e.tensor_tensor(oT[:, :, tk], oT[:, :, tk], o_[:], ALU.add), [ot, "oT"], ["oT"])
        load_cast(S, stg, wz[:], d["w_in"][l, :, :, C_ZG:C_ZG + 512], 8, 512, ["wz"], "pool")
        sq = Rot(S, "sq", 2, [128, 512], BF16)
        aTb = Rot(S, "aTb", 2, [128, 512], BF16)
        for h in range(4):
            for (t0, n) in TG:
                s_, st = sq.next()
                S.act(lambda e, s_=s_, h=h, t0=t0, n=n: e.activation(s_[:, 0:n], oT[:, h, t0:t0 + n], AF.Square), ["oT"], [st])
                p, pt = pp.next()
                S.pe(lambda e, p=p, s_=s_, n=n: e.matmul(p[:, 0:n], ones[:], s_[:, 0:n], start=True, stop=True), [st, "ones"], [pt])
                a, at = tA.next()
                rstd_ops(S, a[:, 0:n], p[:, 0:n], 1.0 / 128, [pt], [at])
                z, zt = proj_fm(S, pp, wz, "wz", hT, h * 128, t0, n)
                b_, bt = tB.next()
                S.act(lambda e, b_=b_, z=z, n=n: e.activation(b_[:, 0:n], z[:, 0:n], AF.Silu), [zt], [bt])
                c_, ct = tC.next()
                S.dve(lambda e, c_=c_, a=a, h=h, t0=t0, n=n: e.scalar_tensor_tensor(c_[:, 0:n], oT[:, h, t0:t0 + n], gn[:, l:l + 1], a[:, 0:n], ALU.mult, ALU.mult),
                      ["oT", at, "gn"], [ct])
                o2, o2t = aTb.next()
                S.pool(lambda e, o2=o2, c_=c_, b_=b_, n=n: e.tensor_tensor(o2[:, 0:n], c_[:, 0:n], b_[:, 0:n], ALU.mult), [ct, bt], [o2t])
                S.dma("sp", d["aTs"][:, h, t0:t0 + n], o2[:, 0:n], [o2t], [])
        S.flush()


def phase_attn(S, nc, d, l, hT, cst, need_ctx):
    ident = cst["ident"]
    with ExitStack() as es:
        S.es = es
        stg = Rot(S, "stg", 2, [128, 2048], F32)
        wqkv = S.sb("wqkv", [128, 8, 768], BF16)
        qT = S.sb("qT", [128, 4, T], BF16)
        kT = S.sb("kT", [128, T], BF16)
        vaug = S.sb("vaug", [128, NT, 2, 65], BF16)
        btok = S.sb("btok", [128, NT, 512], BF16)
        gq = S.sb("gq", [128, 512], F32)
        gk = S.sb("gk", [128, 128], F32)
        S.dma("sp", gq[:], d["gq"][l], [], ["gq"])
        S.dma("sp", gk[:], d["gk"][l], [], ["gk"])
        S.dve(lambda e: e.memset(vaug[:], 1.0), [], ["vaug"])
        for (c0, c1, k0) in ((0, 512, 0), (512, 768, 512)):
            load_cast(S, stg, wqkv[:, :, c0:c1], d["w_in"][l, :, :, C_AQ + c0:C_AQ + c1], 8, c1 - c0, ["wqkv"], "pool")
        with ExitStack() as es2:
            S.es = es2
            pq = Rot(S, "pq", 2, [128, 512], F32, psum=True)
            pkv = Rot(S, "pkv", 2, [128, 256], F32, psum=True)
            ptq = Rot(S, "ptq", 1, [128, 4, 128], BF16, psum=True)
            ptk = Rot(S, "ptk", 1, [128, 128], BF16, psum=True)
            sqb = Rot(S, "sqb", 2, [128, 640], F32)
            ssb = Rot(S, "ssb", 2, [128, 10], F32)
            qn = Rot(S, "qn", 2, [128, 640], F32)
            qr = Rot(S, "qr", 2, [128, 640], BF16)
            qrp = Rot(S, "qrp", 2, [128, 512], BF16)
            cs = Rot(S, "cs", 2, [128, 2, 256], F32)
            tmp1 = Rot(S, "tmp1", 2, [128, 320], F32)
            tmp2 = Rot(S, "tmp2", 2, [128, 320], F32)
            for i in range(NT):
                p, pt = pq.next()
                for k in range(8):
                    S.pe(lambda e, p=p, k=k, i=i: e.matmul(p[:], hT[:, k, i * 128:(i + 1) * 128], wqkv[:, k, 0:512], start=(k == 0), stop=(k == 7)),
                         ["wqkv", ("hT", i)], [pt])
                p2, p2t = pkv.next()
                for k in range(8):
                    S.pe(lambda e, p2=p2, k=k, i=i: e.matmul(p2[:], hT[:, k, i * 128:(i + 1) * 128], wqkv[:, k, 512:768], start=(k == 0), stop=(k == 7)),
                         ["wqkv", ("hT", i)], [p2t])
                S.act(lambda e, p2=p2, i=i: e.copy(vaug[:, i, :, 0:64], p2[:, 128:256].rearrange("p (g d) -> p g d", g=2)), [p2t, "vaug"], [("vaug", i)])
                sq_, sqt = sqb.next()
                S.act(lambda e, sq_=sq_, p=p: e.activation(sq_[:, 0:512], p[:], AF.Square), [pt], [sqt])
                S.act(lambda e, sq_=sq_, p2=p2: e.activation(sq_[:, 512:640], p2[:, 0:128], AF.Square), [p2t], [sqt])
                s_, st = ssb.next()
                S.dve(lambda e, s_=s_, sq_=sq_: e.tensor_reduce(s_[:], sq_[:].rearrange("p (h d) -> p h d", d=64), AX.X, ALU.add), [sqt], [st])
                rstd_ops(S, s_[:], s_[:], 1.0 / 64, [st], [st])
                q_, qnt = qn.next()
                S.dve(lambda e, q_=q_, p=p, s_=s_: e.tensor_tensor(q_[:, 0:512].rearrange("p (h d) -> p h d", d=64), p[:].rearrange("p (h d) -> p h d", d=64),
                                                                   bc(s_[:, 0:8].unsqueeze(2), [128, 8, 64]), ALU.mult), [pt, st], [qnt])
                S.dve(lambda e, q_=q_, p2=p2, s_=s_: e.tensor_tensor(q_[:, 512:640].rearrange("p (h d) -> p h d", d=64), p2[:, 0:128].rearrange("p (h d) -> p h d", d=64),
                                                                     bc(s_[:, 8:10].unsqueeze(2), [128, 2, 64]), ALU.mult), [p2t, st], [qnt])
                S.pool(lambda e, q_=q_: e.tensor_tensor(q_[:, 0:512], q_[:, 0:512], gq[:], ALU.mult), [qnt, "gq"], [qnt])
                S.pool(lambda e, q_=q_: e.tensor_tensor(q_[:, 512:640], q_[:, 512:640], gk[:], ALU.mult), [qnt, "gk"], [qnt])
                r_, rt = qr.next()
                if i < 2:
                    S.dve(lambda e, r_=r_, q_=q_: e.tensor_copy(r_[:], q_[:]), [qnt], [rt])
                else:
                    c_, ct = cs.next()
                    r0 = (i - 2) * 128
                    S.dma("sp", c_[:, 0, :], d["cos8"][r0:r0 + 128, :], [], [ct])
                    S.dma("sp", c_[:, 1, :], d["sin8"][r0:r0 + 128, :], [], [ct])
                    x1 = q_[:].rearrange("p (n two) -> p n two", two=2)[:, :, 0]
                    x2 = q_[:].rearrange("p (n two) -> p n two", two=2)[:, :, 1]
                    o1 = r_[:].rearrange("p (n two) -> p n two", two=2)[:, :, 0]
                    o2 = r_[:].rearrange("p (n two) -> p n two", two=2)[:, :, 1]
                    t1, t1t = tmp1.next()
                    t2, t2t = tmp2.next()
                    for (a0, a1, cc0) in ((0, 256, 0), (256, 320, 0)):
                        w_ = a1 - a0
                        S.dve(lambda e, t1=t1, x1=x1, c_=c_, a0=a0, a1=a1, w_=w_: e.tensor_tensor(t1[:, a0:a1], x1[:, a0:a1], c_[:, 0, 0:w_], ALU.mult), [qnt, ct], [t1t])
                        S.pool(lambda e, t2=t2, x2=x2, c_=c_, a0=a0, a1=a1, w_=w_: e.tensor_tensor(t2[:, a0:a1], x2[:, a0:a1], c_[:, 1, 0:w_], ALU.mult), [qnt, ct], [t2t])
                    S.dve(lambda e, o1=o1, t1=t1, t2=t2: e.tensor_tensor(o1, t1[:], t2[:], ALU.subtract), [t1t, t2t], [rt])
                    t3, t3t = tmp1.next()
                    t4, t4t = tmp2.next()
                    for (a0, a1, cc0) in ((0, 256, 0), (256, 320, 0)):
                        w_ = a1 - a0
                        S.dve(lambda e, t3=t3, x1=x1, c_=c_, a0=a0, a1=a1, w_=w_: e.tensor_tensor(t3[:, a0:a1], x1[:, a0:a1], c_[:, 1, 0:w_], ALU.mult), [qnt, ct], [t3t])
                        S.pool(lambda e, t4=t4, x2=x2, c_=c_, a0=a0, a1=a1, w_=w_: e.tensor_tensor(t4[:, a0:a1], x2[:, a0:a1], c_[:, 0, 0:w_], ALU.mult), [qnt, ct], [t4t])
                    S.dve(lambda e, o2=o2, t3=t3, t4=t4: e.tensor_tensor(o2, t3[:], t4[:], ALU.add), [t3t, t4t, rt], [rt])
                tq, tqt = ptq.next()
                rp, rpt = qrp.next()
                S.pool(lambda e, rp=rp, r_=r_: e.tensor_copy(rp[:].rearrange("p (h g d) -> p h g d", h=4, g=2),
                                                             r_[:, 0:512].rearrange("p (g h d) -> p h g d", g=2, h=4)), [rt], [rpt])
                for pr in range(4):
                    S.pe(lambda e, tq=tq, rp=rp, pr=pr: e.transpose(tq[:, pr, :], rp[:, pr * 128:(pr + 1) * 128], ident[:]), [rpt, "ident"], [tqt])
                S.act(lambda e, tq=tq, i=i: e.copy(qT[:, :, i * 128:(i + 1) * 128], tq[:]), [tqt], [("qT", i)])
                tk_, tkt = ptk.next()
                S.pe(lambda e, tk_=tk_, r_=r_: e.transpose(tk_[:], r_[:, 512:640], ident[:]), [rt, "ident"], [tkt])
                S.dve(lambda e, tk_=tk_, i=i: e.tensor_copy(kT[:, i * 128:(i + 1) * 128], tk_[:]), [tkt], [("kT", i)])
            S.flush()
        S.es = es
        with ExitStack() as es3:
            S.es = es3
            ps_s = Rot(S, "ps_s", 3, [128, 512], F32, psum=True)
            ps_o = Rot(S, "ps_o", 2, [128, 4, 128], F32, psum=True)
            PT = Rot(S, "PT", 3, [128, 512], BF16)
            rc = Rot(S, "rc", 2, [128, 4], F32)
            groups = [(256 + 512 * g, 512, list(range(NT))) for g in range(4)]
            if need_ctx:
                groups.append((0, 256, [0, 1]))
            for (q0, nq, ktiles) in groups:
                nqt = nq // 128
                for hd in range(8):
                    pr, g = hd % 4, hd // 4
                    ps = slice(64 * g, 64 * g + 64)
                    o_, ot = ps_o.next()
                    for ki, kc in enumerate(ktiles):
                        s_, st = ps_s.next()
                        S.pe(lambda e, s_=s_, ps=ps, kc=kc, pr=pr, q0=q0, nq=nq: e.matmul(s_[:, 0:nq], kT[ps, kc * 128:(kc + 1) * 128], qT[ps, pr, q0:q0 + nq], start=True, stop=True),
                             [("kT", kc)] + [("qT", q0 // 128 + t) for t in range(nqt)], [st])
                        p_, ptok = PT.next()
                        S.act(lambda e, p_=p_, s_=s_, nq=nq: e.activation(p_[:, 0:nq], s_[:, 0:nq], AF.Exp, scale=0.125), [st], [ptok])
                        for t in range(nqt):
                            S.pe(lambda e, o_=o_, p_=p_, t=t, kc=kc, g=g, ki=ki, nk=len(ktiles): e.matmul(o_[:, t, 0:65], p_[:, t * 128:(t + 1) * 128], vaug[:, kc, g, :],
                                                                                                        start=(ki == 0), stop=(ki == nk - 1), skip_group_check=True),
                                 [ptok, ("vaug", kc)], [ot])
                    r_, rt = rc.next()
                    S.dve(lambda e, r_=r_, o_=o_, nqt=nqt: e.reciprocal(r_[:, 0:nqt], o_[:, 0:nqt, 64]), [ot], [rt])
                    S.dve(lambda e, r_=r_, o_=o_, nqt=nqt, q0=q0, hd=hd: e.tensor_tensor(btok[:, q0 // 128:q0 // 128 + nqt, hd * 64:(hd + 1) * 64], o_[:, 0:nqt, 0:64],
                                                                                        bc(r_[:, 0:nqt].unsqueeze(2), [128, nqt, 64]), ALU.mult),
                          [ot, rt], [("btok", q0 // 128 + t) for t in range(nqt)])
            S.flush()
        S.es = es
        with ExitStack() as es4:
            S.es = es4
            ptb = Rot(S, "ptb", 2, [128, 4, 128], BF16, psum=True)
            bo = Rot(S, "bo", 2, [128, 4, 128], BF16)
            for i in range(0 if need_ctx else 2, NT):
                p, pt = ptb.next()
                for c in range(4):
                    S.pe(lambda e, p=p, c=c, i=i: e.transpose(p[:, c, :], btok[:, i, c * 128:(c + 1) * 128], ident[:]), [("btok", i), "ident"], [pt])
                o, ot = bo.next()
                S.act(lambda e, o=o, p=p: e.copy(o[:], p[:]), [pt], [ot])
                S.dma("sp", d["bTs"][:, :, i * 128:(i + 1) * 128], o[:], [ot], [])
            S.flush()


def sin_act(S, out, arg_ps, bcol, fcol, n, rt, wt, tmp):
    S.dve(lambda e: e.tensor_scalar(tmp[:, 0:n], arg_ps, bcol, fcol, ALU.add, ALU.mult), rt, ["sin_t"])
    MAGIC = 12582912.0
    kk = S.sin_kk
    S.dve(lambda e: e.tensor_scalar(kk[0:tmp.shape[0], 0:n], tmp[:, 0:n], 1.0 / (2.0 * math.pi), MAGIC, ALU.mult, ALU.add), ["sin_t"], ["sin_k"])
    S.dve(lambda e: e.tensor_scalar(kk[0:tmp.shape[0], 0:n], kk[0:tmp.shape[0], 0:n], -MAGIC, -2.0 * math.pi, ALU.add, ALU.mult), ["sin_k"], ["sin_k"])
    S.dve(lambda e: e.tensor_tensor(tmp[:, 0:n], tmp[:, 0:n], kk[0:tmp.shape[0], 0:n], ALU.add), ["sin_t", "sin_k"], ["sin_t"])
    S.dve(lambda e: e.tensor_scalar(tmp[:, 0:n], tmp[:, 0:n], -3.1415925, 3.1415925, ALU.max, ALU.min), ["sin_t"], ["sin_t"])
    S.act(lambda e: e.activation(out, tmp[:, 0:n], AF.Sin), ["sin_t"], [wt])


def phase_filters(S, nc, d, l, which):
    Ls, tag = (L, "x") if which == "x" else (LC, "c")
    nj = Ls // 128
    with ExitStack() as es:
        S.es = es
        w1 = S.sb("w1", [33, 64], F32)
        wi = S.sb("wi", [64, 2, 64], F32)
        wl = S.sb("wl", [64, 1024], F32)
        b1 = S.sb("b1", [64, 2], F32)
        bi = S.sb("bi", [64, 2, 2], F32)
        fq = S.sb("fq", [64, 2], F32)
        zT = S.sb("zT", [33, Ls], F32)
        h3 = S.sb("h3", [64, Ls], F32)
        Af = S.sb("Af", [128, nj, 512], BF16)
        Bf = S.sb("Bf", [128, nj, 512], BF16)
        S.dma("sp", w1[:], d["hyw1"][l], [], ["w"])
        S.dma("sp", wi[:], d["hywi"][l].rearrange("j i o -> i j o"), [], ["w"])
        S.dma("sp", wl[:], d["hywl"][l], [], ["w"])
        S.dma("sp", b1[:], d["hyb1"], [], ["w"])
        S.dma("sp", bi[:], d["hybi"], [], ["w"])
        S.dma("sp", fq[:], d["hyfreq"], [], ["w"])
        S.dma("sp", zT[:], d["zT" + tag], [], ["zT"])
        pm = Rot(S, "pm", 2, [64, 512], F32, psum=True)
        pf = Rot(S, "pf", 2, [128, 512], F32, psum=True)
        ha = Rot(S, "ha", 2, [64, 512], F32)
        hb = Rot(S, "hb", 2, [64, 512], F32)
        tmp = S.sb("tmpf", [64, 512], F32)
        S.sin_kk = S.sb("sinkk", [64, 512], F32)
        for t0 in range(0, Ls, 512):
            n = min(512, Ls - t0)
            p, pt = pm.next()
            S.pe(lambda e, p=p, t0=t0, n=n: e.matmul(p[:, 0:n], w1[:], zT[:, t0:t0 + n], start=True, stop=True), ["w", "zT"], [pt])
            a, at = ha.next()
            sin_act(S, a[:, 0:n], p[:, 0:n], b1[:, l:l + 1], fq[:, l:l + 1], n, [pt, "w"], at, tmp)
            p, pt = pm.next()
            S.pe(lambda e, p=p, a=a, n=n: e.matmul(p[:, 0:n], wi[:, 0, :], a[:, 0:n], start=True, stop=True), ["w", at], [pt])
            b_, bt = hb.next()
            sin_act(S, b_[:, 0:n], p[:, 0:n], bi[:, l, 0:1], fq[:, l:l + 1], n, [pt, "w"], bt, tmp)
            p, pt = pm.next()
            S.pe(lambda e, p=p, b_=b_, n=n: e.matmul(p[:, 0:n], wi[:, 1, :], b_[:, 0:n], start=True, stop=True), ["w", bt], [pt])
            sin_act(S, h3[:, t0:t0 + n], p[:, 0:n], bi[:, l, 1:2], fq[:, l:l + 1], n, [pt, "w"], "h3", tmp)
        dec = Rot(S, "dec", 2, [128, 512], F32)
        hf = Rot(S, "hf", 2, [128, 512], F32)
        hbk = Rot(S, "hbk", 2, [128, 512], F32)
        for j in range(nj):
            dc, dct = dec.next()
            S.dma("sp", dc[:], d["decay" + tag][j * 128:(j + 1) * 128, :], [], [dct])
            p1, p1t = pf.next()
            S.pe(lambda e, p1=p1, j=j: e.matmul(p1[:], h3[:, j * 128:(j + 1) * 128], wl[:, 0:512], start=True, stop=True), ["h3", "w"], [p1t])
            p2, p2t = pf.next()
            S.pe(lambda e, p2=p2, j=j: e.matmul(p2[:], h3[:, j * 128:(j + 1) * 128], wl[:, 512:1024], start=True, stop=True), ["h3", "w"], [p2t])
            f_, ft = hf.next()
            S.dve(lambda e, f_=f_, p1=p1, dc=dc: e.tensor_tensor(f_[:], p1[:], dc[:], ALU.mult), [p1t, dct], [ft])
            g_, gt = hbk.next()
            S.dve(lambda e, g_=g_, p2=p2, dc=dc: e.tensor_tensor(g_[:], p2[:], dc[:], ALU.mult), [p2t, dct], [gt])
            S.pool(lambda e, f_=f_, g_=g_, j=j: e.tensor_tensor(Af[:, j, :], f_[:], g_[:], ALU.add), [ft, gt], ["Af"])
            S.pool(lambda e, f_=f_, g_=g_, j=j: e.tensor_tensor(Bf[:, j, :], f_[:], g_[:], ALU.subtract), [ft, gt], ["Bf"])
        fw = Rot(S, "fw", 2, [128, nj, 128], BF16)
        ko = Rot(S, "ko", 2, [128, 512], F32)
        nf = Ls // 128
        for part, src in ((0, Af), (1, Bf)):
            for fc in range(nf):
                fcg = part * nf + fc
                w_, wt = fw.next()
                S.dma("sp", w_[:], d["fwd" + tag][:, :, fcg * 128:(fcg + 1) * 128], [], [wt])
                p, pt = pf.next()
                for j in range(nj):
                    S.pe(lambda e, p=p, w_=w_, j=j, src=src: e.matmul(p[:], w_[:, j, :], src[:, j, :], start=(j == 0), stop=(j == nj - 1)),
                         [wt, "Af", "Bf"], [pt])
                o, ot = ko.next()
                S.act(lambda e, o=o, p=p: e.copy(o[:], p[:]), [pt], [ot])
                dst = d["kspx"][l, :, fcg, :] if which == "x" else d["kspc"][:, fcg, :]
                S.dma("sp", dst, o[:], [ot], [])
        S.flush()


def phase_hyena(S, nc, d, l, hT, cst, need_ctx):
    ident = cst["ident"]
    with ExitStack() as es:
        S.es = es
        x0T = S.sb("x0T", [128, 4, T], BF16)
        uT = S.sb("uT", [128, 4, T], BF16)
        cw = S.sb("cw", [128, 12, 3], F32)
        cb = S.sb("cb", [128, 12], F32)
        hbias = S.sb("hbias", [128, 4], F32)
        S.dma("sp", cw[:], d["hycw"][:, l], [], ["cw"])
        S.dma("sp", cb[:], d["hycb"][:, l], [], ["cw"])
        S.dma("sp", hbias[:], d["hybias"][:, l], [], ["cw"])
        utok = S.sb("utok", [128, NT, 512], BF16)
        with ExitStack() as es2:
            S.es = es2
            stg = Rot(S, "stg", 2, [128, 2048], F32)
            wh = Rot(S, "wh", 2, [128, 8, 128], BF16)
            pp = Rot(S, "pp", 3, [128, 512], F32, psum=True)
            pz = Rot(S, "pz", 2, [128, T], F32)
            zc = Rot(S, "zc", 2, [128, T], F32)
            x1c = S.sb("x1c", [128, 4, T], BF16)
            segs = [(0, LC), (LC, T)]
            for ch in range(12):
                w_, wt = wh.next()
                load_cast(S, stg, w_[:], d["w_in"][l, :, :, C_HY + ch * 128:C_HY + (ch + 1) * 128], 8, 128, [wt], "pool")
                z_, zt = pz.next()
                for (t0, n) in TG:
                    p, pt = proj_fm(S, pp, w_, wt, hT, 0, t0, n)
                    S.act(lambda e, z_=z_, p=p, t0=t0, n=n: e.copy(z_[:, t0:t0 + n], p[:, 0:n]), [pt], [zt])
                c_, ct = zc.next()
                S.dve(lambda e, c_=c_, z_=z_, ch=ch: e.tensor_scalar(c_[:], z_[:], cw[:, ch, 1:2], cb[:, ch:ch + 1], ALU.mult, ALU.add), [zt, "cw"], [ct])
                for (a0, a1) in segs:
                    S.dve(lambda e, c_=c_, z_=z_, ch=ch, a0=a0, a1=a1: e.scalar_tensor_tensor(c_[:, a0 + 1:a1], z_[:, a0:a1 - 1], cw[:, ch, 0:1], c_[:, a0 + 1:a1], ALU.mult, ALU.add),
                          [zt, ct, "cw"], [ct])
                    S.dve(lambda e, c_=c_, z_=z_, ch=ch, a0=a0, a1=a1: e.scalar_tensor_tensor(c_[:, a0:a1 - 1], z_[:, a0 + 1:a1], cw[:, ch, 2:3], c_[:, a0:a1 - 1], ALU.mult, ALU.add),
                          [zt, ct, "cw"], [ct])
                sec, cc = ch // 4, ch % 4
                if sec == 0:
                    S.pool(lambda e, c_=c_, cc=cc: e.tensor_copy(x0T[:, cc, :], c_[:]), [ct], [("x0T", cc)])
                elif sec == 1:
                    S.pool(lambda e, c_=c_, cc=cc: e.tensor_copy(x1c[:, cc, :], c_[:]), [ct], [("x1c", cc)])
                else:
                    S.pool(lambda e, c_=c_, cc=cc: e.tensor_tensor(uT[:, cc, :], c_[:], x1c[:, cc, :], ALU.mult), [ct, ("x1c", cc)], [("uT", cc)])
            ptu = Rot(S, "ptu", 2, [128, 4, 128], BF16, psum=True)
            for i in range(NT):
                p, pt = ptu.next()
                for cc in range(4):
                    S.pe(lambda e, p=p, cc=cc, i=i: e.transpose(p[:, cc, :], uT[:, cc, i * 128:(i + 1) * 128], ident[:]), [("uT", cc), "ident"], [pt])
                S.act(lambda e, p=p, i=i: e.copy(utok[:, i, :], p[:].rearrange("p c t -> p (c t)")), [pt], [("utok", i)])
            S.flush()
        for (which, tb, Ls) in ((("c", 0, LC),) if need_ctx else ()) + (("x", LC, L),):
            nj = Ls // 128
            nf = Ls // 128
            i0 = tb // 128
            with ExitStack() as es3:
                S.es = es3
                YT = S.sb("YT", [128, 2 * nf, 512], BF16)
                fw = Rot(S, "fw", 3, [128, nj, 128], BF16)
                kr = Rot(S, "kr", 2, [128, 2, 512], F32)
                pu = Rot(S, "pu", 4, [128, 512], F32, psum=True)
                ur = Rot(S, "ur", 2, [128, 2, 512], F32)
                t1 = Rot(S, "t1", 2, [128, 512], F32)
                t2 = Rot(S, "t2", 2, [128, 512], F32)
                for fc in range(nf):
                    pr_ = []
                    for part in range(2):
                        fcg = part * nf + fc
                        w_, wt = fw.next()
                        S.dma("sp", w_[:], d["fwd" + which][:, :, fcg * 128:(fcg + 1) * 128], [], [wt])
                        p, pt = pu.next()
                        for j in range(nj):
                            S.pe(lambda e, p=p, w_=w_, j=j: e.matmul(p[:], w_[:, j, :], utok[:, i0 + j, :], start=(j == 0), stop=(j == nj - 1)),
                                 [wt, ("utok", i0 + j)], [pt])
                        pr_.append((p, pt))
                    k_, kt = kr.next()
                    for part in range(2):
                        src = d["kspx"][l, :, part * nf + fc, :] if which == "x" else d["kspc"][:, part * nf + fc, :]
                        S.dma("sp", k_[:, part, :], src, [], [kt])
                    u_, ut = ur.next()
                    S.act(lambda e, u_=u_, p=pr_[0][0]: e.copy(u_[:, 0, :], p[:]), [pr_[0][1]], [ut])
                    S.act(lambda e, u_=u_, p=pr_[1][0]: e.copy(u_[:, 1, :], p[:]), [pr_[1][1]], [ut])
                    a, at = t1.next()
                    b_, bt = t2.next()
                    S.dve(lambda e, a=a, u_=u_, k_=k_: e.tensor_tensor(a[:], u_[:, 0, :], k_[:, 0, :], ALU.mult), [ut, kt], [at])
                    S.pool(lambda e, b_=b_, u_=u_, k_=k_: e.tensor_tensor(b_[:], u_[:, 1, :], k_[:, 1, :], ALU.mult), [ut, kt], [bt])
                    S.dve(lambda e, a=a, b_=b_, fc=fc: e.tensor_tensor(YT[:, fc, :], a[:], b_[:], ALU.subtract), [at, bt], [("YT", fc)])
                    a2, a2t = t1.next()
                    b2, b2t = t2.next()
                    S.dve(lambda e, a2=a2, u_=u_, k_=k_: e.tensor_tensor(a2[:], u_[:, 0, :], k_[:, 1, :], ALU.mult), [ut, kt], [a2t])
                    S.pool(lambda e, b2=b2, u_=u_, k_=k_: e.tensor_tensor(b2[:], u_[:, 1, :], k_[:, 0, :], ALU.mult), [ut, kt], [b2t])
                    S.pool(lambda e, a2=a2, b2=b2, fc=fc, nf=nf: e.tensor_tensor(YT[:, nf + fc, :], a2[:], b2[:], ALU.add), [a2t, b2t], [("YT", nf + fc)])
                iv = Rot(S, "iv", 2, [128, 2 * nf, 256], BF16)
                py = Rot(S, "py", 3, [128, 256], F32, psum=True)
                y1 = Rot(S, "y1", 2, [128, 256], F32)
                co = Rot(S, "co", 2, [128, 256], BF16)
                for tg in range(Ls // 256):
                    v_, vt = iv.next()
                    S.dma("sp", v_[:], d["inv" + which][:, :, tg * 256:(tg + 1) * 256], [], [vt])
                    ta = tb + tg * 256
                    for cc in range(4):
                        p, pt = py.next()
                        for f in range(2 * nf):
                            S.pe(lambda e, p=p, v_=v_, f=f, cc=cc: e.matmul(p[:], YT[:, f, cc * 128:(cc + 1) * 128], v_[:, f, :], start=(f == 0), stop=(f == 2 * nf - 1)),
                                 [("YT", f), vt], [pt])
                        y_, yt = y1.next()
                        S.dve(lambda e, y_=y_, p=p, cc=cc, ta=ta: e.scalar_tensor_tensor(y_[:], uT[:, cc, ta:ta + 256], hbias[:, cc:cc + 1], p[:], ALU.mult, ALU.add),
                              [pt, ("uT", cc), "cw"], [yt])
                        o, ot = co.next()
                        S.pool(lambda e, o=o, y_=y_, cc=cc, ta=ta: e.tensor_tensor(o[:], y_[:], x0T[:, cc, ta:ta + 256], ALU.mult), [yt, ("x0T", cc)], [ot])
                        S.dma("sp", d["cTs"][:, cc, ta:ta + 256], o[:], [ot], [])
                S.flush()
            S.es = es


def post_residual(S, d, l, b, sub, ns, i, ypair, ytoks, src, dst, bufs):
    xt, junk, ss, rs, tmpb, al = bufs
    who = 1 if i < 2 else 0
    x, xtok = xt.next()
    S.dma("sp", x[:], src(i), [], [xtok])
    s1, s1t = ss.next()
    for hlf in range(2):
        S.act(lambda e, hlf=hlf, s1=s1: e.activation(junk[:], ypair[hlf][:], AF.Square, accum_out=s1[:, hlf:hlf + 1]), [ytoks[hlf]], ["junk", s1t])
    r1, r1t = rs.next()
    S.dve(lambda e, r1=r1, s1=s1: e.tensor_tensor(r1[:], s1[:, 0:1], s1[:, 1:2], ALU.add), [s1t], [r1t])
    rstd_ops(S, r1[:], r1[:], 1.0 / D, [r1t], [r1t])
    t_, tt = tmpb.next()
    for hlf in range(2):
        S.dve(lambda e, hlf=hlf, t_=t_, r1=r1, who=who: e.scalar_tensor_tensor(t_[:, hlf * 512:(hlf + 1) * 512], ypair[hlf][:], r1[:], al[:, who, hlf * 512:(hlf + 1) * 512], ALU.mult, ALU.mult),
              [ytoks[hlf], r1t, "al"], [tt])
    S.pool(lambda e, t_=t_, x=x: e.tensor_tensor(t_[:], t_[:], x[:], ALU.add), [tt, xtok], [tt])
    S.dma("sp", dst(i), t_[:], [tt], [])


def phase_merge(S, nc, d, l, b, ns, hT, tiles, src, dst):
    with ExitStack() as es:
        S.es = es
        stg = Rot(S, "stg", 2, [128, 2048], F32)
        wo = S.sb("wo", [128, 3, 4, 1024], BF16)
        wout = S.sb("wout", [128, 8, 1024], BF16)
        for bi_, nm in enumerate(("w_oa", "w_ob", "w_oc")):
            load_cast(S, stg, wo[:, bi_, :, :], d[nm][l], 4, 1024, ["wo"], "pool")
        load_cast(S, stg, wout[:], d["w_out"][l], 8, 1024, ["wout"], "pool")
        al = S.sb("al", [128, 2, D], F32)
        S.dma("sp", al[:, 0, :], d["modrep"][l, b, :, 2 * D:3 * D], [], ["al"])
        S.dma("sp", al[:, 1, :], d["modrep"][l, ns, :, 2 * D:3 * D], [], ["al"])
        wg = Rot(S, "wg", 2, [128, 8, 128], BF16)
        br = Rot(S, "br", 2, [128, 3, 4, 512], BF16)
        mT = Rot(S, "mT", 2, [128, 8, 512], BF16)
        pg = Rot(S, "pg", 2, [128, 512], F32, psum=True)
        py = Rot(S, "py", 2, [128, 512], F32, psum=True)
        po = Rot(S, "po", 4, [128, 512], F32, psum=True)
        sg = Rot(S, "sg", 2, [128, 512], F32)
        acc = Rot(S, "acc", 2, [128, 512], F32)
        tm = Rot(S, "tm", 2, [128, 512], F32)
        bufs = (Rot(S, "xt", 2, [128, D], F32), S.sb("junk", [128, 512], F32), Rot(S, "ss", 2, [128, 2], F32),
                Rot(S, "rs", 2, [128, 1], F32), Rot(S, "tmpb", 2, [128, D], F32), al)
        t_lo = min(tiles) * 128
        groups = [(t0, n) for (t0, n) in TG if t0 + n > t_lo]
        if t_lo > 0:
            groups = [(256, 256)] + [(512, 512), (1024, 512), (1536, 512), (2048, 256)]
        for (t0, n) in groups:
            b_, bt = br.next()
            for bi_, nm in enumerate(("aTs", "bTs", "cTs")):
                S.dma("sp", b_[:, bi_, :, 0:n], d[nm][:, :, t0:t0 + n], [], [bt])
            m_, mt = mT.next()
            for oc in range(8):
                a_, at = acc.next()
                for bi_ in range(3):
                    w_, wt = wg.next()
                    c0 = C_GT + bi_ * D + oc * 128
                    load_cast(S, stg, w_[:], d["w_in"][l, :, :, c0:c0 + 128], 8, 128, [wt], "pool")
                    g_, gt = proj_fm(S, pg, w_, wt, hT, 0, t0, n)
                    s_, st = sg.next()
                    S.act(lambda e, s_=s_, g_=g_, n=n: e.activation(s_[:, 0:n], g_[:, 0:n], AF.Sigmoid), [gt], [st])
                    y_, yt = py.next()
                    for k in range(4):
                        S.pe(lambda e, y_=y_, k=k, bi_=bi_, oc=oc, b_=b_, n=n: e.matmul(y_[:, 0:n], wo[:, bi_, k, oc * 128:(oc + 1) * 128], b_[:, bi_, k, 0:n], start=(k == 0), stop=(k == 3)),
                             ["wo", bt], [yt])
                    if bi_ == 0:
                        S.dve(lambda e, a_=a_, s_=s_, y_=y_, n=n: e.tensor_tensor(a_[:, 0:n], s_[:, 0:n], y_[:, 0:n], ALU.mult), [st, yt], [at])
                    else:
                        t_, tt = tm.next()
                        S.dve(lambda e, t_=t_, s_=s_, y_=y_, n=n: e.tensor_tensor(t_[:, 0:n], s_[:, 0:n], y_[:, 0:n], ALU.mult), [st, yt], [tt])
                        S.pool(lambda e, a_=a_, t_=t_, n=n: e.tensor_tensor(a_[:, 0:n], a_[:, 0:n], t_[:, 0:n], ALU.add), [at, tt], [at])
                S.act(lambda e, m_=m_, a_=a_, oc=oc, n=n: e.copy(m_[:, oc, 0:n], a_[:, 0:n]), [at], [mt])
            for ti in range(n // 128):
                i = t0 // 128 + ti
                ys, yts = [], []
                for hlf in range(2):
                    o_, ot = po.next()
                    for k in range(8):
                        S.pe(lambda e, o_=o_, k=k, ti=ti, hlf=hlf, m_=m_: e.matmul(o_[:], m_[:, k, ti * 128:(ti + 1) * 128], wout[:, k, hlf * 512:(hlf + 1) * 512], start=(k == 0), stop=(k == 7)),
                             [mt, "wout"], [ot])
                    ys.append(o_)
                    yts.append(ot)
                post_residual(S, d, l, b, 0, ns, i, ys, yts, src, dst, bufs)
        S.flush()


def phase_ffn(S, nc, d, l, b, ns, norm_fn, tiles, src, dst):
    t_lo = min(tiles) * 128
    segs = [(0, LC), (LC, T)] if t_lo == 0 else [(LC, T)]
    groups = [(t0, n) for (t0, n) in TG] if t_lo == 0 else [(256, 256), (512, 512), (1024, 512), (1536, 512), (2048, 256)]
    with ExitStack() as es:
        S.es = es
        gT = S.sb("gT", [128, 22, T], BF16)
        with ExitStack() as es2:
            S.es = es2
            hT = S.sb("hT2", [128, 8, T], BF16)
            norm_fn(hT)
            S.es = es2
            stg = Rot(S, "stg", 2, [128, 2048], F32)
            wu = Rot(S, "wu", 2, [128, 8, 128], BF16)
            pp = Rot(S, "pp", 3, [128, 512], F32, psum=True)
            ua = Rot(S, "ua", 2, [128, T], F32)
            uc = Rot(S, "uc", 2, [128, T], F32)
            fcw = S.sb("fcw", [128, 44, 3], F32)
            fcb = S.sb("fcb", [128, 44], F32)
            S.dma("sp", fcw[:], d["ffcw"][:, l], [], ["fcw"])
            S.dma("sp", fcb[:], d["ffcb"][:, l], [], ["fcw"])
            for j in range(22):
                res = []
                for half in range(2):
                    ch = half * 22 + j
                    w_, wt = wu.next()
                    load_cast(S, stg, w_[:], d["w_up"][l, :, :, ch * 128:(ch + 1) * 128], 8, 128, [wt], "pool")
                    z_, zt = ua.next()
                    for (t0, n) in groups:
                        p, pt = proj_fm(S, pp, w_, wt, hT, 0, t0, n)
                        S.act(lambda e, z_=z_, p=p, t0=t0, n=n: e.copy(z_[:, t0:t0 + n], p[:, 0:n]), [pt], [zt])
                    c_, ct = uc.next()
                    S.dve(lambda e, c_=c_, z_=z_, ch=ch: e.tensor_scalar(c_[:, t_lo:T], z_[:, t_lo:T], fcw[:, ch, 1:2], fcb[:, ch:ch + 1], ALU.mult, ALU.add), [zt, "fcw"], [ct])
                    for (a0, a1) in segs:
                        S.dve(lambda e, c_=c_, z_=z_, ch=ch, a0=a0, a1=a1: e.scalar_tensor_tensor(c_[:, a0 + 1:a1], z_[:, a0:a1 - 1], fcw[:, ch, 0:1], c_[:, a0 + 1:a1], ALU.mult, ALU.add),
                              [zt, ct, "fcw"], [ct])
                        S.dve(lambda e, c_=c_, z_=z_, ch=ch, a0=a0, a1=a1: e.scalar_tensor_tensor(c_[:, a0:a1 - 1], z_[:, a0 + 1:a1], fcw[:, ch, 2:3], c_[:, a0:a1 - 1], ALU.mult, ALU.add),
                              [zt, ct, "fcw"], [ct])
                    res.append((c_, ct))
                (ca, cat), (cb_, cbt) = res
                S.act(lambda e, ca=ca: e.activation(ca[:, t_lo:T], ca[:, t_lo:T], AF.Silu), [cat], [cat])
                S.pool(lambda e, ca=ca, cb_=cb_, j=j: e.tensor_tensor(gT[:, j, t_lo:T], ca[:, t_lo:T], cb_[:, t_lo:T], ALU.mult), [cat, cbt], [("gT", j)])
            S.flush()
        S.es = es
        with ExitStack() as es3:
            S.es = es3
            stg = Rot(S, "stg", 2, [128, 2048], F32)
            wd = S.sb("wd", [128, 22, 1024], BF16)
            load_cast(S, stg, wd[:], d["w_down"][l], 22, 1024, ["wd"], "pool")
            al = S.sb("al", [128, 2, D], F32)
            S.dma("sp", al[:, 0, :], d["modrep"][l, b, :, 5 * D:6 * D], [], ["al"])
            S.dma("sp", al[:, 1, :], d["modrep"][l, ns, :, 5 * D:6 * D], [], ["al"])
            po = Rot(S, "po", 4, [128, 512], F32, psum=True)
            bufs = (Rot(S, "xt", 2, [128, D], F32), S.sb("junk", [128, 512], F32), Rot(S, "ss", 2, [128, 2], F32),
                    Rot(S, "rs", 2, [128, 1], F32), Rot(S, "tmpb", 2, [128, D], F32), al)
            for i in tiles:
                ys, yts = [], []
                for hlf in range(2):
                    o_, ot = po.next()
                    for k in range(22):
                        S.pe(lambda e, o_=o_, k=k, i=i, hlf=hlf: e.matmul(o_[:], gT[:, k, i * 128:(i + 1) * 128], wd[:, k, hlf * 512:(hlf + 1) * 512], start=(k == 0), stop=(k == 21)),
                             ["wd"], [ot])
                    ys.append(o_)
                    yts.append(ot)
                post_residual(S, d, l, b, 1, ns, i, ys, yts, src, dst, bufs)
            S.flush()


def host_layout_shared(inp):
    m = host_layout(inp, 1, 0)
    m.pop("xin")
    m.pop("cT")
    return m


def host_layout_core(inp, ns, core):
    b0 = core * ns
    m = {}
    m["xin"] = np.ascontiguousarray(np.concatenate([inp["ctx"][b0:b0 + ns], inp["x"][b0:b0 + ns]], axis=1))
    cc = np.concatenate([inp["c"][b0:b0 + ns], inp["c_ctx"][None, :]], axis=0)
    m["cT"] = np.ascontiguousarray(cc.reshape(ns + 1, 8, 128).transpose(2, 1, 0))
    return m


def build_program(ns, consts, layout, stop_after=None):
    nc = bass.Bass("TRN2", target_bir_lowering=False)
    d = declare(nc, ns, consts, layout)
    with ExitStack() as es0:
        S = Sched(nc, es0)
        S.es = es0
        cst = {}
        cst["ident"] = S.sb("ident", [128, 128], BF16)
        cst["ones"] = S.sb("ones", [128, 128], BF16)
        cst["hgmask"] = S.sb("hgmask", [128, 2, 512], BF16)
        cst["rowmask"] = S.sb("rowmask", [128, 4], F32)
        cst["resetmask"] = S.sb("resetmask", [128, 512], F32)
        S.epsT = S.sb("epsT", [128, 1], F32)
        S.mpi = S.sb("mpi", [128, 1], F32)
        S.dma("sp", cst["ident"][:], d["ident"], [], [])
        S.dma("sp", cst["ones"][:], d["ones"], [], [])
        S.dma("sp", cst["hgmask"][:], d["hgmask"].rearrange("d p n -> p d n"), [], [])
        S.dma("sp", cst["rowmask"][:], d["rowmask"], [], [])
        S.dma("sp", cst["resetmask"][:], d["resetmask"], [], [])
        S.dve(lambda e: e.memset(S.epsT[:], EPS), [], [])
        S.dve(lambda e: e.memset(S.mpi[:], -math.pi), [], [])
        S.flush()
        phase_mod(S, nc, d, ns)
        for l in range(2):
            phase_filters(S, nc, d, l, "x")
        phase_filters(S, nc, d, 0, "c")
        import os
        STOP = int(os.environ.get("KSTOP", "99"))
        for l in range(2):
            need_ctx = (l == 0)
            if STOP < 2:
                break
            for b in range(ns):
                src0 = (lambda i, b=b: d["xin"][b, i * 128:(i + 1) * 128, :]) if l == 0 else (lambda i, b=b: d["xs"][b, i * 128:(i + 1) * 128, :])
                xs_t = lambda i, b=b: d["xs"][b, i * 128:(i + 1) * 128, :]
                tiles = list(range(NT)) if need_ctx else list(range(2, NT))
                with ExitStack() as esh:
                    S.es = esh
                    hT = S.sb("hT", [128, 8, T], BF16)
                    phase_norm(S, nc, d, l, b, 0, src0, list(range(NT)), hT, cst["ident"], ns)
                    if STOP >= 3:
                        phase_hgrn(S, nc, d, l, hT, cst)
                    if STOP >= 4:
                        phase_attn(S, nc, d, l, hT, cst, need_ctx)
                    if STOP >= 5:
                        phase_hyena(S, nc, d, l, hT, cst, need_ctx)
                    if STOP >= 6:
                        phase_merge(S, nc, d, l, b, ns, hT, tiles, src0, xs_t)
                if STOP < 7:
                    continue
                if l == 0:
                    dst = xs_t
                else:
                    dst = lambda i, b=b: d["out"][b, (i - 2) * 128:(i - 1) * 128, :]
                norm_fn = lambda hT2, l=l, b=b, tiles=tiles, xs_t=xs_t: phase_norm(S, nc, d, l, b, 1, xs_t, tiles, hT2, cst["ident"], ns)
                phase_ffn(S, nc, d, l, b, ns, norm_fn, tiles, xs_t, dst)
        S.es = es0
        S.flush()
    return nc


NS = 4


def kernel(**inputs):
    inp = {k: np.asarray(v, dtype=np.float32) for k, v in inputs.items()}
    consts = host_constants()
    shared = host_layout_shared(inp)
    cores = [host_layout_core(inp, NS, c) for c in range(NCORES)]
    layout = dict(shared)
    layout.update(cores[0])
    nc = build_program(NS, consts, layout)
    in_maps = []
    for c in range(NCORES):
        m = dict(consts)
        m.update(shared)
        m.update(cores[c])
        in_maps.append(m)
    res = run_bass_kernel_spmd(nc, in_maps, core_ids=list(range(NCORES)))
    out = np.concatenate([np.asarray(r["out"], dtype=np.float32) for r in res.results], axis=0)
    return out
```

```python
import numpy as np
import concourse.bass as bass
import concourse.mybir as mybir
from concourse.bass_utils import run_bass_kernel_spmd
from contextlib import ExitStack

F32 = mybir.dt.float32
BF16 = mybir.dt.bfloat16
AF = mybir.ActivationFunctionType
ALU = mybir.AluOpType
AX = mybir.AxisListType


class _Op:
    __slots__ = ("eng", "fn", "deps", "signal", "idx", "is_dma", "dsem", "dtarget", "seq")

    def __init__(self, eng, fn, is_dma=False):
        self.eng = eng
        self.fn = fn
        self.deps = []
        self.signal = False
        self.idx = 0
        self.is_dma = is_dma
        self.dsem = None
        self.dtarget = 0


class Sched:
    ENG = ("pe", "act", "dve", "pool", "sp")

    def __init__(self, nc, es, ndma=10, dma_queues=("sp", "pool", "act")):
        self.nc = nc
        self.es = es
        self.ops = []
        self.last_w = {}
        self.readers = {}
        self.sem = {e: es.enter_context(nc.semaphore("s_" + e)) for e in self.ENG}
        self.dsem = {q: [es.enter_context(nc.semaphore("d_%s%d" % (q, i))) for i in range(ndma)]
                     for q in dma_queues}
        self.dcount = {q: 0 for q in dma_queues}
        self.ndma = ndma
        self.last_dma = {}
        self.last_eng = {}
        self.n = 0

    def sb(self, name, shape, dt):
        self.uid = getattr(self, "uid", 0) + 1
        return self.es.enter_context(self.nc.sbuf_tensor("sb%d_%s" % (self.uid, name), list(shape), dt))

    def ps(self, name, shape, dt):
        self.uid = getattr(self, "uid", 0) + 1
        return self.es.enter_context(self.nc.psum_tensor("ps%d_%s" % (self.uid, name), list(shape), dt))

    def _record(self, op, reads, writes):
        deps = {}
        for t in reads:
            w = self.last_w.get(t)
            if w is not None:
                deps[id(w)] = (w, "raw")
        for t in writes:
            w = self.last_w.get(t)
            if w is not None and id(w) not in deps:
                deps[id(w)] = (w, "waw")
            for r in self.readers.get(t, ()):
                if id(r) not in deps:
                    deps[id(r)] = (r, "war")
        for p, kind in deps.values():
            if p is op:
                continue
            if not p.is_dma and p.eng == op.eng and not op.is_dma:
                if kind != "raw" or op.eng == "pe":
                    continue
            op.deps.append(p)
            if not p.is_dma:
                p.signal = True
        for t in reads:
            lst = self.readers.setdefault(t, [])
            if not op.is_dma:
                lst[:] = [r for r in lst if r.is_dma or r.eng != op.eng]
            lst.append(op)
        for t in writes:
            self.last_w[t] = op
            self.readers[t] = []
        self.ops.append(op)
        if not op.is_dma:
            self.last_eng[op.eng] = op
        self.n += 1
        return op

    def op(self, eng, fn, reads=(), writes=()):
        return self._record(_Op(eng, fn), reads, writes)

    def dma(self, q, out, in_, reads=(), writes=(), **kw):
        o = _Op(q, lambda e: e.dma_start(out=out, in_=in_, **kw), is_dma=True)
        i = self.dcount[q]
        self.dcount[q] += 1
        slot = i % self.ndma
        o.dsem = (q, slot)
        o.dtarget = 16 * (i // self.ndma + 1)
        prev = self.last_dma.get(o.dsem)
        if prev is not None:
            o.deps.append(prev)
        self.last_dma[o.dsem] = o
        return self._record(o, reads, writes)

    def barrier(self):
        lasts = list(self.last_eng.values()) + list(self.last_dma.values())
        for p in lasts:
            if not p.is_dma:
                p.signal = True
        for e in self.ENG:
            o = _Op(e, None)
            o.deps = [p for p in lasts if p.is_dma or p.eng != e]
            self.ops.append(o)
        self.last_w = {}
        self.readers = {}

    def finish(self):
        self.flush()

    def eps_ap(self, like):
        return self.epsT[0:like.shape[0], 0:1]

    def pe(self, fn, r=(), w=()):
        return self.op("pe", fn, r, w)

    def act(self, fn, r=(), w=()):
        return self.op("act", fn, r, w)

    def dve(self, fn, r=(), w=()):
        return self.op("dve", fn, r, w)

    def pool(self, fn, r=(), w=()):
        return self.op("pool", fn, r, w)

    def flush(self):
        self.barrier()
        if not hasattr(self, "_cnt"):
            self._cnt = {e: 0 for e in self.ENG}
            self._known = {e: {} for e in self.ENG}
        per = {e: [] for e in self.ENG}
        for o in self.ops:
            if o.signal:
                self._cnt[o.eng] += 1
                o.idx = self._cnt[o.eng]
            per[o.eng].append(o)
        self.ops = []
        nc = self.nc
        with nc.Block() as block:
            def emit(engname):
                def body(e):
                    known = self._known[engname]
                    for o in per[engname]:
                        need = {}
                        for p in o.deps:
                            if p.is_dma:
                                key, val = p.dsem, p.dtarget
                            else:
                                key, val = p.eng, p.idx
                            if known.get(key, 0) >= val:
                                continue
                            if need.get(key, 0) < val:
                                need[key] = val
                        for key, val in need.items():
                            s = self.dsem[key[0]][key[1]] if isinstance(key, tuple) else self.sem[key]
                            e.wait_ge(s, val)
                            known[key] = val
                        if o.fn is None:
                            continue
                        ins = o.fn(e)
                        if o.is_dma:
                            ins.then_inc(self.dsem[o.dsem[0]][o.dsem[1]], 16)
                        elif o.signal:
                            ins.then_inc(self.sem[engname], 1)
                return body
            block.tensor(emit("pe"))
            block.scalar(emit("act"))
            block.vector(emit("dve"))
            block.gpsimd(emit("pool"))
            block.sync(emit("sp"))


D = 1024
L = 2048
LC = 256
T = L + LC
NT = T // 128
DIN = 7936
DFF = 2816
NCORES = 8
EPS = 1e-6
import math
import ml_dtypes
NPBF = ml_dtypes.bfloat16

C_HQ, C_ZF, C_ZB, C_HV, C_ZG, C_AQ, C_AK, C_AV, C_HY, C_GT = 0, 512, 1024, 1536, 2048, 2560, 3072, 3200, 3328, 4864

TG = [(0, 512), (512, 512), (1024, 512), (1536, 512), (2048, 256)]


def kmajor(w):
    k, n = w.shape
    return np.ascontiguousarray(w.reshape(k // 128, 128, n).transpose(1, 0, 2))


def rep128(v):
    return np.ascontiguousarray(np.broadcast_to(np.asarray(v)[None, :], (128, v.shape[-1])))


def host_constants():
    c = {}
    c["ident"] = np.eye(128, dtype=np.float32).astype(NPBF)
    c["ones"] = np.ones((128, 128), np.float32).astype(NPBF)
    s = np.arange(128)[:, None]
    t = np.arange(128)[None, :]
    same = (s // 32) == (t // 32)
    mf = (same & (s <= t)).astype(np.float32)
    mb = (same & (s >= t)).astype(np.float32)
    c["hgmask"] = np.stack([np.tile(mf, (1, 4)), np.tile(mb, (1, 4))]).astype(NPBF)
    rm = np.zeros((128, 4), np.float32)
    for j in range(4):
        rm[32 * j:32 * j + 32, j] = 1.0
    c["rowmask"] = rm
    rs = np.ones((128, 512), np.float32)
    rs[:, ::32] = 0.0
    c["resetmask"] = rs
    rows = L // 64
    row = np.repeat(np.arange(rows), 64).astype(np.float32)
    col = np.tile(np.arange(64), rows).astype(np.float32)
    inv = (10000.0 ** (-np.arange(16, dtype=np.float32) / 16)).astype(np.float32)
    ang = np.concatenate([row[:, None] * inv, col[:, None] * inv], axis=-1).astype(np.float32)
    c["cos8"] = np.tile(np.cos(ang).astype(np.float32), (1, 8))
    c["sin8"] = np.tile(np.sin(ang).astype(np.float32), (1, 8))
    for Ls, tag in ((L, "x"), (LC, "c")):
        tt = np.linspace(0.0, 1.0, Ls, dtype=np.float32)[:, None]
        w = (2.0 * math.pi * np.arange(Ls, dtype=np.float32)[:, None] / Ls).astype(np.float32)
        f = np.linspace(1e-4, 15, 16, dtype=np.float32)[None, :]
        z = np.concatenate([tt, np.cos(f * w), -np.sin(f * w)], axis=-1).astype(np.float32)
        c["zT" + tag] = np.ascontiguousarray(z.T)
        deltas = np.abs(np.linspace(math.log(1e-2) / 1.5, math.log(1e-2) / 0.3, 512, dtype=np.float32))
        c["decay" + tag] = np.exp(-tt * deltas[None, :]).astype(np.float32)
        N = 2 * Ls
        om = 2.0 * np.pi * (np.arange(Ls, dtype=np.float64) + 0.5) / N
        tj = np.arange(Ls, dtype=np.float64)
        ph = tj[:, None] * om[None, :]
        fwd = np.concatenate([np.cos(ph), -np.sin(ph)], axis=1)
        inv_m = np.concatenate([np.cos(ph).T, -np.sin(ph).T], axis=0) * (2.0 / N)
        c["fwd" + tag] = kmajor(fwd.astype(np.float32)).astype(NPBF)
        c["inv" + tag] = kmajor(inv_m.astype(np.float32)).astype(NPBF)
    return c


def host_layout(inp, ns, core):
    b0 = core * ns
    m = {}
    m["xin"] = np.ascontiguousarray(np.concatenate([inp["ctx"][b0:b0 + ns], inp["x"][b0:b0 + ns]], axis=1))
    cc = np.concatenate([inp["c"][b0:b0 + ns], inp["c_ctx"][None, :]], axis=0)
    m["cT"] = np.ascontiguousarray(cc.reshape(ns + 1, 8, 128).transpose(2, 1, 0))
    for nm in ("w_ada", "w_in", "w_oa", "w_ob", "w_oc", "w_out", "w_up", "w_down"):
        m[nm] = np.stack([kmajor(inp[nm][l]) for l in range(2)])
    m["b_ada"] = np.stack([rep128(inp["b_ada"][l]) for l in range(2)])
    m["g4"] = np.stack([np.stack([rep128(inp[nm][l]) for nm in ("g_pre_mix", "g_post_mix", "g_pre_ffn", "g_post_ffn")])
                        for l in range(2)])
    lb = inp["hg_lower_bounds"]
    m["lbT"] = np.ascontiguousarray(lb.reshape(2, 2, 4, 128).transpose(3, 0, 1, 2))
    m["hgn"] = np.ascontiguousarray(inp["hg_norm"].T)
    m["gq"] = np.stack([rep128(np.tile(inp["q_norm"][l], 8)) for l in range(2)])
    m["gk"] = np.stack([rep128(np.tile(inp["k_norm"][l], 2)) for l in range(2)])
    cw = inp["hy_conv_w"]
    m["hycw"] = np.ascontiguousarray(cw.reshape(2, 3, 12, 128).transpose(3, 0, 2, 1))
    m["hycb"] = np.ascontiguousarray(inp["hy_conv_b"].reshape(2, 12, 128).transpose(2, 0, 1))
    m["hybias"] = np.ascontiguousarray(inp["hy_bias"].reshape(2, 4, 128).transpose(2, 0, 1))
    m["hyw1"] = np.ascontiguousarray(inp["hy_w1"])
    m["hyb1"] = np.ascontiguousarray(inp["hy_b1"].T)
    m["hywi"] = np.ascontiguousarray(inp["hy_wi"])
    m["hybi"] = np.ascontiguousarray(inp["hy_bi"].transpose(2, 0, 1))
    m["hyfreq"] = np.ascontiguousarray(inp["hy_freq"].T)
    m["hywl"] = np.ascontiguousarray(inp["hy_w_last"])
    fw = inp["ffn_conv_w"]
    m["ffcw"] = np.ascontiguousarray(fw.reshape(2, 3, 44, 128).transpose(3, 0, 2, 1))
    m["ffcb"] = np.ascontiguousarray(inp["ffn_conv_b"].reshape(2, 44, 128).transpose(2, 0, 1))
    return m


class Rot:
    def __init__(self, S, name, n, shape, dt, psum=False):
        self.bufs = [(S.ps if psum else S.sb)("%s%d" % (name, i), shape, dt) for i in range(n)]
        self.name = name
        self.i = 0

    def next(self):
        j = self.i % len(self.bufs)
        self.i += 1
        return self.bufs[j], (self.name, j)


class Ctx:
    pass


def bc(ap, shape):
    return ap.to_broadcast(list(shape))


def declare(nc, ns, consts, layout):
    d = {}
    for k, v in list(consts.items()) + list(layout.items()):
        dt = BF16 if v.dtype == NPBF else F32
        d[k] = nc.dram_tensor(k, list(v.shape), dt, kind="ExternalInput").ap()
    d["out"] = nc.dram_tensor("out", [ns, L, D], F32, kind="ExternalOutput").ap()
    d["xs"] = nc.dram_tensor("xs", [ns, T, D], F32, kind="Internal").ap()
    d["modrep"] = nc.dram_tensor("modrep", [2, ns + 1, 128, 6 * D], F32, kind="Internal").ap()
    d["kspx"] = nc.dram_tensor("kspx", [2, 128, 32, 512], F32, kind="Internal").ap()
    d["kspc"] = nc.dram_tensor("kspc", [128, 4, 512], F32, kind="Internal").ap()
    for nm in ("aTs", "bTs", "cTs"):
        d[nm] = nc.dram_tensor(nm, [128, 4, T], BF16, kind="Internal").ap()
    return d


def rstd_ops(S, out, ssum, inv_n, rt, wt):
    S.act(lambda e: e.activation(out, ssum, AF.Sqrt, bias=S.eps_ap(out), scale=inv_n), rt, wt)
    S.dve(lambda e: e.reciprocal(out, out), wt, wt)


def load_cast(S, stg, dst, src, kc, n, wtoks, eng):
    cap = stg.bufs[0].shape[1]
    kstep = max(1, min(kc, cap // n))
    for k0 in range(0, kc, kstep):
        k1 = min(kc, k0 + kstep)
        buf, tok = stg.next()
        view = buf[:, 0:(k1 - k0) * n].rearrange("p (k n) -> p k n", k=k1 - k0)
        S.dma("sp", view, src[:, k0:k1, :], reads=[], writes=[tok])
        S.castn = getattr(S, "castn", 0) + 1
        if S.castn % 2 == 0:
            S.act(lambda e, view=view, k0=k0, k1=k1: e.copy(dst[:, k0:k1, :], view), [tok], wtoks)
        else:
            S.dve(lambda e, view=view, k0=k0, k1=k1: e.tensor_copy(dst[:, k0:k1, :], view), [tok], wtoks)


def phase_mod(S, nc, d, ns):
    with ExitStack() as es:
        S.es = es
        cT = S.sb("cT_sb", [128, 8, ns + 1], F32)
        sil = S.sb("sil", [128, 8, ns + 1], F32)
        srep = S.sb("srep", [128, 8, ns + 1, 128], BF16)
        onesf = S.sb("onesf", [128, 128], F32)
        stg = Rot(S, "stg", 2, [128, 4096], F32)
        wb = Rot(S, "wb", 2, [128, 8, 512], BF16)
        bb = Rot(S, "bb", 2, [128, 512], F32)
        gb = Rot(S, "gb", 2, [128, 512], F32)
        ob = Rot(S, "ob", 3, [128, 512], F32)
        pp = Rot(S, "pm", 3, [128, 512], F32, psum=True)
        S.dma("sp", cT[:], d["cT"], [], ["cT"])
        S.act(lambda e: e.activation(sil[:], cT[:], AF.Silu), ["cT"], ["sil"])
        S.dve(lambda e: e.memset(onesf[:], 1.0), [], ["onesf"])
        for k in range(8):
            for s in range(ns + 1):
                S.dve(lambda e, k=k, s=s: e.tensor_scalar(srep[:, k, s, :], onesf[:], sil[:, k, s:s + 1], None, ALU.mult),
                      ["sil", "onesf"], ["srep"])
        for l in range(2):
            for cb in range(12):
                sec, half = cb // 2, cb % 2
                w, wt = wb.next()
                load_cast(S, stg, w[:], d["w_ada"][l, :, :, cb * 512:(cb + 1) * 512], 8, 512, [wt], "pool")
                b, bt = bb.next()
                S.dma("sp", b[:], d["b_ada"][l, :, cb * 512:(cb + 1) * 512], [], [bt])
                g, gt = gb.next()
                if sec in (1, 2, 4, 5):
                    gi = {1: 0, 2: 1, 4: 2, 5: 3}[sec]
                    S.dma("sp", g[:], d["g4"][l, gi, :, half * 512:(half + 1) * 512], [], [gt])
                for s in range(ns + 1):
                    p, pt = pp.next()
                    for k in range(8):
                        S.pe(lambda e, p=p, k=k, s=s, w=w: e.matmul(p[:], srep[:, k, s, :], w[:, k, :], start=(k == 0), stop=(k == 7)),
                             ["srep", wt], [pt])
                    o, ot = ob.next()
                    if sec in (0, 3):
                        S.dve(lambda e, o=o, p=p, b=b: e.tensor_tensor(o[:], p[:], b[:], ALU.add), [pt, bt], [ot])
                    elif sec in (1, 4):
                        S.dve(lambda e, o=o, p=p, b=b: e.scalar_tensor_tensor(o[:], p[:], 1.0, b[:], ALU.add, ALU.add), [pt, bt], [ot])
                        S.dve(lambda e, o=o, g=g: e.tensor_tensor(o[:], o[:], g[:], ALU.mult), [ot, gt], [ot])
                    else:
                        S.dve(lambda e, o=o, p=p, b=b: e.tensor_tensor(o[:], p[:], b[:], ALU.add), [pt, bt], [ot])
                        S.dve(lambda e, o=o, g=g: e.tensor_tensor(o[:], o[:], g[:], ALU.mult), [ot, gt], [ot])
                    S.dma("sp", d["modrep"][l, s, :, cb * 512:(cb + 1) * 512], o[:], [ot], [])
        S.flush()


def phase_norm(S, nc, d, l, b, sub, src, tiles, hT, ident, ns):
    with ExitStack() as es:
        S.es = es
        vs = S.sb("vs", [128, 2, 2, D], F32)
        for who, s in ((0, b), (1, ns)):
            if who == 1 and 0 not in tiles:
                continue
            S.dma("sp", vs[:, who, 0, :], d["modrep"][l, s, :, (3 * sub) * D:(3 * sub + 1) * D], [], ["vs"])
            S.dma("sp", vs[:, who, 1, :], d["modrep"][l, s, :, (3 * sub + 1) * D:(3 * sub + 2) * D], [], ["vs"])
        xt = Rot(S, "xt", 2, [128, D], F32)
        junk = S.sb("junk", [128, D], F32)
        ss = Rot(S, "ss", 2, [128, 1], F32)
        rs = Rot(S, "rs", 2, [128, 1], F32)
        h1 = Rot(S, "h1", 2, [128, D], F32)
        hb = Rot(S, "hb", 2, [128, D], BF16)
        pt_ = Rot(S, "ptr", 2, [128, 8, 128], BF16, psum=True)
        for i in tiles:
            who = 1 if i < 2 else 0
            x, xtok = xt.next()
            S.dma("sp", x[:], src(i), [], [xtok])
            s1, s1t = ss.next()
            S.act(lambda e, x=x, s1=s1: e.activation(junk[:], x[:], AF.Square, accum_out=s1[:]), [xtok], ["junk", s1t])
            r1, r1t = rs.next()
            rstd_ops(S, r1[:], s1[:], 1.0 / D, [s1t], [r1t])
            h, ht = h1.next()
            S.dve(lambda e, h=h, x=x, r1=r1, who=who: e.scalar_tensor_tensor(h[:], x[:], r1[:], vs[:, who, 1, :], ALU.mult, ALU.mult),
                  [xtok, r1t, "vs"], [ht])
            hh, hht = hb.next()
            S.pool(lambda e, hh=hh, h=h, who=who: e.tensor_tensor(hh[:], h[:], vs[:, who, 0, :], ALU.add), [ht, "vs"], [hht])
            p, ptk = pt_.next()
            for k in range(8):
                S.pe(lambda e, p=p, hh=hh, k=k: e.transpose(p[:, k, :], hh[:, k * 128:(k + 1) * 128], ident[:]), [hht, "ident"], [ptk])
            S.act(lambda e, p=p, i=i: e.copy(hT[:, :, i * 128:(i + 1) * 128], p[:]), [ptk], [("hT", i)])
        S.flush()


def proj_fm(S, pp, w, wtok, hT, ncols_off, t0, n):
    p, pt = pp.next()
    for k in range(8):
        S.pe(lambda e, p=p, k=k: e.matmul(p[:, 0:n], w[:, k, ncols_off:ncols_off + 128], hT[:, k, t0:t0 + n],
                                          start=(k == 0), stop=(k == 7)),
             [wtok] + [("hT", i) for i in range(t0 // 128, (t0 + n) // 128)], [pt])
    return p, pt


def phase_hgrn(S, nc, d, l, hT, cst):
    ident, ones, hgmask, rowmask, resetmask = cst["ident"], cst["ones"], cst["hgmask"], cst["rowmask"], cst["resetmask"]
    with ExitStack() as es:
        S.es = es
        stg = Rot(S, "stg", 1, [128, 2048], F32)
        wq = S.sb("wq", [128, 8, 512], BF16)
        wz = S.sb("wz", [128, 8, 512], BF16)
        wv = wz
        vtok = S.sb("vtok", [128, NT, 512], BF16)
        kglT = S.sb("kglT", [128, 4, T], BF16)
        oT = S.sb("oT", [128, 4, T], F32)
        qeT = S.sb("qeT", [128, 4, T], BF16)
        keT = S.sb("keT", [128, 4, T], BF16)
        gl = S.sb("gl", [128, 4, T // 32], F32)
        lbv = S.sb("lbv", [128, 2, 2, 4], F32)
        lo = S.sb("lo", [128, 2, 4], F32)
        oml = S.sb("oml", [128, 2, 4], F32)
        gn = S.sb("gn", [128, 2], F32)
        Sf = S.sb("Sf", [128, 4, 128], F32)
        Sb = S.sb("Sb", [128, 4, 128], BF16)
        pp = Rot(S, "pp", 2, [128, 512], F32, psum=True)
        S.dma("sp", lbv[:], d["lbT"], [], ["lbv"])
        S.dma("sp", gn[:], d["hgn"], [], ["gn"])
        if l == 0:
            S.dve(lambda e: e.memset(lo[:], 0.0), [], ["lo"])
        else:
            S.dve(lambda e: e.tensor_tensor(lo[:], lbv[:, 1, :, :], lbv[:, 0, :, :], ALU.subtract), ["lbv"], ["lo"])
            S.act(lambda e: e.activation(lo[:], lo[:], AF.Sigmoid), ["lo"], ["lo"])
        S.dve(lambda e: e.tensor_scalar(oml[:], lo[:], -1.0, 1.0, ALU.mult, ALU.add), ["lo"], ["oml"])
        S.dve(lambda e: e.memset(oT[:], 0.0), [], ["oT"])
        load_cast(S, stg, wq[:], d["w_in"][l, :, :, C_HQ:C_HQ + 512], 8, 512, ["wq"], "pool")
        load_cast(S, stg, wv[:], d["w_in"][l, :, :, C_HV:C_HV + 512], 8, 512, ["wz"], "pool")
        for i in range(NT):
            p, pt = pp.next()
            for k in range(8):
                S.pe(lambda e, p=p, k=k, i=i: e.matmul(p[:], hT[:, k, i * 128:(i + 1) * 128], wv[:, k, :], start=(k == 0), stop=(k == 7)),
                     ["wz", ("hT", i)], [pt])
            S.act(lambda e, p=p, i=i: e.copy(vtok[:, i, :], p[:]), [pt], [("vtok", i)])
        tA = Rot(S, "tA", 2, [128, 512], F32)
        tB = Rot(S, "tB", 2, [128, 512], F32)
        tC = Rot(S, "tC", 2, [128, 512], F32)
        tK = Rot(S, "tK", 1, [128, 512], F32)
        tD = Rot(S, "tD", 1, [128, 512], F32)
        pkt = Rot(S, "pkt", 1, [128, 4, 128], BF16, psum=True)
        pA = Rot(S, "pA", 1, [128, 4, 128], F32, psum=True)
        po = Rot(S, "po", 2, [128, 4, 128], F32, psum=True)
        pu4 = Rot(S, "pu4", 2, [128, 4, 128], F32, psum=True)
        ktm = S.sb("ktm", [128, 4, 512], BF16)
        AT = S.sb("AT", [128, 4, 128], BF16)
        if l == 0 and not getattr(S, "_rep_h", False):
            S._rep_h = True
            print("hgrn sbuf remaining", nc.sbuf_bytes_remaining)
        for dr in range(2):
            load_cast(S, stg, wz[:], d["w_in"][l, :, :, (C_ZF, C_ZB)[dr]:(C_ZF, C_ZB)[dr] + 512], 8, 512, ["wz"], "pool")
            for h in range(4):
                for (t0, n) in TG:
                    nch = n // 32
                    p, pt = proj_fm(S, pp, wz, "wz", hT, h * 128, t0, n)
                    a, at = tA.next()
                    S.act(lambda e, a=a, p=p, n=n: e.activation(a[:, 0:n], p[:, 0:n], AF.Sigmoid), [pt], [at])
                    S.dve(lambda e, a=a, n=n, h=h, dr=dr: e.tensor_scalar(a[:, 0:n], a[:, 0:n], oml[:, dr, h:h + 1], lo[:, dr, h:h + 1], ALU.mult, ALU.add),
                          [at, "oml", "lo"], [at])
                    kk, kt = tK.next()
                    S.pool(lambda e, kk=kk, a=a, n=n: e.tensor_scalar(kk[:, 0:n], a[:, 0:n], -1.0, 1.0, ALU.mult, ALU.add), [at], [kt])
                    b_, bt = tB.next()
                    S.act(lambda e, b_=b_, a=a, n=n: e.activation(b_[:, 0:n], a[:, 0:n], AF.Ln), [at], [bt])
                    c_, ct = tC.next()
                    S.dve(lambda e, c_=c_, b_=b_, n=n: e.tensor_tensor_scan(c_[:, 0:n], resetmask[:, 0:n], b_[:, 0:n], 0.0, ALU.mult, ALU.add),
                          [bt, "resetmask"], [ct])
                    if dr == 1:
                        c3 = c_[:, 0:n].rearrange("p (c t) -> p c t", t=32)
                        b3 = b_[:, 0:n].rearrange("p (c t) -> p c t", t=32)
                        S.pool(lambda e, b_=b_, c_=c_, n=n: e.tensor_tensor(b_[:, 0:n], b_[:, 0:n], c_[:, 0:n], ALU.subtract), [bt, ct], [bt])
                        g_, gt_ = tD.next()
                        g3 = g_[:, 0:n].rearrange("p (c t) -> p c t", t=32)
                        S.dve(lambda e, c3=c3, b3=b3, g3=g3, nch=nch: e.tensor_tensor(g3, b3, bc(c3[:, :, 31:32], [128, nch, 32]), ALU.add), [bt, ct], [gt_])
                        c_, ct = g_, gt_
                    S.act(lambda e, a=a, c_=c_, n=n: e.activation(a[:, 0:n], c_[:, 0:n], AF.Exp), [ct, at], [at])
                    a3 = a[:, 0:n].rearrange("p (c t) -> p c t", t=32)
                    pos = 31 if dr == 0 else 0
                    S.pool(lambda e, a3=a3, h=h, t0=t0, nch=nch, pos=pos: e.tensor_copy(gl[:, h, t0 // 32:t0 // 32 + nch], a3[:, :, pos]), [at], ["gl"])
                    q, qt = proj_fm(S, pp, wq, "wq", hT, h * 128, t0, n)
                    S.dve(lambda e, q=q, a=a, h=h, t0=t0, n=n: e.tensor_tensor(qeT[:, h, t0:t0 + n], q[:, 0:n], a[:, 0:n], ALU.mult),
                          [qt, at], [("qeT", h)])
                    S.act(lambda e, b_=b_, c_=c_, n=n: e.activation(b_[:, 0:n], c_[:, 0:n], AF.Exp, scale=-1.0), [ct, bt], [bt])
                    S.pool(lambda e, b_=b_, kk=kk, h=h, t0=t0, n=n: e.tensor_tensor(keT[:, h, t0:t0 + n], kk[:, 0:n], b_[:, 0:n], ALU.mult),
                           [bt, kt], [("keT", h)])
                    S.dve(lambda e, h=h, t0=t0, n=n, nch=nch: e.tensor_tensor(kglT[:, h, t0:t0 + n].rearrange("p (c t) -> p c t", t=32),
                                                                             keT[:, h, t0:t0 + n].rearrange("p (c t) -> p c t", t=32),
                                                                             bc(gl[:, h, t0 // 32:t0 // 32 + nch].unsqueeze(2), [128, nch, 32]), ALU.mult),
                          [("keT", h), "gl"], [("kglT", h)])
            S.dve(lambda e: e.memset(Sf[:], 0.0), [], [("Sf", h) for h in range(4)])
            S.pool(lambda e: e.memset(Sb[:], 0.0), [], [("Sb", h) for h in range(4)])
            order = list(range(NT)) if dr == 0 else [1, 0] + list(range(NT - 1, 1, -1))
            jord = [0, 1, 2, 3] if dr == 0 else [3, 2, 1, 0]
            for i in order:
                tk = slice(i * 128, (i + 1) * 128)
                pk, pkt_ = pkt.next()
                for h in range(4):
                    S.pe(lambda e, pk=pk, h=h, tk=tk: e.transpose(pk[:, h, :], kglT[:, h, tk], ident[:]), [("kglT", h), "ident"], [pkt_])
                for j in range(4):
                    eng = S.act if j % 2 == 0 else S.dve
                    if j % 2 == 0:
                        S.dve(lambda e, pk=pk, j=j: e.tensor_scalar(ktm[:, j, :], pk[:].rearrange("p h k -> p (h k)"), rowmask[:, j:j + 1], None, ALU.mult),
                              [pkt_, "rowmask"], [("ktm", j)])
                    else:
                        S.dve(lambda e, pk=pk, j=j: e.tensor_scalar(ktm[:, j, :], pk[:].rearrange("p h k -> p (h k)"), rowmask[:, j:j + 1], None, ALU.mult),
                              [pkt_, "rowmask"], [("ktm", j)])
                pa, pat = pA.next()
                for h in range(4):
                    S.pe(lambda e, pa=pa, h=h, tk=tk: e.matmul(pa[:, h, :], keT[:, h, tk], qeT[:, h, tk], start=True, stop=True),
                         [("keT", h), ("qeT", h)], [pat])
                S.dve(lambda e, pa=pa, dr=dr: e.tensor_tensor(AT[:].rearrange("p h t -> p (h t)"), pa[:].rearrange("p h t -> p (h t)"), hgmask[:, dr, :], ALU.mult),
                      [pat, "hgmask"], ["AT"])
                o_, ot = po.next()
                for h in range(4):
                    S.pe(lambda e, o_=o_, h=h, i=i: e.matmul(o_[:, h, :], vtok[:, i, h * 128:(h + 1) * 128], AT[:, h, :], start=(h == 0), stop=False,
                                                             skip_group_check=True),
                         [("vtok", i), "AT"], [ot])
                for jj, j in enumerate(jord):
                    c0 = i * 128 + 32 * j
                    ch = c0 // 32
                    u4, u4n = pu4.next()
                    for h in range(4):
                        S.pe(lambda e, o_=o_, h=h, j=j, c0=c0, jj=jj: e.matmul(o_[:, h, 32 * j:32 * j + 32], Sb[:, h, :], qeT[:, h, c0:c0 + 32],
                                                                               start=False, stop=(jj == 3), skip_group_check=True),
                             [("Sb", h), ("qeT", h)], [ot])
                        S.pe(lambda e, u4=u4, h=h, j=j, i=i: e.matmul(u4[:, h, :], ktm[:, j, h * 128:(h + 1) * 128], vtok[:, i, h * 128:(h + 1) * 128], start=True, stop=True),
                             [("ktm", j), ("vtok", i)], [u4n])
                    for h in range(4):
                        S.dve(lambda e, u4=u4, h=h, ch=ch: e.scalar_tensor_tensor(Sf[:, h, :], Sf[:, h, :], gl[:, h, ch:ch + 1], u4[:, h, :], ALU.mult, ALU.add),
                              [u4n, ("Sf", h), "gl"], [("Sf", h)])
                        S.act(lambda e, h=h: e.copy(Sb[:, h, :], Sf[:, h, :]), [("Sf", h)], [("Sb", h)])
                S.dve(lambda e, o_=o_, tk=tk: e.tensor_tensor(oT[:, :, tk], oT[:, :, tk], o_[:], ALU.add), [ot, "oT"], ["oT"])
        load_cast(S, stg, wz[:], d["w_in"][l, :, :, C_ZG:C_ZG + 512], 8, 512, ["wz"], "pool")
        sq = Rot(S, "sq", 2, [128, 512], BF16)
        aTb = Rot(S, "aTb", 2, [128, 512], BF16)
        for h in range(4):
            for (t0, n) in TG:
                s_, st = sq.next()
                S.act(lambda e, s_=s_, h=h, t0=t0, n=n: e.activation(s_[:, 0:n], oT[:, h, t0:t0 + n], AF.Square), ["oT"], [st])
                p, pt = pp.next()
                S.pe(lambda e, p=p, s_=s_, n=n: e.matmul(p[:, 0:n], ones[:], s_[:, 0:n], start=True, stop=True), [st, "ones"], [pt])
                a, at = tA.next()
                rstd_ops(S, a[:, 0:n], p[:, 0:n], 1.0 / 128, [pt], [at])
                z, zt = proj_fm(S, pp, wz, "wz", hT, h * 128, t0, n)
                b_, bt = tB.next()
                S.act(lambda e, b_=b_, z=z, n=n: e.activation(b_[:, 0:n], z[:, 0:n], AF.Silu), [zt], [bt])
                c_, ct = tC.next()
                S.dve(lambda e, c_=c_, a=a, h=h, t0=t0, n=n: e.scalar_tensor_tensor(c_[:, 0:n], oT[:, h, t0:t0 + n], gn[:, l:l + 1], a[:, 0:n], ALU.mult, ALU.mult),
                      ["oT", at, "gn"], [ct])
                o2, o2t = aTb.next()
                S.pool(lambda e, o2=o2, c_=c_, b_=b_, n=n: e.tensor_tensor(o2[:, 0:n], c_[:, 0:n], b_[:, 0:n], ALU.mult), [ct, bt], [o2t])
                S.dma("sp", d["aTs"][:, h, t0:t0 + n], o2[:, 0:n], [o2t], [])
        S.flush()


def phase_attn(S, nc, d, l, hT, cst, need_ctx):
    ident = cst["ident"]
    with ExitStack() as es:
        S.es = es
        stg = Rot(S, "stg", 2, [128, 2048], F32)
        wqkv = S.sb("wqkv", [128, 8, 768], BF16)
        qT = S.sb("qT", [128, 4, T], BF16)
        kT = S.sb("kT", [128, T], BF16)
        vaug = S.sb("vaug", [128, NT, 2, 65], BF16)
        btok = S.sb("btok", [128, NT, 512], BF16)
        gq = S.sb("gq", [128, 512], F32)
        gk = S.sb("gk", [128, 128], F32)
        S.dma("sp", gq[:], d["gq"][l], [], ["gq"])
        S.dma("sp", gk[:], d["gk"][l], [], ["gk"])
        S.dve(lambda e: e.memset(vaug[:], 1.0), [], ["vaug"])
        for (c0, c1, k0) in ((0, 512, 0), (512, 768, 512)):
            load_cast(S, stg, wqkv[:, :, c0:c1], d["w_in"][l, :, :, C_AQ + c0:C_AQ + c1], 8, c1 - c0, ["wqkv"], "pool")
        with ExitStack() as es2:
            S.es = es2
            pq = Rot(S, "pq", 2, [128, 512], F32, psum=True)
            pkv = Rot(S, "pkv", 2, [128, 256], F32, psum=True)
            ptq = Rot(S, "ptq", 1, [128, 4, 128], BF16, psum=True)
            ptk = Rot(S, "ptk", 1, [128, 128], BF16, psum=True)
            sqb = Rot(S, "sqb", 2, [128, 640], F32)
            ssb = Rot(S, "ssb", 2, [128, 10], F32)
            qn = Rot(S, "qn", 2, [128, 640], F32)
            qr = Rot(S, "qr", 2, [128, 640], BF16)
            qrp = Rot(S, "qrp", 2, [128, 512], BF16)
            cs = Rot(S, "cs", 2, [128, 2, 256], F32)
            tmp1 = Rot(S, "tmp1", 2, [128, 320], F32)
            tmp2 = Rot(S, "tmp2", 2, [128, 320], F32)
            for i in range(NT):
                p, pt = pq.next()
                for k in range(8):
                    S.pe(lambda e, p=p, k=k, i=i: e.matmul(p[:], hT[:, k, i * 128:(i + 1) * 128], wqkv[:, k, 0:512], start=(k == 0), stop=(k == 7)),
                         ["wqkv", ("hT", i)], [pt])
                p2, p2t = pkv.next()
                for k in range(8):
                    S.pe(lambda e, p2=p2, k=k, i=i: e.matmul(p2[:], hT[:, k, i * 128:(i + 1) * 128], wqkv[:, k, 512:768], start=(k == 0), stop=(k == 7)),
                         ["wqkv", ("hT", i)], [p2t])
                S.act(lambda e, p2=p2, i=i: e.copy(vaug[:, i, :, 0:64], p2[:, 128:256].rearrange("p (g d) -> p g d", g=2)), [p2t, "vaug"], [("vaug", i)])
                sq_, sqt = sqb.next()
                S.act(lambda e, sq_=sq_, p=p: e.activation(sq_[:, 0:512], p[:], AF.Square), [pt], [sqt])
                S.act(lambda e, sq_=sq_, p2=p2: e.activation(sq_[:, 512:640], p2[:, 0:128], AF.Square), [p2t], [sqt])
                s_, st = ssb.next()
                S.dve(lambda e, s_=s_, sq_=sq_: e.tensor_reduce(s_[:], sq_[:].rearrange("p (h d) -> p h d", d=64), AX.X, ALU.add), [sqt], [st])
                rstd_ops(S, s_[:], s_[:], 1.0 / 64, [st], [st])
                q_, qnt = qn.next()
                S.dve(lambda e, q_=q_, p=p, s_=s_: e.tensor_tensor(q_[:, 0:512].rearrange("p (h d) -> p h d", d=64), p[:].rearrange("p (h d) -> p h d", d=64),
                                                                   bc(s_[:, 0:8].unsqueeze(2), [128, 8, 64]), ALU.mult), [pt, st], [qnt])
                S.dve(lambda e, q_=q_, p2=p2, s_=s_: e.tensor_tensor(q_[:, 512:640].rearrange("p (h d) -> p h d", d=64), p2[:, 0:128].rearrange("p (h d) -> p h d", d=64),
                                                                     bc(s_[:, 8:10].unsqueeze(2), [128, 2, 64]), ALU.mult), [p2t, st], [qnt])
                S.pool(lambda e, q_=q_: e.tensor_tensor(q_[:, 0:512], q_[:, 0:512], gq[:], ALU.mult), [qnt, "gq"], [qnt])
                S.pool(lambda e, q_=q_: e.tensor_tensor(q_[:, 512:640], q_[:, 512:640], gk[:], ALU.mult), [qnt, "gk"], [qnt])
                r_, rt = qr.next()
                if i < 2:
                    S.dve(lambda e, r_=r_, q_=q_: e.tensor_copy(r_[:], q_[:]), [qnt], [rt])
                else:
                    c_, ct = cs.next()
                    r0 = (i - 2) * 128
                    S.dma("sp", c_[:, 0, :], d["cos8"][r0:r0 + 128, :], [], [ct])
                    S.dma("sp", c_[:, 1, :], d["sin8"][r0:r0 + 128, :], [], [ct])
                    x1 = q_[:].rearrange("p (n two) -> p n two", two=2)[:, :, 0]
                    x2 = q_[:].rearrange("p (n two) -> p n two", two=2)[:, :, 1]
                    o1 = r_[:].rearrange("p (n two) -> p n two", two=2)[:, :, 0]
                    o2 = r_[:].rearrange("p (n two) -> p n two", two=2)[:, :, 1]
                    t1, t1t = tmp1.next()
                    t2, t2t = tmp2.next()
                    for (a0, a1, cc0) in ((0, 256, 0), (256, 320, 0)):
                        w_ = a1 - a0
                        S.dve(lambda e, t1=t1, x1=x1, c_=c_, a0=a0, a1=a1, w_=w_: e.tensor_tensor(t1[:, a0:a1], x1[:, a0:a1], c_[:, 0, 0:w_], ALU.mult), [qnt, ct], [t1t])
                        S.pool(lambda e, t2=t2, x2=x2, c_=c_, a0=a0, a1=a1, w_=w_: e.tensor_tensor(t2[:, a0:a1], x2[:, a0:a1], c_[:, 1, 0:w_], ALU.mult), [qnt, ct], [t2t])
                    S.dve(lambda e, o1=o1, t1=t1, t2=t2: e.tensor_tensor(o1, t1[:], t2[:], ALU.subtract), [t1t, t2t], [rt])
                    t3, t3t = tmp1.next()
                    t4, t4t = tmp2.next()
                    for (a0, a1, cc0) in ((0, 256, 0), (256, 320, 0)):
                        w_ = a1 - a0
                        S.dve(lambda e, t3=t3, x1=x1, c_=c_, a0=a0, a1=a1, w_=w_: e.tensor_tensor(t3[:, a0:a1], x1[:, a0:a1], c_[:, 1, 0:w_], ALU.mult), [qnt, ct], [t3t])
                        S.pool(lambda e, t4=t4, x2=x2, c_=c_, a0=a0, a1=a1, w_=w_: e.tensor_tensor(t4[:, a0:a1], x2[:, a0:a1], c_[:, 0, 0:w_], ALU.mult), [qnt, ct], [t4t])
                    S.dve(lambda e, o2=o2, t3=t3, t4=t4: e.tensor_tensor(o2, t3[:], t4[:], ALU.add), [t3t, t4t, rt], [rt])
                tq, tqt = ptq.next()
                rp, rpt = qrp.next()
                S.pool(lambda e, rp=rp, r_=r_: e.tensor_copy(rp[:].rearrange("p (h g d) -> p h g d", h=4, g=2),
                                                             r_[:, 0:512].rearrange("p (g h d) -> p h g d", g=2, h=4)), [rt], [rpt])
                for pr in range(4):
                    S.pe(lambda e, tq=tq, rp=rp, pr=pr: e.transpose(tq[:, pr, :], rp[:, pr * 128:(pr + 1) * 128], ident[:]), [rpt, "ident"], [tqt])
                S.act(lambda e, tq=tq, i=i: e.copy(qT[:, :, i * 128:(i + 1) * 128], tq[:]), [tqt], [("qT", i)])
                tk_, tkt = ptk.next()
                S.pe(lambda e, tk_=tk_, r_=r_: e.transpose(tk_[:], r_[:, 512:640], ident[:]), [rt, "ident"], [tkt])
                S.dve(lambda e, tk_=tk_, i=i: e.tensor_copy(kT[:, i * 128:(i + 1) * 128], tk_[:]), [tkt], [("kT", i)])
            S.flush()
        S.es = es
        with ExitStack() as es3:
            S.es = es3
            ps_s = Rot(S, "ps_s", 3, [128, 512], F32, psum=True)
            ps_o = Rot(S, "ps_o", 2, [128, 4, 128], F32, psum=True)
            PT = Rot(S, "PT", 3, [128, 512], BF16)
            rc = Rot(S, "rc", 2, [128, 4], F32)
            groups = [(256 + 512 * g, 512, list(range(NT))) for g in range(4)]
            if need_ctx:
                groups.append((0, 256, [0, 1]))
            for (q0, nq, ktiles) in groups:
                nqt = nq // 128
                for hd in range(8):
                    pr, g = hd % 4, hd // 4
                    ps = slice(64 * g, 64 * g + 64)
                    o_, ot = ps_o.next()
                    for ki, kc in enumerate(ktiles):
                        s_, st = ps_s.next()
                        S.pe(lambda e, s_=s_, ps=ps, kc=kc, pr=pr, q0=q0, nq=nq: e.matmul(s_[:, 0:nq], kT[ps, kc * 128:(kc + 1) * 128], qT[ps, pr, q0:q0 + nq], start=True, stop=True),
                             [("kT", kc)] + [("qT", q0 // 128 + t) for t in range(nqt)], [st])
                        p_, ptok = PT.next()
                        S.act(lambda e, p_=p_, s_=s_, nq=nq: e.activation(p_[:, 0:nq], s_[:, 0:nq], AF.Exp, scale=0.125), [st], [ptok])
                        for t in range(nqt):
                            S.pe(lambda e, o_=o_, p_=p_, t=t, kc=kc, g=g, ki=ki, nk=len(ktiles): e.matmul(o_[:, t, 0:65], p_[:, t * 128:(t + 1) * 128], vaug[:, kc, g, :],
                                                                                                        start=(ki == 0 and t == 0), stop=(ki == nk - 1), skip_group_check=True),
                                 [ptok, ("vaug", kc)], [ot])
                    r_, rt = rc.next()
                    S.dve(lambda e, r_=r_, o_=o_, nqt=nqt: e.reciprocal(r_[:, 0:nqt], o_[:, 0:nqt, 64]), [ot], [rt])
                    S.dve(lambda e, r_=r_, o_=o_, nqt=nqt, q0=q0, hd=hd: e.tensor_tensor(btok[:, q0 // 128:q0 // 128 + nqt, hd * 64:(hd + 1) * 64], o_[:, 0:nqt, 0:64],
                                                                                        bc(r_[:, 0:nqt].unsqueeze(2), [128, nqt, 64]), ALU.mult),
                          [ot, rt], [("btok", q0 // 128 + t) for t in range(nqt)])
            S.flush()
        S.es = es
        with ExitStack() as es4:
            S.es = es4
            ptb = Rot(S, "ptb", 2, [128, 4, 128], BF16, psum=True)
            bo = Rot(S, "bo", 2, [128, 4, 128], BF16)
            for i in range(0 if need_ctx else 2, NT):
                p, pt = ptb.next()
                for c in range(4):
                    S.pe(lambda e, p=p, c=c, i=i: e.transpose(p[:, c, :], btok[:, i, c * 128:(c + 1) * 128], ident[:]), [("btok", i), "ident"], [pt])
                o, ot = bo.next()
                S.act(lambda e, o=o, p=p: e.copy(o[:], p[:]), [pt], [ot])
                S.dma("sp", d["bTs"][:, :, i * 128:(i + 1) * 128], o[:], [ot], [])
            S.flush()


def sin_act(S, out, arg_ps, bcol, fcol, n, rt, wt, tmp):
    S.dve(lambda e: e.tensor_scalar(tmp[:, 0:n], arg_ps, bcol, fcol, ALU.add, ALU.mult), rt, ["sin_t"])
    MAGIC = 12582912.0
    kk = S.sin_kk
    S.dve(lambda e: e.tensor_scalar(kk[0:tmp.shape[0], 0:n], tmp[:, 0:n], 1.0 / (2.0 * math.pi), MAGIC, ALU.mult, ALU.add), ["sin_t"], ["sin_k"])
    S.dve(lambda e: e.tensor_scalar(kk[0:tmp.shape[0], 0:n], kk[0:tmp.shape[0], 0:n], -MAGIC, -2.0 * math.pi, ALU.add, ALU.mult), ["sin_k"], ["sin_k"])
    S.dve(lambda e: e.tensor_tensor(tmp[:, 0:n], tmp[:, 0:n], kk[0:tmp.shape[0], 0:n], ALU.add), ["sin_t", "sin_k"], ["sin_t"])
    S.dve(lambda e: e.tensor_scalar(tmp[:, 0:n], tmp[:, 0:n], -3.1415925, 3.1415925, ALU.max, ALU.min), ["sin_t"], ["sin_t"])
    S.act(lambda e: e.activation(out, tmp[:, 0:n], AF.Sin), ["sin_t"], [wt])


def phase_filters(S, nc, d, l, which):
    Ls, tag = (L, "x") if which == "x" else (LC, "c")
    nj = Ls // 128
    with ExitStack() as es:
        S.es = es
        w1 = S.sb("w1", [33, 64], F32)
        wi = S.sb("wi", [64, 2, 64], F32)
        wl = S.sb("wl", [64, 1024], F32)
        b1 = S.sb("b1", [64, 2], F32)
        bi = S.sb("bi", [64, 2, 2], F32)
        fq = S.sb("fq", [64, 2], F32)
        zT = S.sb("zT", [33, Ls], F32)
        h3 = S.sb("h3", [64, Ls], F32)
        Af = S.sb("Af", [128, nj, 512], BF16)
        Bf = S.sb("Bf", [128, nj, 512], BF16)
        S.dma("sp", w1[:], d["hyw1"][l], [], ["w"])
        S.dma("sp", wi[:], d["hywi"][l].rearrange("j i o -> i j o"), [], ["w"])
        S.dma("sp", wl[:], d["hywl"][l], [], ["w"])
        S.dma("sp", b1[:], d["hyb1"], [], ["w"])
        S.dma("sp", bi[:], d["hybi"], [], ["w"])
        S.dma("sp", fq[:], d["hyfreq"], [], ["w"])
        S.dma("sp", zT[:], d["zT" + tag], [], ["zT"])
        pm = Rot(S, "pm", 2, [64, 512], F32, psum=True)
        pf = Rot(S, "pf", 2, [128, 512], F32, psum=True)
        ha = Rot(S, "ha", 2, [64, 512], F32)
        hb = Rot(S, "hb", 2, [64, 512], F32)
        tmp = S.sb("tmpf", [64, 512], F32)
        S.sin_kk = S.sb("sinkk", [64, 512], F32)
        for t0 in range(0, Ls, 512):
            n = min(512, Ls - t0)
            p, pt = pm.next()
            S.pe(lambda e, p=p, t0=t0, n=n: e.matmul(p[:, 0:n], w1[:], zT[:, t0:t0 + n], start=True, stop=True), ["w", "zT"], [pt])
            a, at = ha.next()
            sin_act(S, a[:, 0:n], p[:, 0:n], b1[:, l:l + 1], fq[:, l:l + 1], n, [pt, "w"], at, tmp)
            p, pt = pm.next()
            S.pe(lambda e, p=p, a=a, n=n: e.matmul(p[:, 0:n], wi[:, 0, :], a[:, 0:n], start=True, stop=True), ["w", at], [pt])
            b_, bt = hb.next()
            sin_act(S, b_[:, 0:n], p[:, 0:n], bi[:, l, 0:1], fq[:, l:l + 1], n, [pt, "w"], bt, tmp)
            p, pt = pm.next()
            S.pe(lambda e, p=p, b_=b_, n=n: e.matmul(p[:, 0:n], wi[:, 1, :], b_[:, 0:n], start=True, stop=True), ["w", bt], [pt])
            sin_act(S, h3[:, t0:t0 + n], p[:, 0:n], bi[:, l, 1:2], fq[:, l:l + 1], n, [pt, "w"], "h3", tmp)
        dec = Rot(S, "dec", 2, [128, 512], F32)
        hf = Rot(S, "hf", 2, [128, 512], F32)
        hbk = Rot(S, "hbk", 2, [128, 512], F32)
        for j in range(nj):
            dc, dct = dec.next()
            S.dma("sp", dc[:], d["decay" + tag][j * 128:(j + 1) * 128, :], [], [dct])
            p1, p1t = pf.next()
            S.pe(lambda e, p1=p1, j=j: e.matmul(p1[:], h3[:, j * 128:(j + 1) * 128], wl[:, 0:512], start=True, stop=True), ["h3", "w"], [p1t])
            p2, p2t = pf.next()
            S.pe(lambda e, p2=p2, j=j: e.matmul(p2[:], h3[:, j * 128:(j + 1) * 128], wl[:, 512:1024], start=True, stop=True), ["h3", "w"], [p2t])
            f_, ft = hf.next()
            S.dve(lambda e, f_=f_, p1=p1, dc=dc: e.tensor_tensor(f_[:], p1[:], dc[:], ALU.mult), [p1t, dct], [ft])
            g_, gt = hbk.next()
            S.dve(lambda e, g_=g_, p2=p2, dc=dc: e.tensor_tensor(g_[:], p2[:], dc[:], ALU.mult), [p2t, dct], [gt])
            S.pool(lambda e, f_=f_, g_=g_, j=j: e.tensor_tensor(Af[:, j, :], f_[:], g_[:], ALU.add), [ft, gt], ["Af"])
            S.pool(lambda e, f_=f_, g_=g_, j=j: e.tensor_tensor(Bf[:, j, :], f_[:], g_[:], ALU.subtract), [ft, gt], ["Bf"])
        fw = Rot(S, "fw", 2, [128, nj, 128], BF16)
        ko = Rot(S, "ko", 2, [128, 512], F32)
        nf = Ls // 128
        for part, src in ((0, Af), (1, Bf)):
            for fc in range(nf):
                fcg = part * nf + fc
                w_, wt = fw.next()
                S.dma("sp", w_[:], d["fwd" + tag][:, :, fcg * 128:(fcg + 1) * 128], [], [wt])
                p, pt = pf.next()
                for j in range(nj):
                    S.pe(lambda e, p=p, w_=w_, j=j, src=src: e.matmul(p[:], w_[:, j, :], src[:, j, :], start=(j == 0), stop=(j == nj - 1)),
                         [wt, "Af", "Bf"], [pt])
                o, ot = ko.next()
                S.act(lambda e, o=o, p=p: e.copy(o[:], p[:]), [pt], [ot])
                dst = d["kspx"][l, :, fcg, :] if which == "x" else d["kspc"][:, fcg, :]
                S.dma("sp", dst, o[:], [ot], [])
        S.flush()


def phase_hyena(S, nc, d, l, hT, cst, need_ctx):
    ident = cst["ident"]
    with ExitStack() as es:
        S.es = es
        x0T = S.sb("x0T", [128, 4, T], BF16)
        uT = S.sb("uT", [128, 4, T], BF16)
        cw = S.sb("cw", [128, 12, 3], F32)
        cb = S.sb("cb", [128, 12], F32)
        hbias = S.sb("hbias", [128, 4], F32)
        S.dma("sp", cw[:], d["hycw"][:, l], [], ["cw"])
        S.dma("sp", cb[:], d["hycb"][:, l], [], ["cw"])
        S.dma("sp", hbias[:], d["hybias"][:, l], [], ["cw"])
        utok = S.sb("utok", [128, NT, 512], BF16)
        with ExitStack() as es2:
            S.es = es2
            stg = Rot(S, "stg", 2, [128, 2048], F32)
            wh = Rot(S, "wh", 2, [128, 8, 128], BF16)
            pp = Rot(S, "pp", 3, [128, 512], F32, psum=True)
            pz = Rot(S, "pz", 2, [128, T], F32)
            zc = Rot(S, "zc", 2, [128, T], F32)
            x1c = S.sb("x1c", [128, 4, T], BF16)
            segs = [(0, LC), (LC, T)]
            for ch in range(12):
                w_, wt = wh.next()
                load_cast(S, stg, w_[:], d["w_in"][l, :, :, C_HY + ch * 128:C_HY + (ch + 1) * 128], 8, 128, [wt], "pool")
                z_, zt = pz.next()
                for (t0, n) in TG:
                    p, pt = proj_fm(S, pp, w_, wt, hT, 0, t0, n)
                    S.act(lambda e, z_=z_, p=p, t0=t0, n=n: e.copy(z_[:, t0:t0 + n], p[:, 0:n]), [pt], [zt])
                c_, ct = zc.next()
                S.dve(lambda e, c_=c_, z_=z_, ch=ch: e.tensor_scalar(c_[:], z_[:], cw[:, ch, 1:2], cb[:, ch:ch + 1], ALU.mult, ALU.add), [zt, "cw"], [ct])
                for (a0, a1) in segs:
                    S.dve(lambda e, c_=c_, z_=z_, ch=ch, a0=a0, a1=a1: e.scalar_tensor_tensor(c_[:, a0 + 1:a1], z_[:, a0:a1 - 1], cw[:, ch, 0:1], c_[:, a0 + 1:a1], ALU.mult, ALU.add),
                          [zt, ct, "cw"], [ct])
                    S.dve(lambda e, c_=c_, z_=z_, ch=ch, a0=a0, a1=a1: e.scalar_tensor_tensor(c_[:, a0:a1 - 1], z_[:, a0 + 1:a1], cw[:, ch, 2:3], c_[:, a0:a1 - 1], ALU.mult, ALU.add),
                          [zt, ct, "cw"], [ct])
                sec, cc = ch // 4, ch % 4
                if sec == 0:
                    S.pool(lambda e, c_=c_, cc=cc: e.tensor_copy(x0T[:, cc, :], c_[:]), [ct], [("x0T", cc)])
                elif sec == 1:
                    S.pool(lambda e, c_=c_, cc=cc: e.tensor_copy(x1c[:, cc, :], c_[:]), [ct], [("x1c", cc)])
                else:
                    S.pool(lambda e, c_=c_, cc=cc: e.tensor_tensor(uT[:, cc, :], c_[:], x1c[:, cc, :], ALU.mult), [ct, ("x1c", cc)], [("uT", cc)])
            ptu = Rot(S, "ptu", 2, [128, 4, 128], BF16, psum=True)
            for i in range(NT):
                p, pt = ptu.next()
                for cc in range(4):
                    S.pe(lambda e, p=p, cc=cc, i=i: e.transpose(p[:, cc, :], uT[:, cc, i * 128:(i + 1) * 128], ident[:]), [("uT", cc), "ident"], [pt])
                S.act(lambda e, p=p, i=i: e.copy(utok[:, i, :], p[:].rearrange("p c t -> p (c t)")), [pt], [("utok", i)])
            S.flush()
        for (which, tb, Ls) in ((("c", 0, LC),) if need_ctx else ()) + (("x", LC, L),):
            nj = Ls // 128
            nf = Ls // 128
            i0 = tb // 128
            with ExitStack() as es3:
                S.es = es3
                YT = S.sb("YT", [128, 2 * nf, 512], BF16)
                fw = Rot(S, "fw", 3, [128, nj, 128], BF16)
                kr = Rot(S, "kr", 2, [128, 2, 512], F32)
                pu = Rot(S, "pu", 4, [128, 512], F32, psum=True)
                ur = Rot(S, "ur", 2, [128, 2, 512], F32)
                t1 = Rot(S, "t1", 2, [128, 512], F32)
                t2 = Rot(S, "t2", 2, [128, 512], F32)
                for fc in range(nf):
                    pr_ = []
                    for part in range(2):
                        fcg = part * nf + fc
                        w_, wt = fw.next()
                        S.dma("sp", w_[:], d["fwd" + which][:, :, fcg * 128:(fcg + 1) * 128], [], [wt])
                        p, pt = pu.next()
                        for j in range(nj):
                            S.pe(lambda e, p=p, w_=w_, j=j: e.matmul(p[:], w_[:, j, :], utok[:, i0 + j, :], start=(j == 0), stop=(j == nj - 1)),
                                 [wt, ("utok", i0 + j)], [pt])
                        pr_.append((p, pt))
                    k_, kt = kr.next()
                    for part in range(2):
                        src = d["kspx"][l, :, part * nf + fc, :] if which == "x" else d["kspc"][:, part * nf + fc, :]
                        S.dma("sp", k_[:, part, :], src, [], [kt])
                    u_, ut = ur.next()
                    S.act(lambda e, u_=u_, p=pr_[0][0]: e.copy(u_[:, 0, :], p[:]), [pr_[0][1]], [ut])
                    S.act(lambda e, u_=u_, p=pr_[1][0]: e.copy(u_[:, 1, :], p[:]), [pr_[1][1]], [ut])
                    a, at = t1.next()
                    b_, bt = t2.next()
                    S.dve(lambda e, a=a, u_=u_, k_=k_: e.tensor_tensor(a[:], u_[:, 0, :], k_[:, 0, :], ALU.mult), [ut, kt], [at])
                    S.pool(lambda e, b_=b_, u_=u_, k_=k_: e.tensor_tensor(b_[:], u_[:, 1, :], k_[:, 1, :], ALU.mult), [ut, kt], [bt])
                    S.dve(lambda e, a=a, b_=b_, fc=fc: e.tensor_tensor(YT[:, fc, :], a[:], b_[:], ALU.subtract), [at, bt], [("YT", fc)])
                    a2, a2t = t1.next()
                    b2, b2t = t2.next()
                    S.dve(lambda e, a2=a2, u_=u_, k_=k_: e.tensor_tensor(a2[:], u_[:, 0, :], k_[:, 1, :], ALU.mult), [ut, kt], [a2t])
                    S.pool(lambda e, b2=b2, u_=u_, k_=k_: e.tensor_tensor(b2[:], u_[:, 1, :], k_[:, 0, :], ALU.mult), [ut, kt], [b2t])
                    S.pool(lambda e, a2=a2, b2=b2, fc=fc, nf=nf: e.tensor_tensor(YT[:, nf + fc, :], a2[:], b2[:], ALU.add), [a2t, b2t], [("YT", nf + fc)])
                iv = Rot(S, "iv", 2, [128, 2 * nf, 256], BF16)
                py = Rot(S, "py", 3, [128, 256], F32, psum=True)
                y1 = Rot(S, "y1", 2, [128, 256], F32)
                co = Rot(S, "co", 2, [128, 256], BF16)
                for tg in range(Ls // 256):
                    v_, vt = iv.next()
                    S.dma("sp", v_[:], d["inv" + which][:, :, tg * 256:(tg + 1) * 256], [], [vt])
                    ta = tb + tg * 256
                    for cc in range(4):
                        p, pt = py.next()
                        for f in range(2 * nf):
                            S.pe(lambda e, p=p, v_=v_, f=f, cc=cc: e.matmul(p[:], YT[:, f, cc * 128:(cc + 1) * 128], v_[:, f, :], start=(f == 0), stop=(f == 2 * nf - 1)),
                                 [("YT", f), vt], [pt])
                        y_, yt = y1.next()
                        S.dve(lambda e, y_=y_, p=p, cc=cc, ta=ta: e.scalar_tensor_tensor(y_[:], uT[:, cc, ta:ta + 256], hbias[:, cc:cc + 1], p[:], ALU.mult, ALU.add),
                              [pt, ("uT", cc), "cw"], [yt])
                        o, ot = co.next()
                        S.pool(lambda e, o=o, y_=y_, cc=cc, ta=ta: e.tensor_tensor(o[:], y_[:], x0T[:, cc, ta:ta + 256], ALU.mult), [yt, ("x0T", cc)], [ot])
                        S.dma("sp", d["cTs"][:, cc, ta:ta + 256], o[:], [ot], [])
                S.flush()
            S.es = es


def post_residual(S, d, l, b, sub, ns, i, ypair, ytoks, src, dst, bufs):
    xt, junk, ss, rs, tmpb, al = bufs
    who = 1 if i < 2 else 0
    x, xtok = xt.next()
    S.dma("sp", x[:], src(i), [], [xtok])
    s1, s1t = ss.next()
    for hlf in range(2):
        S.act(lambda e, hlf=hlf, s1=s1: e.activation(junk[:], ypair[hlf][:], AF.Square, accum_out=s1[:, hlf:hlf + 1]), [ytoks[hlf]], ["junk", s1t])
    r1, r1t = rs.next()
    S.dve(lambda e, r1=r1, s1=s1: e.tensor_tensor(r1[:], s1[:, 0:1], s1[:, 1:2], ALU.add), [s1t], [r1t])
    rstd_ops(S, r1[:], r1[:], 1.0 / D, [r1t], [r1t])
    t_, tt = tmpb.next()
    for hlf in range(2):
        S.dve(lambda e, hlf=hlf, t_=t_, r1=r1, who=who: e.scalar_tensor_tensor(t_[:, hlf * 512:(hlf + 1) * 512], ypair[hlf][:], r1[:], al[:, who, hlf * 512:(hlf + 1) * 512], ALU.mult, ALU.mult),
              [ytoks[hlf], r1t, "al"], [tt])
    S.pool(lambda e, t_=t_, x=x: e.tensor_tensor(t_[:], t_[:], x[:], ALU.add), [tt, xtok], [tt])
    S.dma("sp", dst(i), t_[:], [tt], [])


def phase_merge(S, nc, d, l, b, ns, hT, tiles, src, dst):
    with ExitStack() as es:
        S.es = es
        stg = Rot(S, "stg", 4, [128, 2048], F32)
        wo = S.sb("wo", [128, 3, 4, 1024], BF16)
        wout = S.sb("wout", [128, 8, 1024], BF16)
        for bi_, nm in enumerate(("w_oa", "w_ob", "w_oc")):
            load_cast(S, stg, wo[:, bi_, :, :], d[nm][l], 4, 1024, ["wo"], "pool")
        load_cast(S, stg, wout[:], d["w_out"][l], 8, 1024, ["wout"], "pool")
        al = S.sb("al", [128, 2, D], F32)
        S.dma("sp", al[:, 0, :], d["modrep"][l, b, :, 2 * D:3 * D], [], ["al"])
        S.dma("sp", al[:, 1, :], d["modrep"][l, ns, :, 2 * D:3 * D], [], ["al"])
        wg = Rot(S, "wg", 4, [128, 8, 128], BF16)
        br = Rot(S, "br", 2, [128, 3, 4, 512], BF16)
        mT = Rot(S, "mT", 2, [128, 8, 512], BF16)
        pg = Rot(S, "pg", 2, [128, 512], F32, psum=True)
        py = Rot(S, "py", 2, [128, 512], F32, psum=True)
        po = Rot(S, "po", 4, [128, 512], F32, psum=True)
        sg = Rot(S, "sg", 2, [128, 512], F32)
        acc = Rot(S, "acc", 2, [128, 512], F32)
        tm = Rot(S, "tm", 2, [128, 512], F32)
        bufs = (Rot(S, "xt", 2, [128, D], F32), S.sb("junk", [128, 512], F32), Rot(S, "ss", 2, [128, 2], F32),
                Rot(S, "rs", 2, [128, 1], F32), Rot(S, "tmpb", 2, [128, D], F32), al)
        t_lo = min(tiles) * 128
        groups = [(t0, n) for (t0, n) in TG if t0 + n > t_lo]
        if t_lo > 0:
            groups = [(256, 256)] + [(512, 512), (1024, 512), (1536, 512), (2048, 256)]
        for (t0, n) in groups:
            b_, bt = br.next()
            for bi_, nm in enumerate(("aTs", "bTs", "cTs")):
                S.dma("sp", b_[:, bi_, :, 0:n], d[nm][:, :, t0:t0 + n], [], [bt])
            m_, mt = mT.next()
            for oc in range(8):
                a_, at = acc.next()
                for bi_ in range(3):
                    w_, wt = wg.next()
                    c0 = C_GT + bi_ * D + oc * 128
                    load_cast(S, stg, w_[:], d["w_in"][l, :, :, c0:c0 + 128], 8, 128, [wt], "pool")
                    g_, gt = proj_fm(S, pg, w_, wt, hT, 0, t0, n)
                    s_, st = sg.next()
                    S.act(lambda e, s_=s_, g_=g_, n=n: e.activation(s_[:, 0:n], g_[:, 0:n], AF.Sigmoid), [gt], [st])
                    y_, yt = py.next()
                    for k in range(4):
                        S.pe(lambda e, y_=y_, k=k, bi_=bi_, oc=oc, b_=b_, n=n: e.matmul(y_[:, 0:n], wo[:, bi_, k, oc * 128:(oc + 1) * 128], b_[:, bi_, k, 0:n], start=(k == 0), stop=(k == 3)),
                             ["wo", bt], [yt])
                    if bi_ == 0:
                        S.dve(lambda e, a_=a_, s_=s_, y_=y_, n=n: e.tensor_tensor(a_[:, 0:n], s_[:, 0:n], y_[:, 0:n], ALU.mult), [st, yt], [at])
                    else:
                        t_, tt = tm.next()
                        S.dve(lambda e, t_=t_, s_=s_, y_=y_, n=n: e.tensor_tensor(t_[:, 0:n], s_[:, 0:n], y_[:, 0:n], ALU.mult), [st, yt], [tt])
                        S.pool(lambda e, a_=a_, t_=t_, n=n: e.tensor_tensor(a_[:, 0:n], a_[:, 0:n], t_[:, 0:n], ALU.add), [at, tt], [at])
                S.act(lambda e, m_=m_, a_=a_, oc=oc, n=n: e.copy(m_[:, oc, 0:n], a_[:, 0:n]), [at], [mt])
            for ti in range(n // 128):
                i = t0 // 128 + ti
                ys, yts = [], []
                for hlf in range(2):
                    o_, ot = po.next()
                    for k in range(8):
                        S.pe(lambda e, o_=o_, k=k, ti=ti, hlf=hlf, m_=m_: e.matmul(o_[:], m_[:, k, ti * 128:(ti + 1) * 128], wout[:, k, hlf * 512:(hlf + 1) * 512], start=(k == 0), stop=(k == 7)),
                             [mt, "wout"], [ot])
                    ys.append(o_)
                    yts.append(ot)
                post_residual(S, d, l, b, 0, ns, i, ys, yts, src, dst, bufs)
        S.flush()


def phase_ffn(S, nc, d, l, b, ns, norm_fn, tiles, src, dst):
    t_lo = min(tiles) * 128
    segs = [(0, LC), (LC, T)] if t_lo == 0 else [(LC, T)]
    groups = [(t0, n) for (t0, n) in TG] if t_lo == 0 else [(256, 256), (512, 512), (1024, 512), (1536, 512), (2048, 256)]
    with ExitStack() as es:
        S.es = es
        gT = S.sb("gT", [128, 22, T], BF16)
        with ExitStack() as es2:
            S.es = es2
            hT = S.sb("hT2", [128, 8, T], BF16)
            norm_fn(hT)
            S.es = es2
            stg = Rot(S, "stg", 2, [128, 2048], F32)
            wu = Rot(S, "wu", 4, [128, 8, 128], BF16)
            pp = Rot(S, "pp", 3, [128, 512], F32, psum=True)
            ua = Rot(S, "ua", 2, [128, T], F32)
            uc = Rot(S, "uc", 2, [128, T], F32)
            fcw = S.sb("fcw", [128, 44, 3], F32)
            fcb = S.sb("fcb", [128, 44], F32)
            S.dma("sp", fcw[:], d["ffcw"][:, l], [], ["fcw"])
            S.dma("sp", fcb[:], d["ffcb"][:, l], [], ["fcw"])
            for j in range(22):
                res = []
                for half in range(2):
                    ch = half * 22 + j
                    w_, wt = wu.next()
                    load_cast(S, stg, w_[:], d["w_up"][l, :, :, ch * 128:(ch + 1) * 128], 8, 128, [wt], "pool")
                    z_, zt = ua.next()
                    for (t0, n) in groups:
                        p, pt = proj_fm(S, pp, w_, wt, hT, 0, t0, n)
                        S.act(lambda e, z_=z_, p=p, t0=t0, n=n: e.copy(z_[:, t0:t0 + n], p[:, 0:n]), [pt], [zt])
                    c_, ct = uc.next()
                    S.dve(lambda e, c_=c_, z_=z_, ch=ch: e.tensor_scalar(c_[:, t_lo:T], z_[:, t_lo:T], fcw[:, ch, 1:2], fcb[:, ch:ch + 1], ALU.mult, ALU.add), [zt, "fcw"], [ct])
                    for (a0, a1) in segs:
                        S.dve(lambda e, c_=c_, z_=z_, ch=ch, a0=a0, a1=a1: e.scalar_tensor_tensor(c_[:, a0 + 1:a1], z_[:, a0:a1 - 1], fcw[:, ch, 0:1], c_[:, a0 + 1:a1], ALU.mult, ALU.add),
                              [zt, ct, "fcw"], [ct])
                        S.dve(lambda e, c_=c_, z_=z_, ch=ch, a0=a0, a1=a1: e.scalar_tensor_tensor(c_[:, a0:a1 - 1], z_[:, a0 + 1:a1], fcw[:, ch, 2:3], c_[:, a0:a1 - 1], ALU.mult, ALU.add),
                              [zt, ct, "fcw"], [ct])
                    res.append((c_, ct))
                (ca, cat), (cb_, cbt) = res
                S.act(lambda e, ca=ca: e.activation(ca[:, t_lo:T], ca[:, t_lo:T], AF.Silu), [cat], [cat])
                S.pool(lambda e, ca=ca, cb_=cb_, j=j: e.tensor_tensor(gT[:, j, t_lo:T], ca[:, t_lo:T], cb_[:, t_lo:T], ALU.mult), [cat, cbt], [("gT", j)])
            S.flush()
        S.es = es
        with ExitStack() as es3:
            S.es = es3
            stg = Rot(S, "stg", 2, [128, 2048], F32)
            wd = S.sb("wd", [128, 22, 1024], BF16)
            load_cast(S, stg, wd[:], d["w_down"][l], 22, 1024, ["wd"], "pool")
            al = S.sb("al", [128, 2, D], F32)
            S.dma("sp", al[:, 0, :], d["modrep"][l, b, :, 5 * D:6 * D], [], ["al"])
            S.dma("sp", al[:, 1, :], d["modrep"][l, ns, :, 5 * D:6 * D], [], ["al"])
            po = Rot(S, "po", 4, [128, 512], F32, psum=True)
            bufs = (Rot(S, "xt", 2, [128, D], F32), S.sb("junk", [128, 512], F32), Rot(S, "ss", 2, [128, 2], F32),
                    Rot(S, "rs", 2, [128, 1], F32), Rot(S, "tmpb", 2, [128, D], F32), al)
            for i in tiles:
                ys, yts = [], []
                for hlf in range(2):
                    o_, ot = po.next()
                    for k in range(22):
                        S.pe(lambda e, o_=o_, k=k, i=i, hlf=hlf: e.matmul(o_[:], gT[:, k, i * 128:(i + 1) * 128], wd[:, k, hlf * 512:(hlf + 1) * 512], start=(k == 0), stop=(k == 21)),
                             ["wd"], [ot])
                    ys.append(o_)
                    yts.append(ot)
                post_residual(S, d, l, b, 1, ns, i, ys, yts, src, dst, bufs)
            S.flush()


def host_layout_shared(inp):
    m = host_layout(inp, 1, 0)
    m.pop("xin")
    m.pop("cT")
    return m


def host_layout_core(inp, ns, core):
    b0 = core * ns
    m = {}
    m["xin"] = np.ascontiguousarray(np.concatenate([inp["ctx"][b0:b0 + ns], inp["x"][b0:b0 + ns]], axis=1))
    cc = np.concatenate([inp["c"][b0:b0 + ns], inp["c_ctx"][None, :]], axis=0)
    m["cT"] = np.ascontiguousarray(cc.reshape(ns + 1, 8, 128).transpose(2, 1, 0))
    return m


def build_program(ns, consts, layout, stop_after=None):
    nc = bass.Bass("TRN2", target_bir_lowering=False)
    d = declare(nc, ns, consts, layout)
    with ExitStack() as es0:
        S = Sched(nc, es0)
        S.es = es0
        cst = {}
        cst["ident"] = S.sb("ident", [128, 128], BF16)
        cst["ones"] = S.sb("ones", [128, 128], BF16)
        cst["hgmask"] = S.sb("hgmask", [128, 2, 512], BF16)
        cst["rowmask"] = S.sb("rowmask", [128, 4], F32)
        cst["resetmask"] = S.sb("resetmask", [128, 512], F32)
        S.epsT = S.sb("epsT", [128, 1], F32)
        S.mpi = S.sb("mpi", [128, 1], F32)
        S.dma("sp", cst["ident"][:], d["ident"], [], [])
        S.dma("sp", cst["ones"][:], d["ones"], [], [])
        S.dma("sp", cst["hgmask"][:], d["hgmask"].rearrange("d p n -> p d n"), [], [])
        S.dma("sp", cst["rowmask"][:], d["rowmask"], [], [])
        S.dma("sp", cst["resetmask"][:], d["resetmask"], [], [])
        S.dve(lambda e: e.memset(S.epsT[:], EPS), [], [])
        S.dve(lambda e: e.memset(S.mpi[:], -math.pi), [], [])
        S.flush()
        phase_mod(S, nc, d, ns)
        for l in range(2):
            phase_filters(S, nc, d, l, "x")
        phase_filters(S, nc, d, 0, "c")
        import os
        STOP = int(os.environ.get("KSTOP", "99"))
        for l in range(2):
            need_ctx = (l == 0)
            if STOP < 2:
                break
            for b in range(ns):
                src0 = (lambda i, b=b: d["xin"][b, i * 128:(i + 1) * 128, :]) if l == 0 else (lambda i, b=b: d["xs"][b, i * 128:(i + 1) * 128, :])
                xs_t = lambda i, b=b: d["xs"][b, i * 128:(i + 1) * 128, :]
                tiles = list(range(NT)) if need_ctx else list(range(2, NT))
                with ExitStack() as esh:
                    S.es = esh
                    hT = S.sb("hT", [128, 8, T], BF16)
                    phase_norm(S, nc, d, l, b, 0, src0, list(range(NT)), hT, cst["ident"], ns)
                    if STOP >= 3:
                        phase_hgrn(S, nc, d, l, hT, cst)
                    if STOP >= 4:
                        phase_attn(S, nc, d, l, hT, cst, need_ctx)
                    if STOP >= 5:
                        phase_hyena(S, nc, d, l, hT, cst, need_ctx)
                    if STOP >= 6:
                        phase_merge(S, nc, d, l, b, ns, hT, tiles, src0, xs_t)
                if STOP < 7:
                    continue
                if l == 0:
                    dst = xs_t
                else:
                    dst = lambda i, b=b: d["out"][b, (i - 2) * 128:(i - 1) * 128, :]
                norm_fn = lambda hT2, l=l, b=b, tiles=tiles, xs_t=xs_t: phase_norm(S, nc, d, l, b, 1, xs_t, tiles, hT2, cst["ident"], ns)
                phase_ffn(S, nc, d, l, b, ns, norm_fn, tiles, xs_t, dst)
        S.es = es0
        S.flush()
    return nc


NS = 4


def kernel(**inputs):
    inp = {k: np.asarray(v, dtype=np.float32) for k, v in inputs.items()}
    consts = host_constants()
    shared = host_layout_shared(inp)
    cores = [host_layout_core(inp, NS, c) for c in range(NCORES)]
    layout = dict(shared)
    layout.update(cores[0])
    nc = build_program(NS, consts, layout)
    in_maps = []
    for c in range(NCORES):
        m = dict(consts)
        m.update(shared)
        m.update(cores[c])
        in_maps.append(m)
    res = run_bass_kernel_spmd(nc, in_maps, core_ids=list(range(NCORES)))
    out = np.concatenate([np.asarray(r["out"], dtype=np.float32) for r in res.results], axis=0)
    return out
```

```python
import numpy as np
import concourse.bass as bass
import concourse.mybir as mybir
from concourse.bass_utils import run_bass_kernel_spmd
from contextlib import ExitStack

F32 = mybir.dt.float32
BF16 = mybir.dt.bfloat16
AF = mybir.ActivationFunctionType
ALU = mybir.AluOpType
AX = mybir.AxisListType


class _Op:
    __slots__ = ("eng", "fn", "deps", "signal", "idx", "is_dma", "dsem", "dtarget", "seq")

    def __init__(self, eng, fn, is_dma=False):
        self.eng = eng
        self.fn = fn
        self.deps = []
        self.signal = False
        self.idx = 0
        self.is_dma = is_dma
        self.dsem = None
        self.dtarget = 0


class Sched:
    ENG = ("pe", "act", "dve", "pool", "sp")

    def __init__(self, nc, es, ndma=10, dma_queues=("sp", "pool", "act")):
        self.nc = nc
        self.es = es
        self.ops = []
        self.last_w = {}
        self.readers = {}
        self.sem = {e: es.enter_context(nc.semaphore("s_" + e)) for e in self.ENG}
        self.dsem = {q: [es.enter_context(nc.semaphore("d_%s%d" % (q, i))) for i in range(ndma)]
                     for q in dma_queues}
        self.dcount = {q: 0 for q in dma_queues}
        self.ndma = ndma
        self.last_dma = {}
        self.last_eng = {}
        self.n = 0

    def sb(self, name, shape, dt):
        self.uid = getattr(self, "uid", 0) + 1
        return self.es.enter_context(self.nc.sbuf_tensor("sb%d_%s" % (self.uid, name), list(shape), dt))

    def ps(self, name, shape, dt):
        self.uid = getattr(self, "uid", 0) + 1
        return self.es.enter_context(self.nc.psum_tensor("ps%d_%s" % (self.uid, name), list(shape), dt))

    def _record(self, op, reads, writes):
        deps = {}
        for t in reads:
            w = self.last_w.get(t)
            if w is not None:
                deps[id(w)] = (w, "raw")
        for t in writes:
            w = self.last_w.get(t)
            if w is not None and id(w) not in deps:
                deps[id(w)] = (w, "waw")
            for r in self.readers.get(t, ()):
                if id(r) not in deps:
                    deps[id(r)] = (r, "war")
        for p, kind in deps.values():
            if p is op:
                continue
            if not p.is_dma and p.eng == op.eng and not op.is_dma:
                if kind != "raw" or op.eng == "pe":
                    continue
            op.deps.append(p)
            if not p.is_dma:
                p.signal = True
        for t in reads:
            lst = self.readers.setdefault(t, [])
            if not op.is_dma:
                lst[:] = [r for r in lst if r.is_dma or r.eng != op.eng]
            lst.append(op)
        for t in writes:
            self.last_w[t] = op
            self.readers[t] = []
        self.ops.append(op)
        if not op.is_dma:
            self.last_eng[op.eng] = op
        self.n += 1
        return op

    def op(self, eng, fn, reads=(), writes=()):
        return self._record(_Op(eng, fn), reads, writes)

    def dma(self, q, out, in_, reads=(), writes=(), **kw):
        o = _Op(q, lambda e: e.dma_start(out=out, in_=in_, **kw), is_dma=True)
        i = self.dcount[q]
        self.dcount[q] += 1
        slot = i % self.ndma
        o.dsem = (q, slot)
        o.dtarget = 16 * (i // self.ndma + 1)
        prev = self.last_dma.get(o.dsem)
        if prev is not None:
            o.deps.append(prev)
        self.last_dma[o.dsem] = o
        return self._record(o, reads, writes)

    def barrier(self):
        lasts = list(self.last_eng.values()) + list(self.last_dma.values())
        for p in lasts:
            if not p.is_dma:
                p.signal = True
        for e in self.ENG:
            o = _Op(e, None)
            o.deps = [p for p in lasts if p.is_dma or p.eng != e]
            self.ops.append(o)
        self.last_w = {}
        self.readers = {}

    def finish(self):
        self.flush()

    def eps_ap(self, like):
        return self.epsT[0:like.shape[0], 0:1]

    def pe(self, fn, r=(), w=()):
        return self.op("pe", fn, r, w)

    def act(self, fn, r=(), w=()):
        return self.op("act", fn, r, w)

    def dve(self, fn, r=(), w=()):
        return self.op("dve", fn, r, w)

    def pool(self, fn, r=(), w=()):
        return self.op("pool", fn, r, w)

    def flush(self):
        self.barrier()
        if not hasattr(self, "_cnt"):
            self._cnt = {e: 0 for e in self.ENG}
            self._known = {e: {} for e in self.ENG}
        per = {e: [] for e in self.ENG}
        for o in self.ops:
            if o.signal:
                self._cnt[o.eng] += 1
                o.idx = self._cnt[o.eng]
            per[o.eng].append(o)
        self.ops = []
        nc = self.nc
        with nc.Block() as block:
            def emit(engname):
                def body(e):
                    known = self._known[engname]
                    for o in per[engname]:
                        need = {}
                        for p in o.deps:
                            if p.is_dma:
                                key, val = p.dsem, p.dtarget
                            else:
                                key, val = p.eng, p.idx
                            if known.get(key, 0) >= val:
                                continue
                            if need.get(key, 0) < val:
                                need[key] = val
                        for key, val in need.items():
                            s = self.dsem[key[0]][key[1]] if isinstance(key, tuple) else self.sem[key]
                            e.wait_ge(s, val)
                            known[key] = val
                        if o.fn is None:
                            continue
                        ins = o.fn(e)
                        if o.is_dma:
                            ins.then_inc(self.dsem[o.dsem[0]][o.dsem[1]], 16)
                        elif o.signal:
                            ins.then_inc(self.sem[engname], 1)
                return body
            block.tensor(emit("pe"))
            block.scalar(emit("act"))
            block.vector(emit("dve"))
            block.gpsimd(emit("pool"))
            block.sync(emit("sp"))


D = 1024
L = 2048
LC = 256
T = L + LC
NT = T // 128
DIN = 7936
DFF = 2816
NCORES = 8
EPS = 1e-6
import math
import ml_dtypes
NPBF = ml_dtypes.bfloat16

C_HQ, C_ZF, C_ZB, C_HV, C_ZG, C_AQ, C_AK, C_AV, C_HY, C_GT = 0, 512, 1024, 1536, 2048, 2560, 3072, 3200, 3328, 4864

TG = [(0, 512), (512, 512), (1024, 512), (1536, 512), (2048, 256)]


def kmajor(w):
    k, n = w.shape
    return np.ascontiguousarray(w.reshape(k // 128, 128, n).transpose(1, 0, 2))


def rep128(v):
    return np.ascontiguousarray(np.broadcast_to(np.asarray(v)[None, :], (128, v.shape[-1])))


def host_constants():
    c = {}
    c["ident"] = np.eye(128, dtype=np.float32).astype(NPBF)
    c["ones"] = np.ones((128, 128), np.float32).astype(NPBF)
    s = np.arange(128)[:, None]
    t = np.arange(128)[None, :]
    same = (s // 32) == (t // 32)
    mf = (same & (s <= t)).astype(np.float32)
    mb = (same & (s >= t)).astype(np.float32)
    c["hgmask"] = np.stack([np.tile(mf, (1, 4)), np.tile(mb, (1, 4))]).astype(NPBF)
    rm = np.zeros((128, 4), np.float32)
    for j in range(4):
        rm[32 * j:32 * j + 32, j] = 1.0
    c["rowmask"] = rm
    rs = np.ones((128, 512), np.float32)
    rs[:, ::32] = 0.0
    c["resetmask"] = rs
    rows = L // 64
    row = np.repeat(np.arange(rows), 64).astype(np.float32)
    col = np.tile(np.arange(64), rows).astype(np.float32)
    inv = (10000.0 ** (-np.arange(16, dtype=np.float32) / 16)).astype(np.float32)
    ang = np.concatenate([row[:, None] * inv, col[:, None] * inv], axis=-1).astype(np.float32)
    c["cos8"] = np.tile(np.cos(ang).astype(np.float32), (1, 8))
    c["sin8"] = np.tile(np.sin(ang).astype(np.float32), (1, 8))
    for Ls, tag in ((L, "x"), (LC, "c")):
        tt = np.linspace(0.0, 1.0, Ls, dtype=np.float32)[:, None]
        w = (2.0 * math.pi * np.arange(Ls, dtype=np.float32)[:, None] / Ls).astype(np.float32)
        f = np.linspace(1e-4, 15, 16, dtype=np.float32)[None, :]
        z = np.concatenate([tt, np.cos(f * w), -np.sin(f * w)], axis=-1).astype(np.float32)
        c["zT" + tag] = np.ascontiguousarray(z.T)
        deltas = np.abs(np.linspace(math.log(1e-2) / 1.5, math.log(1e-2) / 0.3, 512, dtype=np.float32))
        c["decay" + tag] = np.exp(-tt * deltas[None, :]).astype(np.float32)
        N = 2 * Ls
        om = 2.0 * np.pi * (np.arange(Ls, dtype=np.float64) + 0.5) / N
        tj = np.arange(Ls, dtype=np.float64)
        ph = tj[:, None] * om[None, :]
        fwd = np.concatenate([np.cos(ph), -np.sin(ph)], axis=1)
        inv_m = np.concatenate([np.cos(ph).T, -np.sin(ph).T], axis=0) * (2.0 / N)
        c["fwd" + tag] = kmajor(fwd.astype(np.float32)).astype(NPBF)
        c["inv" + tag] = kmajor(inv_m.astype(np.float32)).astype(NPBF)
    return c


def host_layout(inp, ns, core):
    b0 = core * ns
    m = {}
    m["xin"] = np.ascontiguousarray(np.concatenate([inp["ctx"][b0:b0 + ns], inp["x"][b0:b0 + ns]], axis=1))
    cc = np.concatenate([inp["c"][b0:b0 + ns], inp["c_ctx"][None, :]], axis=0)
    m["cT"] = np.ascontiguousarray(cc.reshape(ns + 1, 8, 128).transpose(2, 1, 0))
    for nm in ("w_ada", "w_in", "w_oa", "w_ob", "w_oc", "w_out", "w_up", "w_down"):
        m[nm] = np.stack([kmajor(inp[nm][l]) for l in range(2)])
    m["b_ada"] = np.stack([rep128(inp["b_ada"][l]) for l in range(2)])
    m["g4"] = np.stack([np.stack([rep128(inp[nm][l]) for nm in ("g_pre_mix", "g_post_mix", "g_pre_ffn", "g_post_ffn")])
                        for l in range(2)])
    lb = inp["hg_lower_bounds"]
    m["lbT"] = np.ascontiguousarray(lb.reshape(2, 2, 4, 128).transpose(3, 0, 1, 2))
    m["hgn"] = np.ascontiguousarray(inp["hg_norm"].T)
    m["gq"] = np.stack([rep128(np.tile(inp["q_norm"][l], 8)) for l in range(2)])
    m["gk"] = np.stack([rep128(np.tile(inp["k_norm"][l], 2)) for l in range(2)])
    cw = inp["hy_conv_w"]
    m["hycw"] = np.ascontiguousarray(cw.reshape(2, 3, 12, 128).transpose(3, 0, 2, 1))
    m["hycb"] = np.ascontiguousarray(inp["hy_conv_b"].reshape(2, 12, 128).transpose(2, 0, 1))
    m["hybias"] = np.ascontiguousarray(inp["hy_bias"].reshape(2, 4, 128).transpose(2, 0, 1))
    m["hyw1"] = np.ascontiguousarray(inp["hy_w1"])
    m["hyb1"] = np.ascontiguousarray(inp["hy_b1"].T)
    m["hywi"] = np.ascontiguousarray(inp["hy_wi"])
    m["hybi"] = np.ascontiguousarray(inp["hy_bi"].transpose(2, 0, 1))
    m["hyfreq"] = np.ascontiguousarray(inp["hy_freq"].T)
    m["hywl"] = np.ascontiguousarray(inp["hy_w_last"])
    fw = inp["ffn_conv_w"]
    m["ffcw"] = np.ascontiguousarray(fw.reshape(2, 3, 44, 128).transpose(3, 0, 2, 1))
    m["ffcb"] = np.ascontiguousarray(inp["ffn_conv_b"].reshape(2, 44, 128).transpose(2, 0, 1))
    return m


class Rot:
    def __init__(self, S, name, n, shape, dt, psum=False):
        self.bufs = [(S.ps if psum else S.sb)("%s%d" % (name, i), shape, dt) for i in range(n)]
        self.name = name
        self.i = 0

    def next(self):
        j = self.i % len(self.bufs)
        self.i += 1
        return self.bufs[j], (self.name, j)


class Ctx:
    pass


def bc(ap, shape):
    return ap.to_broadcast(list(shape))


def declare(nc, ns, consts, layout):
    d = {}
    for k, v in list(consts.items()) + list(layout.items()):
        dt = BF16 if v.dtype == NPBF else F32
        d[k] = nc.dram_tensor(k, list(v.shape), dt, kind="ExternalInput").ap()
    d["out"] = nc.dram_tensor("out", [ns, L, D], F32, kind="ExternalOutput").ap()
    d["xs"] = nc.dram_tensor("xs", [ns, T, D], F32, kind="Internal").ap()
    d["modrep"] = nc.dram_tensor("modrep", [2, ns + 1, 128, 6 * D], F32, kind="Internal").ap()
    d["kspx"] = nc.dram_tensor("kspx", [2, 128, 32, 512], F32, kind="Internal").ap()
    d["kspc"] = nc.dram_tensor("kspc", [128, 4, 512], F32, kind="Internal").ap()
    for nm in ("aTs", "bTs", "cTs"):
        d[nm] = nc.dram_tensor(nm, [128, 4, T], BF16, kind="Internal").ap()
    return d


def rstd_ops(S, out, ssum, inv_n, rt, wt):
    S.act(lambda e: e.activation(out, ssum, AF.Sqrt, bias=S.eps_ap(out), scale=inv_n), rt, wt)
    S.dve(lambda e: e.reciprocal(out, out), wt, wt)


def load_cast(S, stg, dst, src, kc, n, wtoks, eng):
    cap = stg.bufs[0].shape[1]
    kstep = max(1, min(kc, cap // n))
    for k0 in range(0, kc, kstep):
        k1 = min(kc, k0 + kstep)
        buf, tok = stg.next()
        view = buf[:, 0:(k1 - k0) * n].rearrange("p (k n) -> p k n", k=k1 - k0)
        S.dma("sp", view, src[:, k0:k1, :], reads=[], writes=[tok])
        S.castn = getattr(S, "castn", 0) + 1
        if S.castn % 2 == 0:
            S.act(lambda e, view=view, k0=k0, k1=k1: e.copy(dst[:, k0:k1, :], view), [tok], wtoks)
        else:
            S.dve(lambda e, view=view, k0=k0, k1=k1: e.tensor_copy(dst[:, k0:k1, :], view), [tok], wtoks)


def phase_mod(S, nc, d, ns):
    with ExitStack() as es:
        S.es = es
        cT = S.sb("cT_sb", [128, 8, ns + 1], F32)
        sil = S.sb("sil", [128, 8, ns + 1], F32)
        srep = S.sb("srep", [128, 8, ns + 1, 128], BF16)
        onesf = S.sb("onesf", [128, 128], F32)
        stg = Rot(S, "stg", 2, [128, 4096], F32)
        wb = Rot(S, "wb", 2, [128, 8, 512], BF16)
        bb = Rot(S, "bb", 2, [128, 512], F32)
        gb = Rot(S, "gb", 2, [128, 512], F32)
        ob = Rot(S, "ob", 3, [128, 512], F32)
        pp = Rot(S, "pm", 3, [128, 512], F32, psum=True)
        S.dma("sp", cT[:], d["cT"], [], ["cT"])
        S.act(lambda e: e.activation(sil[:], cT[:], AF.Silu), ["cT"], ["sil"])
        S.dve(lambda e: e.memset(onesf[:], 1.0), [], ["onesf"])
        for k in range(8):
            for s in range(ns + 1):
                S.dve(lambda e, k=k, s=s: e.tensor_scalar(srep[:, k, s, :], onesf[:], sil[:, k, s:s + 1], None, ALU.mult),
                      ["sil", "onesf"], ["srep"])
        for l in range(2):
            for cb in range(12):
                sec, half = cb // 2, cb % 2
                w, wt = wb.next()
                load_cast(S, stg, w[:], d["w_ada"][l, :, :, cb * 512:(cb + 1) * 512], 8, 512, [wt], "pool")
                b, bt = bb.next()
                S.dma("sp", b[:], d["b_ada"][l, :, cb * 512:(cb + 1) * 512], [], [bt])
                g, gt = gb.next()
                if sec in (1, 2, 4, 5):
                    gi = {1: 0, 2: 1, 4: 2, 5: 3}[sec]
                    S.dma("sp", g[:], d["g4"][l, gi, :, half * 512:(half + 1) * 512], [], [gt])
                for s in range(ns + 1):
                    p, pt = pp.next()
                    for k in range(8):
                        S.pe(lambda e, p=p, k=k, s=s, w=w: e.matmul(p[:], srep[:, k, s, :], w[:, k, :], start=(k == 0), stop=(k == 7)),
                             ["srep", wt], [pt])
                    o, ot = ob.next()
                    if sec in (0, 3):
                        S.dve(lambda e, o=o, p=p, b=b: e.tensor_tensor(o[:], p[:], b[:], ALU.add), [pt, bt], [ot])
                    elif sec in (1, 4):
                        S.dve(lambda e, o=o, p=p, b=b: e.scalar_tensor_tensor(o[:], p[:], 1.0, b[:], ALU.add, ALU.add), [pt, bt], [ot])
                        S.dve(lambda e, o=o, g=g: e.tensor_tensor(o[:], o[:], g[:], ALU.mult), [ot, gt], [ot])
                    else:
                        S.dve(lambda e, o=o, p=p, b=b: e.tensor_tensor(o[:], p[:], b[:], ALU.add), [pt, bt], [ot])
                        S.dve(lambda e, o=o, g=g: e.tensor_tensor(o[:], o[:], g[:], ALU.mult), [ot, gt], [ot])
                    S.dma("sp", d["modrep"][l, s, :, cb * 512:(cb + 1) * 512], o[:], [ot], [])
        S.flush()


def phase_norm(S, nc, d, l, b, sub, src, tiles, hT, ident, ns):
    with ExitStack() as es:
        S.es = es
        vs = S.sb("vs", [128, 2, 2, D], F32)
        for who, s in ((0, b), (1, ns)):
            if who == 1 and 0 not in tiles:
                continue
            S.dma("sp", vs[:, who, 0, :], d["modrep"][l, s, :, (3 * sub) * D:(3 * sub + 1) * D], [], ["vs"])
            S.dma("sp", vs[:, who, 1, :], d["modrep"][l, s, :, (3 * sub + 1) * D:(3 * sub + 2) * D], [], ["vs"])
        xt = Rot(S, "xt", 3, [128, D], F32)
        junk = S.sb("junk", [128, D], F32)
        ss = Rot(S, "ss", 2, [128, 1], F32)
        rs = Rot(S, "rs", 2, [128, 1], F32)
        h1 = Rot(S, "h1", 3, [128, D], F32)
        hb = Rot(S, "hb", 3, [128, D], BF16)
        pt_ = Rot(S, "ptr", 2, [128, 8, 128], BF16, psum=True)
        for i in tiles:
            who = 1 if i < 2 else 0
            x, xtok = xt.next()
            S.dma("sp", x[:], src(i), [], [xtok])
            s1, s1t = ss.next()
            S.act(lambda e, x=x, s1=s1: e.activation(junk[:], x[:], AF.Square, accum_out=s1[:]), [xtok], ["junk", s1t])
            r1, r1t = rs.next()
            rstd_ops(S, r1[:], s1[:], 1.0 / D, [s1t], [r1t])
            h, ht = h1.next()
            S.dve(lambda e, h=h, x=x, r1=r1, who=who: e.scalar_tensor_tensor(h[:], x[:], r1[:], vs[:, who, 1, :], ALU.mult, ALU.mult),
                  [xtok, r1t, "vs"], [ht])
            hh, hht = hb.next()
            S.pool(lambda e, hh=hh, h=h, who=who: e.tensor_tensor(hh[:], h[:], vs[:, who, 0, :], ALU.add), [ht, "vs"], [hht])
            p, ptk = pt_.next()
            for k in range(8):
                S.pe(lambda e, p=p, hh=hh, k=k: e.transpose(p[:, k, :], hh[:, k * 128:(k + 1) * 128], ident[:]), [hht, "ident"], [ptk])
            S.act(lambda e, p=p, i=i: e.copy(hT[:, :, i * 128:(i + 1) * 128], p[:]), [ptk], [("hT", i)])
        S.flush()


def proj_fm(S, pp, w, wtok, hT, ncols_off, t0, n):
    p, pt = pp.next()
    for k in range(8):
        S.pe(lambda e, p=p, k=k: e.matmul(p[:, 0:n], w[:, k, ncols_off:ncols_off + 128], hT[:, k, t0:t0 + n],
                                          start=(k == 0), stop=(k == 7)),
             [wtok] + [("hT", i) for i in range(t0 // 128, (t0 + n) // 128)], [pt])
    return p, pt


def phase_hgrn(S, nc, d, l, hT, cst):
    ident, ones, hgmask, rowmask, resetmask = cst["ident"], cst["ones"], cst["hgmask"], cst["rowmask"], cst["resetmask"]
    with ExitStack() as es:
        S.es = es
        stg = Rot(S, "stg", 1, [128, 2048], F32)
        wq = S.sb("wq", [128, 8, 512], BF16)
        wz = S.sb("wz", [128, 8, 512], BF16)
        wv = wz
        vtok = S.sb("vtok", [128, NT, 512], BF16)
        kglT = S.sb("kglT", [128, 4, T], BF16)
        oT = S.sb("oT", [128, 4, T], F32)
        qeT = S.sb("qeT", [128, 4, T], BF16)
        keT = S.sb("keT", [128, 4, T], BF16)
        gl = S.sb("gl", [128, 4, T // 32], F32)
        lbv = S.sb("lbv", [128, 2, 2, 4], F32)
        lo = S.sb("lo", [128, 2, 4], F32)
        oml = S.sb("oml", [128, 2, 4], F32)
        gn = S.sb("gn", [128, 2], F32)
        Sf = S.sb("Sf", [128, 4, 128], F32)
        Sb = S.sb("Sb", [128, 4, 128], BF16)
        pp = Rot(S, "pp", 2, [128, 512], F32, psum=True)
        S.dma("sp", lbv[:], d["lbT"], [], ["lbv"])
        S.dma("sp", gn[:], d["hgn"], [], ["gn"])
        if l == 0:
            S.dve(lambda e: e.memset(lo[:], 0.0), [], ["lo"])
        else:
            S.dve(lambda e: e.tensor_tensor(lo[:], lbv[:, 1, :, :], lbv[:, 0, :, :], ALU.subtract), ["lbv"], ["lo"])
            S.act(lambda e: e.activation(lo[:], lo[:], AF.Sigmoid), ["lo"], ["lo"])
        S.dve(lambda e: e.tensor_scalar(oml[:], lo[:], -1.0, 1.0, ALU.mult, ALU.add), ["lo"], ["oml"])
        S.dve(lambda e: e.memset(oT[:], 0.0), [], ["oT"])
        load_cast(S, stg, wq[:], d["w_in"][l, :, :, C_HQ:C_HQ + 512], 8, 512, ["wq"], "pool")
        load_cast(S, stg, wv[:], d["w_in"][l, :, :, C_HV:C_HV + 512], 8, 512, ["wz"], "pool")
        for i in range(NT):
            p, pt = pp.next()
            for k in range(8):
                S.pe(lambda e, p=p, k=k, i=i: e.matmul(p[:], hT[:, k, i * 128:(i + 1) * 128], wv[:, k, :], start=(k == 0), stop=(k == 7)),
                     ["wz", ("hT", i)], [pt])
            S.act(lambda e, p=p, i=i: e.copy(vtok[:, i, :], p[:]), [pt], [("vtok", i)])
        tA = Rot(S, "tA", 2, [128, 512], F32)
        tB = Rot(S, "tB", 2, [128, 512], F32)
        tC = Rot(S, "tC", 2, [128, 512], F32)
        tK = Rot(S, "tK", 1, [128, 512], F32)
        tD = Rot(S, "tD", 1, [128, 512], F32)
        pkt = Rot(S, "pkt", 1, [128, 4, 128], BF16, psum=True)
        pA = Rot(S, "pA", 1, [128, 4, 128], F32, psum=True)
        po = Rot(S, "po", 2, [128, 4, 128], F32, psum=True)
        pu4 = Rot(S, "pu4", 2, [128, 4, 128], F32, psum=True)
        ktm = S.sb("ktm", [128, 4, 512], BF16)
        AT = S.sb("AT", [128, 4, 128], BF16)
        if l == 0 and not getattr(S, "_rep_h", False):
            S._rep_h = True
            print("hgrn sbuf remaining", nc.sbuf_bytes_remaining)
        for dr in range(2):
            load_cast(S, stg, wz[:], d["w_in"][l, :, :, (C_ZF, C_ZB)[dr]:(C_ZF, C_ZB)[dr] + 512], 8, 512, ["wz"], "pool")
            for h in range(4):
                for (t0, n) in TG:
                    nch = n // 32
                    p, pt = proj_fm(S, pp, wz, "wz", hT, h * 128, t0, n)
                    a, at = tA.next()
                    S.act(lambda e, a=a, p=p, n=n: e.activation(a[:, 0:n], p[:, 0:n], AF.Sigmoid), [pt], [at])
                    S.dve(lambda e, a=a, n=n, h=h, dr=dr: e.tensor_scalar(a[:, 0:n], a[:, 0:n], oml[:, dr, h:h + 1], lo[:, dr, h:h + 1], ALU.mult, ALU.add),
                          [at, "oml", "lo"], [at])
                    kk, kt = tK.next()
                    S.pool(lambda e, kk=kk, a=a, n=n: e.tensor_scalar(kk[:, 0:n], a[:, 0:n], -1.0, 1.0, ALU.mult, ALU.add), [at], [kt])
                    b_, bt = tB.next()
                    S.act(lambda e, b_=b_, a=a, n=n: e.activation(b_[:, 0:n], a[:, 0:n], AF.Ln), [at], [bt])
                    c_, ct = tC.next()
                    S.dve(lambda e, c_=c_, b_=b_, n=n: e.tensor_tensor_scan(c_[:, 0:n], resetmask[:, 0:n], b_[:, 0:n], 0.0, ALU.mult, ALU.add),
                          [bt, "resetmask"], [ct])
                    if dr == 1:
                        c3 = c_[:, 0:n].rearrange("p (c t) -> p c t", t=32)
                        b3 = b_[:, 0:n].rearrange("p (c t) -> p c t", t=32)
                        S.pool(lambda e, b_=b_, c_=c_, n=n: e.tensor_tensor(b_[:, 0:n], b_[:, 0:n], c_[:, 0:n], ALU.subtract), [bt, ct], [bt])
                        g_, gt_ = tD.next()
                        g3 = g_[:, 0:n].rearrange("p (c t) -> p c t", t=32)
                        S.dve(lambda e, c3=c3, b3=b3, g3=g3, nch=nch: e.tensor_tensor(g3, b3, bc(c3[:, :, 31:32], [128, nch, 32]), ALU.add), [bt, ct], [gt_])
                        c_, ct = g_, gt_
                    S.act(lambda e, a=a, c_=c_, n=n: e.activation(a[:, 0:n], c_[:, 0:n], AF.Exp), [ct, at], [at])
                    a3 = a[:, 0:n].rearrange("p (c t) -> p c t", t=32)
                    pos = 31 if dr == 0 else 0
                    S.pool(lambda e, a3=a3, h=h, t0=t0, nch=nch, pos=pos: e.tensor_copy(gl[:, h, t0 // 32:t0 // 32 + nch], a3[:, :, pos]), [at], ["gl"])
                    q, qt = proj_fm(S, pp, wq, "wq", hT, h * 128, t0, n)
                    S.dve(lambda e, q=q, a=a, h=h, t0=t0, n=n: e.tensor_tensor(qeT[:, h, t0:t0 + n], q[:, 0:n], a[:, 0:n], ALU.mult),
                          [qt, at], [("qeT", h)])
                    S.act(lambda e, b_=b_, c_=c_, n=n: e.activation(b_[:, 0:n], c_[:, 0:n], AF.Exp, scale=-1.0), [ct, bt], [bt])
                    S.pool(lambda e, b_=b_, kk=kk, h=h, t0=t0, n=n: e.tensor_tensor(keT[:, h, t0:t0 + n], kk[:, 0:n], b_[:, 0:n], ALU.mult),
                           [bt, kt], [("keT", h)])
                    S.dve(lambda e, h=h, t0=t0, n=n, nch=nch: e.tensor_tensor(kglT[:, h, t0:t0 + n].rearrange("p (c t) -> p c t", t=32),
                                                                             keT[:, h, t0:t0 + n].rearrange("p (c t) -> p c t", t=32),
                                                                             bc(gl[:, h, t0 // 32:t0 // 32 + nch].unsqueeze(2), [128, nch, 32]), ALU.mult),
                          [("keT", h), "gl"], [("kglT", h)])
            S.dve(lambda e: e.memset(Sf[:], 0.0), [], [("Sf", h) for h in range(4)])
            S.pool(lambda e: e.memset(Sb[:], 0.0), [], [("Sb", h) for h in range(4)])
            order = list(range(NT)) if dr == 0 else [1, 0] + list(range(NT - 1, 1, -1))
            jord = [0, 1, 2, 3] if dr == 0 else [3, 2, 1, 0]
            for i in order:
                tk = slice(i * 128, (i + 1) * 128)
                pk, pkt_ = pkt.next()
                for h in range(4):
                    S.pe(lambda e, pk=pk, h=h, tk=tk: e.transpose(pk[:, h, :], kglT[:, h, tk], ident[:]), [("kglT", h), "ident"], [pkt_])
                for j in range(4):
                    eng = S.act if j % 2 == 0 else S.dve
                    if j % 2 == 0:
                        S.dve(lambda e, pk=pk, j=j: e.tensor_scalar(ktm[:, j, :], pk[:].rearrange("p h k -> p (h k)"), rowmask[:, j:j + 1], None, ALU.mult),
                              [pkt_, "rowmask"], [("ktm", j)])
                    else:
                        S.dve(lambda e, pk=pk, j=j: e.tensor_scalar(ktm[:, j, :], pk[:].rearrange("p h k -> p (h k)"), rowmask[:, j:j + 1], None, ALU.mult),
                              [pkt_, "rowmask"], [("ktm", j)])
                pa, pat = pA.next()
                for h in range(4):
                    S.pe(lambda e, pa=pa, h=h, tk=tk: e.matmul(pa[:, h, :], keT[:, h, tk], qeT[:, h, tk], start=True, stop=True),
                         [("keT", h), ("qeT", h)], [pat])
                S.dve(lambda e, pa=pa, dr=dr: e.tensor_tensor(AT[:].rearrange("p h t -> p (h t)"), pa[:].rearrange("p h t -> p (h t)"), hgmask[:, dr, :], ALU.mult),
                      [pat, "hgmask"], ["AT"])
                o_, ot = po.next()
                for h in range(4):
                    S.pe(lambda e, o_=o_, h=h, i=i: e.matmul(o_[:, h, :], vtok[:, i, h * 128:(h + 1) * 128], AT[:, h, :], start=(h == 0), stop=False,
                                                             skip_group_check=True),
                         [("vtok", i), "AT"], [ot])
                for jj, j in enumerate(jord):
                    c0 = i * 128 + 32 * j
                    ch = c0 // 32
                    u4, u4n = pu4.next()
                    for h in range(4):
                        S.pe(lambda e, o_=o_, h=h, j=j, c0=c0, jj=jj: e.matmul(o_[:, h, 32 * j:32 * j + 32], Sb[:, h, :], qeT[:, h, c0:c0 + 32],
                                                                               start=False, stop=(jj == 3), skip_group_check=True),
                             [("Sb", h), ("qeT", h)], [ot])
                        S.pe(lambda e, u4=u4, h=h, j=j, i=i: e.matmul(u4[:, h, :], ktm[:, j, h * 128:(h + 1) * 128], vtok[:, i, h * 128:(h + 1) * 128], start=True, stop=True),
                             [("ktm", j), ("vtok", i)], [u4n])
                    for h in range(4):
                        S.dve(lambda e, u4=u4, h=h, ch=ch: e.scalar_tensor_tensor(Sf[:, h, :], Sf[:, h, :], gl[:, h, ch:ch + 1], u4[:, h, :], ALU.mult, ALU.add),
                              [u4n, ("Sf", h), "gl"], [("Sf", h)])
                        S.act(lambda e, h=h: e.copy(Sb[:, h, :], Sf[:, h, :]), [("Sf", h)], [("Sb", h)])
                S.dve(lambda e, o_=o_, tk=tk: e.tensor_tensor(oT[:, :, tk], oT[:, :, tk], o_[:], ALU.add), [ot, "oT"], ["oT"])
        load_cast(S, stg, wz[:], d["w_in"][l, :, :, C_ZG:C_ZG + 512], 8, 512, ["wz"], "pool")
        sq = Rot(S, "sq", 2, [128, 512], BF16)
        aTb = Rot(S, "aTb", 2, [128, 512], BF16)
        for h in range(4):
            for (t0, n) in TG:
                s_, st = sq.next()
                S.act(lambda e, s_=s_, h=h, t0=t0, n=n: e.activation(s_[:, 0:n], oT[:, h, t0:t0 + n], AF.Square), ["oT"], [st])
                p, pt = pp.next()
                S.pe(lambda e, p=p, s_=s_, n=n: e.matmul(p[:, 0:n], ones[:], s_[:, 0:n], start=True, stop=True), [st, "ones"], [pt])
                a, at = tA.next()
                rstd_ops(S, a[:, 0:n], p[:, 0:n], 1.0 / 128, [pt], [at])
                z, zt = proj_fm(S, pp, wz, "wz", hT, h * 128, t0, n)
                b_, bt = tB.next()
                S.act(lambda e, b_=b_, z=z, n=n: e.activation(b_[:, 0:n], z[:, 0:n], AF.Silu), [zt], [bt])
                c_, ct = tC.next()
                S.dve(lambda e, c_=c_, a=a, h=h, t0=t0, n=n: e.scalar_tensor_tensor(c_[:, 0:n], oT[:, h, t0:t0 + n], gn[:, l:l + 1], a[:, 0:n], ALU.mult, ALU.mult),
                      ["oT", at, "gn"], [ct])
                o2, o2t = aTb.next()
                S.pool(lambda e, o2=o2, c_=c_, b_=b_, n=n: e.tensor_tensor(o2[:, 0:n], c_[:, 0:n], b_[:, 0:n], ALU.mult), [ct, bt], [o2t])
                S.dma("sp", d["aTs"][:, h, t0:t0 + n], o2[:, 0:n], [o2t], [])
        S.flush()


def phase_attn(S, nc, d, l, hT, cst, need_ctx):
    ident = cst["ident"]
    with ExitStack() as es:
        S.es = es
        stg = Rot(S, "stg", 2, [128, 2048], F32)
        wqkv = S.sb("wqkv", [128, 8, 768], BF16)
        qT = S.sb("qT", [128, 4, T], BF16)
        kT = S.sb("kT", [128, T], BF16)
        vaug = S.sb("vaug", [128, NT, 2, 65], BF16)
        btok = S.sb("btok", [128, NT, 512], BF16)
        gq = S.sb("gq", [128, 512], F32)
        gk = S.sb("gk", [128, 128], F32)
        S.dma("sp", gq[:], d["gq"][l], [], ["gq"])
        S.dma("sp", gk[:], d["gk"][l], [], ["gk"])
        S.dve(lambda e: e.memset(vaug[:], 1.0), [], ["vaug"])
        for (c0, c1, k0) in ((0, 512, 0), (512, 768, 512)):
            load_cast(S, stg, wqkv[:, :, c0:c1], d["w_in"][l, :, :, C_AQ + c0:C_AQ + c1], 8, c1 - c0, ["wqkv"], "pool")
        with ExitStack() as es2:
            S.es = es2
            pq = Rot(S, "pq", 3, [128, 512], F32, psum=True)
            pkv = Rot(S, "pkv", 2, [128, 256], F32, psum=True)
            ptq = Rot(S, "ptq", 2, [128, 4, 128], BF16, psum=True)
            ptk = Rot(S, "ptk", 1, [128, 128], BF16, psum=True)
            sqb = Rot(S, "sqb", 3, [128, 640], F32)
            ssb = Rot(S, "ssb", 3, [128, 10], F32)
            qn = Rot(S, "qn", 3, [128, 640], F32)
            qr = Rot(S, "qr", 3, [128, 640], BF16)
            qrp = Rot(S, "qrp", 3, [128, 512], BF16)
            cs = Rot(S, "cs", 3, [128, 2, 256], F32)
            tmp1 = Rot(S, "tmp1", 4, [128, 320], F32)
            tmp2 = Rot(S, "tmp2", 4, [128, 320], F32)
            for i in range(NT):
                p, pt = pq.next()
                for k in range(8):
                    S.pe(lambda e, p=p, k=k, i=i: e.matmul(p[:], hT[:, k, i * 128:(i + 1) * 128], wqkv[:, k, 0:512], start=(k == 0), stop=(k == 7)),
                         ["wqkv", ("hT", i)], [pt])
                p2, p2t = pkv.next()
                for k in range(8):
                    S.pe(lambda e, p2=p2, k=k, i=i: e.matmul(p2[:], hT[:, k, i * 128:(i + 1) * 128], wqkv[:, k, 512:768], start=(k == 0), stop=(k == 7)),
                         ["wqkv", ("hT", i)], [p2t])
                S.act(lambda e, p2=p2, i=i: e.copy(vaug[:, i, :, 0:64], p2[:, 128:256].rearrange("p (g d) -> p g d", g=2)), [p2t, "vaug"], [("vaug", i)])
                sq_, sqt = sqb.next()
                S.act(lambda e, sq_=sq_, p=p: e.activation(sq_[:, 0:512], p[:], AF.Square), [pt], [sqt])
                S.act(lambda e, sq_=sq_, p2=p2: e.activation(sq_[:, 512:640], p2[:, 0:128], AF.Square), [p2t], [sqt])
                s_, st = ssb.next()
                S.dve(lambda e, s_=s_, sq_=sq_: e.tensor_reduce(s_[:], sq_[:].rearrange("p (h d) -> p h d", d=64), AX.X, ALU.add), [sqt], [st])
                rstd_ops(S, s_[:], s_[:], 1.0 / 64, [st], [st])
                q_, qnt = qn.next()
                S.dve(lambda e, q_=q_, p=p, s_=s_: e.tensor_tensor(q_[:, 0:512].rearrange("p (h d) -> p h d", d=64), p[:].rearrange("p (h d) -> p h d", d=64),
                                                                   bc(s_[:, 0:8].unsqueeze(2), [128, 8, 64]), ALU.mult), [pt, st], [qnt])
                S.dve(lambda e, q_=q_, p2=p2, s_=s_: e.tensor_tensor(q_[:, 512:640].rearrange("p (h d) -> p h d", d=64), p2[:, 0:128].rearrange("p (h d) -> p h d", d=64),
                                                                     bc(s_[:, 8:10].unsqueeze(2), [128, 2, 64]), ALU.mult), [p2t, st], [qnt])
                S.pool(lambda e, q_=q_: e.tensor_tensor(q_[:, 0:512], q_[:, 0:512], gq[:], ALU.mult), [qnt, "gq"], [qnt])
                S.pool(lambda e, q_=q_: e.tensor_tensor(q_[:, 512:640], q_[:, 512:640], gk[:], ALU.mult), [qnt, "gk"], [qnt])
                r_, rt = qr.next()
                if i < 2:
                    S.dve(lambda e, r_=r_, q_=q_: e.tensor_copy(r_[:], q_[:]), [qnt], [rt])
                else:
                    c_, ct = cs.next()
                    r0 = (i - 2) * 128
                    S.dma("sp", c_[:, 0, :], d["cos8"][r0:r0 + 128, :], [], [ct])
                    S.dma("sp", c_[:, 1, :], d["sin8"][r0:r0 + 128, :], [], [ct])
                    x1 = q_[:].rearrange("p (n two) -> p n two", two=2)[:, :, 0]
                    x2 = q_[:].rearrange("p (n two) -> p n two", two=2)[:, :, 1]
                    o1 = r_[:].rearrange("p (n two) -> p n two", two=2)[:, :, 0]
                    o2 = r_[:].rearrange("p (n two) -> p n two", two=2)[:, :, 1]
                    t1, t1t = tmp1.next()
                    t2, t2t = tmp2.next()
                    for (a0, a1, cc0) in ((0, 256, 0), (256, 320, 0)):
                        w_ = a1 - a0
                        S.dve(lambda e, t1=t1, x1=x1, c_=c_, a0=a0, a1=a1, w_=w_: e.tensor_tensor(t1[:, a0:a1], x1[:, a0:a1], c_[:, 0, 0:w_], ALU.mult), [qnt, ct], [t1t])
                        S.pool(lambda e, t2=t2, x2=x2, c_=c_, a0=a0, a1=a1, w_=w_: e.tensor_tensor(t2[:, a0:a1], x2[:, a0:a1], c_[:, 1, 0:w_], ALU.mult), [qnt, ct], [t2t])
                    S.dve(lambda e, o1=o1, t1=t1, t2=t2: e.tensor_tensor(o1, t1[:], t2[:], ALU.subtract), [t1t, t2t], [rt])
                    t3, t3t = tmp1.next()
                    t4, t4t = tmp2.next()
                    for (a0, a1, cc0) in ((0, 256, 0), (256, 320, 0)):
                        w_ = a1 - a0
                        S.dve(lambda e, t3=t3, x1=x1, c_=c_, a0=a0, a1=a1, w_=w_: e.tensor_tensor(t3[:, a0:a1], x1[:, a0:a1], c_[:, 1, 0:w_], ALU.mult), [qnt, ct], [t3t])
                        S.pool(lambda e, t4=t4, x2=x2, c_=c_, a0=a0, a1=a1, w_=w_: e.tensor_tensor(t4[:, a0:a1], x2[:, a0:a1], c_[:, 0, 0:w_], ALU.mult), [qnt, ct], [t4t])
                    S.dve(lambda e, o2=o2, t3=t3, t4=t4: e.tensor_tensor(o2, t3[:], t4[:], ALU.add), [t3t, t4t, rt], [rt])
                tq, tqt = ptq.next()
                rp, rpt = qrp.next()
                S.pool(lambda e, rp=rp, r_=r_: e.tensor_copy(rp[:].rearrange("p (h g d) -> p h g d", h=4, g=2),
                                                             r_[:, 0:512].rearrange("p (g h d) -> p h g d", g=2, h=4)), [rt], [rpt])
                for pr in range(4):
                    S.pe(lambda e, tq=tq, rp=rp, pr=pr: e.transpose(tq[:, pr, :], rp[:, pr * 128:(pr + 1) * 128], ident[:]), [rpt, "ident"], [tqt])
                S.act(lambda e, tq=tq, i=i: e.copy(qT[:, :, i * 128:(i + 1) * 128], tq[:]), [tqt], [("qT", i)])
                tk_, tkt = ptk.next()
                S.pe(lambda e, tk_=tk_, r_=r_: e.transpose(tk_[:], r_[:, 512:640], ident[:]), [rt, "ident"], [tkt])
                S.dve(lambda e, tk_=tk_, i=i: e.tensor_copy(kT[:, i * 128:(i + 1) * 128], tk_[:]), [tkt], [("kT", i)])
            S.flush()
        S.es = es
        with ExitStack() as es3:
            S.es = es3
            ps_s = Rot(S, "ps_s", 2, [128, 2, 512], F32, psum=True)
            ps_o = Rot(S, "ps_o", 4, [128, 4, 128], F32, psum=True)
            PT = Rot(S, "PT", 3, [128, 2, 512], BF16)
            rc = Rot(S, "rc", 4, [128, 4], F32)
            groups = [(256 + 512 * g, 512, list(range(NT))) for g in range(4)]
            if need_ctx:
                groups.append((0, 256, [0, 1]))
            for (q0, nq, ktiles) in groups:
                nqt = nq // 128
                for pr in range(4):
                    obufs = [ps_o.next() for g in range(2)]
                    for ki, kc in enumerate(ktiles):
                        s_, st = ps_s.next()
                        for g in range(2):
                            ps = slice(64 * g, 64 * g + 64)
                            S.pe(lambda e, s_=s_, ps=ps, kc=kc, pr=pr, q0=q0, nq=nq, g=g: e.matmul(s_[:, g, 0:nq], kT[ps, kc * 128:(kc + 1) * 128], qT[ps, pr, q0:q0 + nq], start=True, stop=True),
                                 [("kT", kc)] + [("qT", q0 // 128 + t) for t in range(nqt)], [st])
                        p_, ptok = PT.next()
                        S.act(lambda e, p_=p_, s_=s_, nq=nq: e.activation(p_[:, :, 0:nq], s_[:, :, 0:nq], AF.Exp, scale=0.125), [st], [ptok])
                        for g in range(2):
                            o_, ot = obufs[g]
                            for t in range(nqt):
                                S.pe(lambda e, o_=o_, p_=p_, t=t, kc=kc, g=g, ki=ki, nk=len(ktiles): e.matmul(o_[:, t, 0:65], p_[:, g, t * 128:(t + 1) * 128], vaug[:, kc, g, :],
                                                                                                            start=(ki == 0 and t == 0), stop=(ki == nk - 1), skip_group_check=True),
                                     [ptok, ("vaug", kc)], [ot])
                    for g in range(2):
                        hd = pr + 4 * g
                        o_, ot = obufs[g]
                        r_, rt = rc.next()
                        S.dve(lambda e, r_=r_, o_=o_, nqt=nqt: e.reciprocal(r_[:, 0:nqt], o_[:, 0:nqt, 64]), [ot], [rt])
                        S.dve(lambda e, r_=r_, o_=o_, nqt=nqt, q0=q0, hd=hd: e.tensor_tensor(btok[:, q0 // 128:q0 // 128 + nqt, hd * 64:(hd + 1) * 64], o_[:, 0:nqt, 0:64],
                                                                                            bc(r_[:, 0:nqt].unsqueeze(2), [128, nqt, 64]), ALU.mult),
                              [ot, rt], [("btok", q0 // 128 + t) for t in range(nqt)])
            S.flush()
        S.es = es
        with ExitStack() as es4:
            S.es = es4
            ptb = Rot(S, "ptb", 2, [128, 4, 128], BF16, psum=True)
            bo = Rot(S, "bo", 2, [128, 4, 128], BF16)
            for i in range(0 if need_ctx else 2, NT):
                p, pt = ptb.next()
                for c in range(4):
                    S.pe(lambda e, p=p, c=c, i=i: e.transpose(p[:, c, :], btok[:, i, c * 128:(c + 1) * 128], ident[:]), [("btok", i), "ident"], [pt])
                o, ot = bo.next()
                S.act(lambda e, o=o, p=p: e.copy(o[:], p[:]), [pt], [ot])
                S.dma("sp", d["bTs"][:, :, i * 128:(i + 1) * 128], o[:], [ot], [])
            S.flush()


def sin_act(S, out, arg_ps, bcol, fcol, n, rt, wt, tmp):
    S.dve(lambda e: e.tensor_scalar(tmp[:, 0:n], arg_ps, bcol, fcol, ALU.add, ALU.mult), rt, ["sin_t"])
    MAGIC = 12582912.0
    kk = S.sin_kk
    S.dve(lambda e: e.tensor_scalar(kk[0:tmp.shape[0], 0:n], tmp[:, 0:n], 1.0 / (2.0 * math.pi), MAGIC, ALU.mult, ALU.add), ["sin_t"], ["sin_k"])
    S.dve(lambda e: e.tensor_scalar(kk[0:tmp.shape[0], 0:n], kk[0:tmp.shape[0], 0:n], -MAGIC, -2.0 * math.pi, ALU.add, ALU.mult), ["sin_k"], ["sin_k"])
    S.dve(lambda e: e.tensor_tensor(tmp[:, 0:n], tmp[:, 0:n], kk[0:tmp.shape[0], 0:n], ALU.add), ["sin_t", "sin_k"], ["sin_t"])
    S.dve(lambda e: e.tensor_scalar(tmp[:, 0:n], tmp[:, 0:n], -3.1415925, 3.1415925, ALU.max, ALU.min), ["sin_t"], ["sin_t"])
    S.act(lambda e: e.activation(out, tmp[:, 0:n], AF.Sin), ["sin_t"], [wt])


def phase_filters(S, nc, d, l, which):
    Ls, tag = (L, "x") if which == "x" else (LC, "c")
    nj = Ls // 128
    with ExitStack() as es:
        S.es = es
        w1 = S.sb("w1", [33, 64], F32)
        wi = S.sb("wi", [64, 2, 64], F32)
        wl = S.sb("wl", [64, 1024], F32)
        b1 = S.sb("b1", [64, 2], F32)
        bi = S.sb("bi", [64, 2, 2], F32)
        fq = S.sb("fq", [64, 2], F32)
        zT = S.sb("zT", [33, Ls], F32)
        h3 = S.sb("h3", [64, Ls], F32)
        Af = S.sb("Af", [128, nj, 512], BF16)
        Bf = S.sb("Bf", [128, nj, 512], BF16)
        S.dma("sp", w1[:], d["hyw1"][l], [], ["w"])
        S.dma("sp", wi[:], d["hywi"][l].rearrange("j i o -> i j o"), [], ["w"])
        S.dma("sp", wl[:], d["hywl"][l], [], ["w"])
        S.dma("sp", b1[:], d["hyb1"], [], ["w"])
        S.dma("sp", bi[:], d["hybi"], [], ["w"])
        S.dma("sp", fq[:], d["hyfreq"], [], ["w"])
        S.dma("sp", zT[:], d["zT" + tag], [], ["zT"])
        pm = Rot(S, "pm", 2, [64, 512], F32, psum=True)
        pf = Rot(S, "pf", 2, [128, 512], F32, psum=True)
        ha = Rot(S, "ha", 2, [64, 512], F32)
        hb = Rot(S, "hb", 2, [64, 512], F32)
        tmp = S.sb("tmpf", [64, 512], F32)
        S.sin_kk = S.sb("sinkk", [64, 512], F32)
        for t0 in range(0, Ls, 512):
            n = min(512, Ls - t0)
            p, pt = pm.next()
            S.pe(lambda e, p=p, t0=t0, n=n: e.matmul(p[:, 0:n], w1[:], zT[:, t0:t0 + n], start=True, stop=True), ["w", "zT"], [pt])
            a, at = ha.next()
            sin_act(S, a[:, 0:n], p[:, 0:n], b1[:, l:l + 1], fq[:, l:l + 1], n, [pt, "w"], at, tmp)
            p, pt = pm.next()
            S.pe(lambda e, p=p, a=a, n=n: e.matmul(p[:, 0:n], wi[:, 0, :], a[:, 0:n], start=True, stop=True), ["w", at], [pt])
            b_, bt = hb.next()
            sin_act(S, b_[:, 0:n], p[:, 0:n], bi[:, l, 0:1], fq[:, l:l + 1], n, [pt, "w"], bt, tmp)
            p, pt = pm.next()
            S.pe(lambda e, p=p, b_=b_, n=n: e.matmul(p[:, 0:n], wi[:, 1, :], b_[:, 0:n], start=True, stop=True), ["w", bt], [pt])
            sin_act(S, h3[:, t0:t0 + n], p[:, 0:n], bi[:, l, 1:2], fq[:, l:l + 1], n, [pt, "w"], "h3", tmp)
        dec = Rot(S, "dec", 2, [128, 512], F32)
        hf = Rot(S, "hf", 2, [128, 512], F32)
        hbk = Rot(S, "hbk", 2, [128, 512], F32)
        for j in range(nj):
            dc, dct = dec.next()
            S.dma("sp", dc[:], d["decay" + tag][j * 128:(j + 1) * 128, :], [], [dct])
            p1, p1t = pf.next()
            S.pe(lambda e, p1=p1, j=j: e.matmul(p1[:], h3[:, j * 128:(j + 1) * 128], wl[:, 0:512], start=True, stop=True), ["h3", "w"], [p1t])
            p2, p2t = pf.next()
            S.pe(lambda e, p2=p2, j=j: e.matmul(p2[:], h3[:, j * 128:(j + 1) * 128], wl[:, 512:1024], start=True, stop=True), ["h3", "w"], [p2t])
            f_, ft = hf.next()
            S.dve(lambda e, f_=f_, p1=p1, dc=dc: e.tensor_tensor(f_[:], p1[:], dc[:], ALU.mult), [p1t, dct], [ft])
            g_, gt = hbk.next()
            S.dve(lambda e, g_=g_, p2=p2, dc=dc: e.tensor_tensor(g_[:], p2[:], dc[:], ALU.mult), [p2t, dct], [gt])
            S.pool(lambda e, f_=f_, g_=g_, j=j: e.tensor_tensor(Af[:, j, :], f_[:], g_[:], ALU.add), [ft, gt], ["Af"])
            S.pool(lambda e, f_=f_, g_=g_, j=j: e.tensor_tensor(Bf[:, j, :], f_[:], g_[:], ALU.subtract), [ft, gt], ["Bf"])
        fw = Rot(S, "fw", 2, [128, nj, 128], BF16)
        ko = Rot(S, "ko", 2, [128, 512], F32)
        nf = Ls // 128
        for part, src in ((0, Af), (1, Bf)):
            for fc in range(nf):
                fcg = part * nf + fc
                w_, wt = fw.next()
                S.dma("sp", w_[:], d["fwd" + tag][:, :, fcg * 128:(fcg + 1) * 128], [], [wt])
                p, pt = pf.next()
                for j in range(nj):
                    S.pe(lambda e, p=p, w_=w_, j=j, src=src: e.matmul(p[:], w_[:, j, :], src[:, j, :], start=(j == 0), stop=(j == nj - 1)),
                         [wt, "Af", "Bf"], [pt])
                o, ot = ko.next()
                S.act(lambda e, o=o, p=p: e.copy(o[:], p[:]), [pt], [ot])
                dst = d["kspx"][l, :, fcg, :] if which == "x" else d["kspc"][:, fcg, :]
                S.dma("sp", dst, o[:], [ot], [])
        S.flush()


def phase_hyena(S, nc, d, l, hT, cst, need_ctx):
    ident = cst["ident"]
    with ExitStack() as es:
        S.es = es
        x0T = S.sb("x0T", [128, 4, T], BF16)
        uT = S.sb("uT", [128, 4, T], BF16)
        cw = S.sb("cw", [128, 12, 3], F32)
        cb = S.sb("cb", [128, 12], F32)
        hbias = S.sb("hbias", [128, 4], F32)
        S.dma("sp", cw[:], d["hycw"][:, l], [], ["cw"])
        S.dma("sp", cb[:], d["hycb"][:, l], [], ["cw"])
        S.dma("sp", hbias[:], d["hybias"][:, l], [], ["cw"])
        utok = S.sb("utok", [128, NT, 512], BF16)
        with ExitStack() as es2:
            S.es = es2
            stg = Rot(S, "stg", 2, [128, 2048], F32)
            wh = Rot(S, "wh", 2, [128, 8, 128], BF16)
            pp = Rot(S, "pp", 3, [128, 512], F32, psum=True)
            pz = Rot(S, "pz", 2, [128, T], F32)
            zc = Rot(S, "zc", 2, [128, T], F32)
            x1c = S.sb("x1c", [128, 4, T], BF16)
            segs = [(0, LC), (LC, T)]
            for ch in range(12):
                w_, wt = wh.next()
                load_cast(S, stg, w_[:], d["w_in"][l, :, :, C_HY + ch * 128:C_HY + (ch + 1) * 128], 8, 128, [wt], "pool")
                z_, zt = pz.next()
                for (t0, n) in TG:
                    p, pt = proj_fm(S, pp, w_, wt, hT, 0, t0, n)
                    S.act(lambda e, z_=z_, p=p, t0=t0, n=n: e.copy(z_[:, t0:t0 + n], p[:, 0:n]), [pt], [zt])
                c_, ct = zc.next()
                S.dve(lambda e, c_=c_, z_=z_, ch=ch: e.tensor_scalar(c_[:], z_[:], cw[:, ch, 1:2], cb[:, ch:ch + 1], ALU.mult, ALU.add), [zt, "cw"], [ct])
                for (a0, a1) in segs:
                    S.dve(lambda e, c_=c_, z_=z_, ch=ch, a0=a0, a1=a1: e.scalar_tensor_tensor(c_[:, a0 + 1:a1], z_[:, a0:a1 - 1], cw[:, ch, 0:1], c_[:, a0 + 1:a1], ALU.mult, ALU.add),
                          [zt, ct, "cw"], [ct])
                    S.dve(lambda e, c_=c_, z_=z_, ch=ch, a0=a0, a1=a1: e.scalar_tensor_tensor(c_[:, a0:a1 - 1], z_[:, a0 + 1:a1], cw[:, ch, 2:3], c_[:, a0:a1 - 1], ALU.mult, ALU.add),
                          [zt, ct, "cw"], [ct])
                sec, cc = ch // 4, ch % 4
                if sec == 0:
                    S.pool(lambda e, c_=c_, cc=cc: e.tensor_copy(x0T[:, cc, :], c_[:]), [ct], [("x0T", cc)])
                elif sec == 1:
                    S.pool(lambda e, c_=c_, cc=cc: e.tensor_copy(x1c[:, cc, :], c_[:]), [ct], [("x1c", cc)])
                else:
                    S.pool(lambda e, c_=c_, cc=cc: e.tensor_tensor(uT[:, cc, :], c_[:], x1c[:, cc, :], ALU.mult), [ct, ("x1c", cc)], [("uT", cc)])
            ptu = Rot(S, "ptu", 2, [128, 4, 128], BF16, psum=True)
            for i in range(NT):
                p, pt = ptu.next()
                for cc in range(4):
                    S.pe(lambda e, p=p, cc=cc, i=i: e.transpose(p[:, cc, :], uT[:, cc, i * 128:(i + 1) * 128], ident[:]), [("uT", cc), "ident"], [pt])
                S.act(lambda e, p=p, i=i: e.copy(utok[:, i, :], p[:].rearrange("p c t -> p (c t)")), [pt], [("utok", i)])
            S.flush()
        for (which, tb, Ls) in ((("c", 0, LC),) if need_ctx else ()) + (("x", LC, L),):
            nj = Ls // 128
            nf = Ls // 128
            i0 = tb // 128
            with ExitStack() as es3:
                S.es = es3
                YT = S.sb("YT", [128, 2 * nf, 512], BF16)
                fw = Rot(S, "fw", 3, [128, nj, 128], BF16)
                kr = Rot(S, "kr", 2, [128, 2, 512], F32)
                pu = Rot(S, "pu", 4, [128, 512], F32, psum=True)
                ur = Rot(S, "ur", 2, [128, 2, 512], F32)
                t1 = Rot(S, "t1", 2, [128, 512], F32)
                t2 = Rot(S, "t2", 2, [128, 512], F32)
                for fc in range(nf):
                    pr_ = []
                    for part in range(2):
                        fcg = part * nf + fc
                        w_, wt = fw.next()
                        S.dma("sp", w_[:], d["fwd" + which][:, :, fcg * 128:(fcg + 1) * 128], [], [wt])
                        p, pt = pu.next()
                        for j in range(nj):
                            S.pe(lambda e, p=p, w_=w_, j=j: e.matmul(p[:], w_[:, j, :], utok[:, i0 + j, :], start=(j == 0), stop=(j == nj - 1)),
                                 [wt, ("utok", i0 + j)], [pt])
                        pr_.append((p, pt))
                    k_, kt = kr.next()
                    for part in range(2):
                        src = d["kspx"][l, :, part * nf + fc, :] if which == "x" else d["kspc"][:, part * nf + fc, :]
                        S.dma("sp", k_[:, part, :], src, [], [kt])
                    u_, ut = ur.next()
                    S.act(lambda e, u_=u_, p=pr_[0][0]: e.copy(u_[:, 0, :], p[:]), [pr_[0][1]], [ut])
                    S.act(lambda e, u_=u_, p=pr_[1][0]: e.copy(u_[:, 1, :], p[:]), [pr_[1][1]], [ut])
                    a, at = t1.next()
                    b_, bt = t2.next()
                    S.dve(lambda e, a=a, u_=u_, k_=k_: e.tensor_tensor(a[:], u_[:, 0, :], k_[:, 0, :], ALU.mult), [ut, kt], [at])
                    S.pool(lambda e, b_=b_, u_=u_, k_=k_: e.tensor_tensor(b_[:], u_[:, 1, :], k_[:, 1, :], ALU.mult), [ut, kt], [bt])
                    S.dve(lambda e, a=a, b_=b_, fc=fc: e.tensor_tensor(YT[:, fc, :], a[:], b_[:], ALU.subtract), [at, bt], [("YT", fc)])
                    a2, a2t = t1.next()
                    b2, b2t = t2.next()
                    S.dve(lambda e, a2=a2, u_=u_, k_=k_: e.tensor_tensor(a2[:], u_[:, 0, :], k_[:, 1, :], ALU.mult), [ut, kt], [a2t])
                    S.pool(lambda e, b2=b2, u_=u_, k_=k_: e.tensor_tensor(b2[:], u_[:, 1, :], k_[:, 0, :], ALU.mult), [ut, kt], [b2t])
                    S.pool(lambda e, a2=a2, b2=b2, fc=fc, nf=nf: e.tensor_tensor(YT[:, nf + fc, :], a2[:], b2[:], ALU.add), [a2t, b2t], [("YT", nf + fc)])
                iv = Rot(S, "iv", 2, [128, 2 * nf, 256], BF16)
                py = Rot(S, "py", 3, [128, 256], F32, psum=True)
                y1 = Rot(S, "y1", 2, [128, 256], F32)
                co = Rot(S, "co", 2, [128, 256], BF16)
                for tg in range(Ls // 256):
                    v_, vt = iv.next()
                    S.dma("sp", v_[:], d["inv" + which][:, :, tg * 256:(tg + 1) * 256], [], [vt])
                    ta = tb + tg * 256
                    for cc in range(4):
                        p, pt = py.next()
                        for f in range(2 * nf):
                            S.pe(lambda e, p=p, v_=v_, f=f, cc=cc: e.matmul(p[:], YT[:, f, cc * 128:(cc + 1) * 128], v_[:, f, :], start=(f == 0), stop=(f == 2 * nf - 1)),
                                 [("YT", f), vt], [pt])
                        y_, yt = y1.next()
                        S.dve(lambda e, y_=y_, p=p, cc=cc, ta=ta: e.scalar_tensor_tensor(y_[:], uT[:, cc, ta:ta + 256], hbias[:, cc:cc + 1], p[:], ALU.mult, ALU.add),
                              [pt, ("uT", cc), "cw"], [yt])
                        o, ot = co.next()
                        S.pool(lambda e, o=o, y_=y_, cc=cc, ta=ta: e.tensor_tensor(o[:], y_[:], x0T[:, cc, ta:ta + 256], ALU.mult), [yt, ("x0T", cc)], [ot])
                        S.dma("sp", d["cTs"][:, cc, ta:ta + 256], o[:], [ot], [])
                S.flush()
            S.es = es


def post_residual(S, d, l, b, sub, ns, i, ypair, ytoks, src, dst, bufs):
    xt, junk, ss, rs, tmpb, al = bufs
    who = 1 if i < 2 else 0
    x, xtok = xt.next()
    S.dma("sp", x[:], src(i), [], [xtok])
    s1, s1t = ss.next()
    for hlf in range(2):
        S.act(lambda e, hlf=hlf, s1=s1: e.activation(junk[:], ypair[hlf][:], AF.Square, accum_out=s1[:, hlf:hlf + 1]), [ytoks[hlf]], ["junk", s1t])
    r1, r1t = rs.next()
    S.dve(lambda e, r1=r1, s1=s1: e.tensor_tensor(r1[:], s1[:, 0:1], s1[:, 1:2], ALU.add), [s1t], [r1t])
    rstd_ops(S, r1[:], r1[:], 1.0 / D, [r1t], [r1t])
    t_, tt = tmpb.next()
    for hlf in range(2):
        S.dve(lambda e, hlf=hlf, t_=t_, r1=r1, who=who: e.scalar_tensor_tensor(t_[:, hlf * 512:(hlf + 1) * 512], ypair[hlf][:], r1[:], al[:, who, hlf * 512:(hlf + 1) * 512], ALU.mult, ALU.mult),
              [ytoks[hlf], r1t, "al"], [tt])
    S.pool(lambda e, t_=t_, x=x: e.tensor_tensor(t_[:], t_[:], x[:], ALU.add), [tt, xtok], [tt])
    S.dma("sp", dst(i), t_[:], [tt], [])


def phase_merge(S, nc, d, l, b, ns, hT, tiles, src, dst):
    with ExitStack() as es:
        S.es = es
        stg = Rot(S, "stg", 4, [128, 2048], F32)
        wo = S.sb("wo", [128, 3, 4, 1024], BF16)
        wout = S.sb("wout", [128, 8, 1024], BF16)
        for bi_, nm in enumerate(("w_oa", "w_ob", "w_oc")):
            load_cast(S, stg, wo[:, bi_, :, :], d[nm][l], 4, 1024, ["wo"], "pool")
        load_cast(S, stg, wout[:], d["w_out"][l], 8, 1024, ["wout"], "pool")
        al = S.sb("al", [128, 2, D], F32)
        S.dma("sp", al[:, 0, :], d["modrep"][l, b, :, 2 * D:3 * D], [], ["al"])
        S.dma("sp", al[:, 1, :], d["modrep"][l, ns, :, 2 * D:3 * D], [], ["al"])
        wg = Rot(S, "wg", 4, [128, 8, 128], BF16)
        br = Rot(S, "br", 2, [128, 3, 4, 512], BF16)
        mT = Rot(S, "mT", 2, [128, 8, 512], BF16)
        pg = Rot(S, "pg", 2, [128, 512], F32, psum=True)
        py = Rot(S, "py", 2, [128, 512], F32, psum=True)
        po = Rot(S, "po", 4, [128, 512], F32, psum=True)
        sg = Rot(S, "sg", 2, [128, 512], F32)
        acc = Rot(S, "acc", 2, [128, 512], F32)
        tm = Rot(S, "tm", 2, [128, 512], F32)
        bufs = (Rot(S, "xt", 2, [128, D], F32), S.sb("junk", [128, 512], F32), Rot(S, "ss", 2, [128, 2], F32),
                Rot(S, "rs", 2, [128, 1], F32), Rot(S, "tmpb", 2, [128, D], F32), al)
        t_lo = min(tiles) * 128
        groups = [(t0, n) for (t0, n) in TG if t0 + n > t_lo]
        if t_lo > 0:
            groups = [(256, 256)] + [(512, 512), (1024, 512), (1536, 512), (2048, 256)]
        for (t0, n) in groups:
            b_, bt = br.next()
            for bi_, nm in enumerate(("aTs", "bTs", "cTs")):
                S.dma("sp", b_[:, bi_, :, 0:n], d[nm][:, :, t0:t0 + n], [], [bt])
            m_, mt = mT.next()
            for oc in range(8):
                a_, at = acc.next()
                for bi_ in range(3):
                    w_, wt = wg.next()
                    c0 = C_GT + bi_ * D + oc * 128
                    load_cast(S, stg, w_[:], d["w_in"][l, :, :, c0:c0 + 128], 8, 128, [wt], "pool")
                    g_, gt = proj_fm(S, pg, w_, wt, hT, 0, t0, n)
                    s_, st = sg.next()
                    S.act(lambda e, s_=s_, g_=g_, n=n: e.activation(s_[:, 0:n], g_[:, 0:n], AF.Sigmoid), [gt], [st])
                    y_, yt = py.next()
                    for k in range(4):
                        S.pe(lambda e, y_=y_, k=k, bi_=bi_, oc=oc, b_=b_, n=n: e.matmul(y_[:, 0:n], wo[:, bi_, k, oc * 128:(oc + 1) * 128], b_[:, bi_, k, 0:n], start=(k == 0), stop=(k == 3)),
                             ["wo", bt], [yt])
                    if bi_ == 0:
                        S.dve(lambda e, a_=a_, s_=s_, y_=y_, n=n: e.tensor_tensor(a_[:, 0:n], s_[:, 0:n], y_[:, 0:n], ALU.mult), [st, yt], [at])
                    else:
                        t_, tt = tm.next()
                        S.dve(lambda e, t_=t_, s_=s_, y_=y_, n=n: e.tensor_tensor(t_[:, 0:n], s_[:, 0:n], y_[:, 0:n], ALU.mult), [st, yt], [tt])
                        S.pool(lambda e, a_=a_, t_=t_, n=n: e.tensor_tensor(a_[:, 0:n], a_[:, 0:n], t_[:, 0:n], ALU.add), [at, tt], [at])
                S.act(lambda e, m_=m_, a_=a_, oc=oc, n=n: e.copy(m_[:, oc, 0:n], a_[:, 0:n]), [at], [mt])
            for ti in range(n // 128):
                i = t0 // 128 + ti
                ys, yts = [], []
                for hlf in range(2):
                    o_, ot = po.next()
                    for k in range(8):
                        S.pe(lambda e, o_=o_, k=k, ti=ti, hlf=hlf, m_=m_: e.matmul(o_[:], m_[:, k, ti * 128:(ti + 1) * 128], wout[:, k, hlf * 512:(hlf + 1) * 512], start=(k == 0), stop=(k == 7)),
                             [mt, "wout"], [ot])
                    ys.append(o_)
                    yts.append(ot)
                post_residual(S, d, l, b, 0, ns, i, ys, yts, src, dst, bufs)
        S.flush()


def phase_ffn(S, nc, d, l, b, ns, norm_fn, tiles, src, dst):
    t_lo = min(tiles) * 128
    segs = [(0, LC), (LC, T)] if t_lo == 0 else [(LC, T)]
    groups = [(t0, n) for (t0, n) in TG] if t_lo == 0 else [(256, 256), (512, 512), (1024, 512), (1536, 512), (2048, 256)]
    with ExitStack() as es:
        S.es = es
        gT = S.sb("gT", [128, 22, T], BF16)
        with ExitStack() as es2:
            S.es = es2
            hT = S.sb("hT2", [128, 8, T], BF16)
            norm_fn(hT)
            S.es = es2
            stg = Rot(S, "stg", 2, [128, 2048], F32)
            wu = Rot(S, "wu", 4, [128, 8, 128], BF16)
            pp = Rot(S, "pp", 3, [128, 512], F32, psum=True)
            ua = Rot(S, "ua", 2, [128, T], F32)
            uc = Rot(S, "uc", 2, [128, T], F32)
            fcw = S.sb("fcw", [128, 44, 3], F32)
            fcb = S.sb("fcb", [128, 44], F32)
            S.dma("sp", fcw[:], d["ffcw"][:, l], [], ["fcw"])
            S.dma("sp", fcb[:], d["ffcb"][:, l], [], ["fcw"])
            for j in range(22):
                res = []
                for half in range(2):
                    ch = half * 22 + j
                    w_, wt = wu.next()
                    load_cast(S, stg, w_[:], d["w_up"][l, :, :, ch * 128:(ch + 1) * 128], 8, 128, [wt], "pool")
                    z_, zt = ua.next()
                    for (t0, n) in groups:
                        p, pt = proj_fm(S, pp, w_, wt, hT, 0, t0, n)
                        S.act(lambda e, z_=z_, p=p, t0=t0, n=n: e.copy(z_[:, t0:t0 + n], p[:, 0:n]), [pt], [zt])
                    c_, ct = uc.next()
                    S.dve(lambda e, c_=c_, z_=z_, ch=ch: e.tensor_scalar(c_[:, t_lo:T], z_[:, t_lo:T], fcw[:, ch, 1:2], fcb[:, ch:ch + 1], ALU.mult, ALU.add), [zt, "fcw"], [ct])
                    for (a0, a1) in segs:
                        S.dve(lambda e, c_=c_, z_=z_, ch=ch, a0=a0, a1=a1: e.scalar_tensor_tensor(c_[:, a0 + 1:a1], z_[:, a0:a1 - 1], fcw[:, ch, 0:1], c_[:, a0 + 1:a1], ALU.mult, ALU.add),
                              [zt, ct, "fcw"], [ct])
                        S.dve(lambda e, c_=c_, z_=z_, ch=ch, a0=a0, a1=a1: e.scalar_tensor_tensor(c_[:, a0:a1 - 1], z_[:, a0 + 1:a1], fcw[:, ch, 2:3], c_[:, a0:a1 - 1], ALU.mult, ALU.add),
                              [zt, ct, "fcw"], [ct])
                    res.append((c_, ct))
                (ca, cat), (cb_, cbt) = res
                S.act(lambda e, ca=ca: e.activation(ca[:, t_lo:T], ca[:, t_lo:T], AF.Silu), [cat], [cat])
                S.pool(lambda e, ca=ca, cb_=cb_, j=j: e.tensor_tensor(gT[:, j, t_lo:T], ca[:, t_lo:T], cb_[:, t_lo:T], ALU.mult), [cat, cbt], [("gT", j)])
            S.flush()
        S.es = es
        with ExitStack() as es3:
            S.es = es3
            stg = Rot(S, "stg", 2, [128, 2048], F32)
            wd = S.sb("wd", [128, 22, 1024], BF16)
            load_cast(S, stg, wd[:], d["w_down"][l], 22, 1024, ["wd"], "pool")
            al = S.sb("al", [128, 2, D], F32)
            S.dma("sp", al[:, 0, :], d["modrep"][l, b, :, 5 * D:6 * D], [], ["al"])
            S.dma("sp", al[:, 1, :], d["modrep"][l, ns, :, 5 * D:6 * D], [], ["al"])
            po = Rot(S, "po", 4, [128, 512], F32, psum=True)
            bufs = (Rot(S, "xt", 2, [128, D], F32), S.sb("junk", [128, 512], F32), Rot(S, "ss", 2, [128, 2], F32),
                    Rot(S, "rs", 2, [128, 1], F32), Rot(S, "tmpb", 2, [128, D], F32), al)
            for i in tiles:
                ys, yts = [], []
                for hlf in range(2):
                    o_, ot = po.next()
                    for k in range(22):
                        S.pe(lambda e, o_=o_, k=k, i=i, hlf=hlf: e.matmul(o_[:], gT[:, k, i * 128:(i + 1) * 128], wd[:, k, hlf * 512:(hlf + 1) * 512], start=(k == 0), stop=(k == 21)),
                             ["wd"], [ot])
                    ys.append(o_)
                    yts.append(ot)
                post_residual(S, d, l, b, 1, ns, i, ys, yts, src, dst, bufs)
            S.flush()


def host_layout_shared(inp):
    m = host_layout(inp, 1, 0)
    m.pop("xin")
    m.pop("cT")
    return m


def host_layout_core(inp, ns, core):
    b0 = core * ns
    m = {}
    m["xin"] = np.ascontiguousarray(np.concatenate([inp["ctx"][b0:b0 + ns], inp["x"][b0:b0 + ns]], axis=1))
    cc = np.concatenate([inp["c"][b0:b0 + ns], inp["c_ctx"][None, :]], axis=0)
    m["cT"] = np.ascontiguousarray(cc.reshape(ns + 1, 8, 128).transpose(2, 1, 0))
    return m


def build_program(ns, consts, layout, stop_after=None):
    nc = bass.Bass("TRN2", target_bir_lowering=False)
    d = declare(nc, ns, consts, layout)
    with ExitStack() as es0:
        S = Sched(nc, es0)
        S.es = es0
        cst = {}
        cst["ident"] = S.sb("ident", [128, 128], BF16)
        cst["ones"] = S.sb("ones", [128, 128], BF16)
        cst["hgmask"] = S.sb("hgmask", [128, 2, 512], BF16)
        cst["rowmask"] = S.sb("rowmask", [128, 4], F32)
        cst["resetmask"] = S.sb("resetmask", [128, 512], F32)
        S.epsT = S.sb("epsT", [128, 1], F32)
        S.mpi = S.sb("mpi", [128, 1], F32)
        S.dma("sp", cst["ident"][:], d["ident"], [], [])
        S.dma("sp", cst["ones"][:], d["ones"], [], [])
        S.dma("sp", cst["hgmask"][:], d["hgmask"].rearrange("d p n -> p d n"), [], [])
        S.dma("sp", cst["rowmask"][:], d["rowmask"], [], [])
        S.dma("sp", cst["resetmask"][:], d["resetmask"], [], [])
        S.dve(lambda e: e.memset(S.epsT[:], EPS), [], [])
        S.dve(lambda e: e.memset(S.mpi[:], -math.pi), [], [])
        S.flush()
        phase_mod(S, nc, d, ns)
        for l in range(2):
            phase_filters(S, nc, d, l, "x")
        phase_filters(S, nc, d, 0, "c")
        import os
        STOP = int(os.environ.get("KSTOP", "99"))
        for l in range(2):
            need_ctx = (l == 0)
            if STOP < 2:
                break
            for b in range(ns):
                src0 = (lambda i, b=b: d["xin"][b, i * 128:(i + 1) * 128, :]) if l == 0 else (lambda i, b=b: d["xs"][b, i * 128:(i + 1) * 128, :])
                xs_t = lambda i, b=b: d["xs"][b, i * 128:(i + 1) * 128, :]
                tiles = list(range(NT)) if need_ctx else list(range(2, NT))
                with ExitStack() as esh:
                    S.es = esh
                    hT = S.sb("hT", [128, 8, T], BF16)
                    phase_norm(S, nc, d, l, b, 0, src0, list(range(NT)), hT, cst["ident"], ns)
                    if STOP >= 3:
                        phase_hgrn(S, nc, d, l, hT, cst)
                    if STOP >= 4:
                        phase_attn(S, nc, d, l, hT, cst, need_ctx)
                    if STOP >= 5:
                        phase_hyena(S, nc, d, l, hT, cst, need_ctx)
                    if STOP >= 6:
                        phase_merge(S, nc, d, l, b, ns, hT, tiles, src0, xs_t)
                if STOP < 7:
                    continue
                if l == 0:
                    dst = xs_t
                else:
                    dst = lambda i, b=b: d["out"][b, (i - 2) * 128:(i - 1) * 128, :]
                norm_fn = lambda hT2, l=l, b=b, tiles=tiles, xs_t=xs_t: phase_norm(S, nc, d, l, b, 1, xs_t, tiles, hT2, cst["ident"], ns)
                phase_ffn(S, nc, d, l, b, ns, norm_fn, tiles, xs_t, dst)
        S.es = es0
        S.flush()
    return nc


NS = 4


def kernel(**inputs):
    inp = {k: np.asarray(v, dtype=np.float32) for k, v in inputs.items()}
    consts = host_constants()
    shared = host_layout_shared(inp)
    cores = [host_layout_core(inp, NS, c) for c in range(NCORES)]
    layout = dict(shared)
    layout.update(cores[0])
    nc = build_program(NS, consts, layout)
    in_maps = []
    for c in range(NCORES):
        m = dict(consts)
        m.update(shared)
        m.update(cores[c])
        in_maps.append(m)
    res = run_bass_kernel_spmd(nc, in_maps, core_ids=list(range(NCORES)))
    out = np.concatenate([np.asarray(r["out"], dtype=np.float32) for r in res.results], axis=0)
    return out
```

```python
import numpy as np
import concourse.bass as bass
import concourse.mybir as mybir
from concourse.bass_utils import run_bass_kernel_spmd
from contextlib import ExitStack

F32 = mybir.dt.float32
BF16 = mybir.dt.bfloat16
AF = mybir.ActivationFunctionType
ALU = mybir.AluOpType
AX = mybir.AxisListType


class _Op:
    __slots__ = ("eng", "fn", "deps", "signal", "idx", "is_dma", "dsem", "dtarget", "seq")

    def __init__(self, eng, fn, is_dma=False):
        self.eng = eng
        self.fn = fn
        self.deps = []
        self.signal = False
        self.idx = 0
        self.is_dma = is_dma
        self.dsem = None
        self.dtarget = 0


class Sched:
    ENG = ("pe", "act", "dve", "pool", "sp")

    def __init__(self, nc, es, ndma=10, dma_queues=("sp", "pool", "act")):
        self.nc = nc
        self.es = es
        self.ops = []
        self.last_w = {}
        self.readers = {}
        self.sem = {e: es.enter_context(nc.semaphore("s_" + e)) for e in self.ENG}
        self.dsem = {q: [es.enter_context(nc.semaphore("d_%s%d" % (q, i))) for i in range(ndma)]
                     for q in dma_queues}
        self.dcount = {q: 0 for q in dma_queues}
        self.ndma = ndma
        self.last_dma = {}
        self.last_eng = {}
        self.n = 0

    def sb(self, name, shape, dt):
        self.uid = getattr(self, "uid", 0) + 1
        return self.es.enter_context(self.nc.sbuf_tensor("sb%d_%s" % (self.uid, name), list(shape), dt))

    def ps(self, name, shape, dt):
        self.uid = getattr(self, "uid", 0) + 1
        return self.es.enter_context(self.nc.psum_tensor("ps%d_%s" % (self.uid, name), list(shape), dt))

    def _record(self, op, reads, writes):
        deps = {}
        for t in reads:
            w = self.last_w.get(t)
            if w is not None:
                deps[id(w)] = (w, "raw")
        for t in writes:
            w = self.last_w.get(t)
            if w is not None and id(w) not in deps:
                deps[id(w)] = (w, "waw")
            for r in self.readers.get(t, ()):
                if id(r) not in deps:
                    deps[id(r)] = (r, "war")
        for p, kind in deps.values():
            if p is op:
                continue
            if not p.is_dma and p.eng == op.eng and not op.is_dma:
                if kind != "raw" or op.eng == "pe":
                    continue
            op.deps.append(p)
            if not p.is_dma:
                p.signal = True
        for t in reads:
            lst = self.readers.setdefault(t, [])
            if not op.is_dma:
                lst[:] = [r for r in lst if r.is_dma or r.eng != op.eng]
            lst.append(op)
        for t in writes:
            self.last_w[t] = op
            self.readers[t] = []
        self.ops.append(op)
        if not op.is_dma:
            self.last_eng[op.eng] = op
        self.n += 1
        return op

    def op(self, eng, fn, reads=(), writes=()):
        return self._record(_Op(eng, fn), reads, writes)

    def dma(self, q, out, in_, reads=(), writes=(), **kw):
        o = _Op(q, lambda e: e.dma_start(out=out, in_=in_, **kw), is_dma=True)
        i = self.dcount[q]
        self.dcount[q] += 1
        slot = i % self.ndma
        o.dsem = (q, slot)
        o.dtarget = 16 * (i // self.ndma + 1)
        prev = self.last_dma.get(o.dsem)
        if prev is not None:
            o.deps.append(prev)
        self.last_dma[o.dsem] = o
        return self._record(o, reads, writes)

    def barrier(self):
        lasts = list(self.last_eng.values()) + list(self.last_dma.values())
        for p in lasts:
            if not p.is_dma:
                p.signal = True
        for e in self.ENG:
            o = _Op(e, None)
            o.deps = [p for p in lasts if p.is_dma or p.eng != e]
            self.ops.append(o)
        self.last_w = {}
        self.readers = {}

    def finish(self):
        self.flush()

    def eps_ap(self, like):
        return self.epsT[0:like.shape[0], 0:1]

    def pe(self, fn, r=(), w=()):
        return self.op("pe", fn, r, w)

    def act(self, fn, r=(), w=()):
        return self.op("act", fn, r, w)

    def dve(self, fn, r=(), w=()):
        return self.op("dve", fn, r, w)

    def pool(self, fn, r=(), w=()):
        return self.op("pool", fn, r, w)

    def flush(self):
        self.barrier()
        if not hasattr(self, "_cnt"):
            self._cnt = {e: 0 for e in self.ENG}
            self._known = {e: {} for e in self.ENG}
        per = {e: [] for e in self.ENG}
        for o in self.ops:
            if o.signal:
                self._cnt[o.eng] += 1
                o.idx = self._cnt[o.eng]
            per[o.eng].append(o)
        self.ops = []
        nc = self.nc
        with nc.Block() as block:
            def emit(engname):
                def body(e):
                    known = self._known[engname]
                    for o in per[engname]:
                        need = {}
                        for p in o.deps:
                            if p.is_dma:
                                key, val = p.dsem, p.dtarget
                            else:
                                key, val = p.eng, p.idx
                            if known.get(key, 0) >= val:
                                continue
                            if need.get(key, 0) < val:
                                need[key] = val
                        for key, val in need.items():
                            s = self.dsem[key[0]][key[1]] if isinstance(key, tuple) else self.sem[key]
                            e.wait_ge(s, val)
                            known[key] = val
                        if o.fn is None:
                            continue
                        ins = o.fn(e)
                        if o.is_dma:
                            ins.then_inc(self.dsem[o.dsem[0]][o.dsem[1]], 16)
                        elif o.signal:
                            ins.then_inc(self.sem[engname], 1)
                return body
            block.tensor(emit("pe"))
            block.scalar(emit("act"))
            block.vector(emit("dve"))
            block.gpsimd(emit("pool"))
            block.sync(emit("sp"))


D = 1024
L = 2048
LC = 256
T = L + LC
NT = T // 128
DIN = 7936
DFF = 2816
NCORES = 8
EPS = 1e-6
import math
import ml_dtypes
NPBF = ml_dtypes.bfloat16

C_HQ, C_ZF, C_ZB, C_HV, C_ZG, C_AQ, C_AK, C_AV, C_HY, C_GT = 0, 512, 1024, 1536, 2048, 2560, 3072, 3200, 3328, 4864

TG = [(0, 512), (512, 512), (1024, 512), (1536, 512), (2048, 256)]


def kmajor(w):
    k, n = w.shape
    return np.ascontiguousarray(w.reshape(k // 128, 128, n).transpose(1, 0, 2))


def rep128(v):
    return np.ascontiguousarray(np.broadcast_to(np.asarray(v)[None, :], (128, v.shape[-1])))


def host_constants():
    c = {}
    c["ident"] = np.eye(128, dtype=np.float32).astype(NPBF)
    c["ones"] = np.ones((128, 128), np.float32).astype(NPBF)
    s = np.arange(128)[:, None]
    t = np.arange(128)[None, :]
    same = (s // 32) == (t // 32)
    mf = (same & (s <= t)).astype(np.float32)
    mb = (same & (s >= t)).astype(np.float32)
    c["hgmask"] = np.stack([np.tile(mf, (1, 4)), np.tile(mb, (1, 4))]).astype(NPBF)
    rm = np.zeros((128, 4), np.float32)
    for j in range(4):
        rm[32 * j:32 * j + 32, j] = 1.0
    c["rowmask"] = rm
    rs = np.ones((128, 512), np.float32)
    rs[:, ::32] = 0.0
    c["resetmask"] = rs
    rows = L // 64
    row = np.repeat(np.arange(rows), 64).astype(np.float32)
    col = np.tile(np.arange(64), rows).astype(np.float32)
    inv = (10000.0 ** (-np.arange(16, dtype=np.float32) / 16)).astype(np.float32)
    ang = np.concatenate([row[:, None] * inv, col[:, None] * inv], axis=-1).astype(np.float32)
    c["cos8"] = np.tile(np.cos(ang).astype(np.float32), (1, 8))
    c["sin8"] = np.tile(np.sin(ang).astype(np.float32), (1, 8))
    for Ls, tag in ((L, "x"), (LC, "c")):
        tt = np.linspace(0.0, 1.0, Ls, dtype=np.float32)[:, None]
        w = (2.0 * math.pi * np.arange(Ls, dtype=np.float32)[:, None] / Ls).astype(np.float32)
        f = np.linspace(1e-4, 15, 16, dtype=np.float32)[None, :]
        z = np.concatenate([tt, np.cos(f * w), -np.sin(f * w)], axis=-1).astype(np.float32)
        c["zT" + tag] = np.ascontiguousarray(z.T)
        deltas = np.abs(np.linspace(math.log(1e-2) / 1.5, math.log(1e-2) / 0.3, 512, dtype=np.float32))
        c["decay" + tag] = np.exp(-tt * deltas[None, :]).astype(np.float32)
        N = 2 * Ls
        om = 2.0 * np.pi * (np.arange(Ls, dtype=np.float64) + 0.5) / N
        tj = np.arange(Ls, dtype=np.float64)
        ph = tj[:, None] * om[None, :]
        fwd = np.concatenate([np.cos(ph), -np.sin(ph)], axis=1)
        inv_m = np.concatenate([np.cos(ph).T, -np.sin(ph).T], axis=0) * (2.0 / N)
        c["fwd" + tag] = kmajor(fwd.astype(np.float32)).astype(NPBF)
        c["inv" + tag] = kmajor(inv_m.astype(np.float32)).astype(NPBF)
    return c


def host_layout(inp, ns, core):
    b0 = core * ns
    m = {}
    m["xin"] = np.ascontiguousarray(np.concatenate([inp["ctx"][b0:b0 + ns], inp["x"][b0:b0 + ns]], axis=1))
    cc = np.concatenate([inp["c"][b0:b0 + ns], inp["c_ctx"][None, :]], axis=0)
    m["cT"] = np.ascontiguousarray(cc.reshape(ns + 1, 8, 128).transpose(2, 1, 0))
    for nm in ("w_ada", "w_in", "w_oa", "w_ob", "w_oc", "w_out", "w_up", "w_down"):
        m[nm] = np.stack([kmajor(inp[nm][l]) for l in range(2)])
    m["b_ada"] = np.stack([rep128(inp["b_ada"][l]) for l in range(2)])
    m["g4"] = np.stack([np.stack([rep128(inp[nm][l]) for nm in ("g_pre_mix", "g_post_mix", "g_pre_ffn", "g_post_ffn")])
                        for l in range(2)])
    lb = inp["hg_lower_bounds"]
    m["lbT"] = np.ascontiguousarray(lb.reshape(2, 2, 4, 128).transpose(3, 0, 1, 2))
    m["hgn"] = np.ascontiguousarray(inp["hg_norm"].T)
    m["gq"] = np.stack([rep128(np.tile(inp["q_norm"][l], 8)) for l in range(2)])
    m["gk"] = np.stack([rep128(np.tile(inp["k_norm"][l], 2)) for l in range(2)])
    cw = inp["hy_conv_w"]
    m["hycw"] = np.ascontiguousarray(cw.reshape(2, 3, 12, 128).transpose(3, 0, 2, 1))
    m["hycb"] = np.ascontiguousarray(inp["hy_conv_b"].reshape(2, 12, 128).transpose(2, 0, 1))
    m["hybias"] = np.ascontiguousarray(inp["hy_bias"].reshape(2, 4, 128).transpose(2, 0, 1))
    m["hyw1"] = np.ascontiguousarray(inp["hy_w1"])
    m["hyb1"] = np.ascontiguousarray(inp["hy_b1"].T)
    m["hywi"] = np.ascontiguousarray(inp["hy_wi"])
    m["hybi"] = np.ascontiguousarray(inp["hy_bi"].transpose(2, 0, 1))
    m["hyfreq"] = np.ascontiguousarray(inp["hy_freq"].T)
    m["hywl"] = np.ascontiguousarray(inp["hy_w_last"])
    fw = inp["ffn_conv_w"]
    m["ffcw"] = np.ascontiguousarray(fw.reshape(2, 3, 44, 128).transpose(3, 0, 2, 1))
    m["ffcb"] = np.ascontiguousarray(inp["ffn_conv_b"].reshape(2, 44, 128).transpose(2, 0, 1))
    return m


class Rot:
    def __init__(self, S, name, n, shape, dt, psum=False):
        self.bufs = [(S.ps if psum else S.sb)("%s%d" % (name, i), shape, dt) for i in range(n)]
        self.name = name
        self.i = 0

    def next(self):
        j = self.i % len(self.bufs)
        self.i += 1
        return self.bufs[j], (self.name, j)


class Ctx:
    pass


def bc(ap, shape):
    return ap.to_broadcast(list(shape))


def declare(nc, ns, consts, layout):
    d = {}
    for k, v in list(consts.items()) + list(layout.items()):
        dt = BF16 if v.dtype == NPBF else F32
        d[k] = nc.dram_tensor(k, list(v.shape), dt, kind="ExternalInput").ap()
    d["out"] = nc.dram_tensor("out", [ns, L, D], F32, kind="ExternalOutput").ap()
    d["xs"] = nc.dram_tensor("xs", [ns, T, D], F32, kind="Internal").ap()
    d["modrep"] = nc.dram_tensor("modrep", [2, ns + 1, 128, 6 * D], F32, kind="Internal").ap()
    d["kspx"] = nc.dram_tensor("kspx", [2, 128, 32, 512], F32, kind="Internal").ap()
    d["kspc"] = nc.dram_tensor("kspc", [128, 4, 512], F32, kind="Internal").ap()
    for nm in ("aTs", "bTs", "cTs"):
        d[nm] = nc.dram_tensor(nm, [128, 4, T], BF16, kind="Internal").ap()
    return d


def rstd_ops(S, out, ssum, inv_n, rt, wt):
    S.act(lambda e: e.activation(out, ssum, AF.Sqrt, bias=S.eps_ap(out), scale=inv_n), rt, wt)
    S.dve(lambda e: e.reciprocal(out, out), wt, wt)


def load_cast(S, stg, dst, src, kc, n, wtoks, eng):
    cap = stg.bufs[0].shape[1]
    kstep = max(1, min(kc, cap // n))
    for k0 in range(0, kc, kstep):
        k1 = min(kc, k0 + kstep)
        buf, tok = stg.next()
        view = buf[:, 0:(k1 - k0) * n].rearrange("p (k n) -> p k n", k=k1 - k0)
        S.dma("sp", view, src[:, k0:k1, :], reads=[], writes=[tok])
        S.castn = getattr(S, "castn", 0) + 1
        if S.castn % 2 == 0:
            S.act(lambda e, view=view, k0=k0, k1=k1: e.copy(dst[:, k0:k1, :], view), [tok], wtoks)
        else:
            S.dve(lambda e, view=view, k0=k0, k1=k1: e.tensor_copy(dst[:, k0:k1, :], view), [tok], wtoks)


def phase_mod(S, nc, d, ns):
    with ExitStack() as es:
        S.es = es
        cT = S.sb("cT_sb", [128, 8, ns + 1], F32)
        sil = S.sb("sil", [128, 8, ns + 1], F32)
        srep = S.sb("srep", [128, 8, ns + 1, 128], BF16)
        onesf = S.sb("onesf", [128, 128], F32)
        stg = Rot(S, "stg", 2, [128, 4096], F32)
        wb = Rot(S, "wb", 2, [128, 8, 512], BF16)
        bb = Rot(S, "bb", 2, [128, 512], F32)
        gb = Rot(S, "gb", 2, [128, 512], F32)
        ob = Rot(S, "ob", 3, [128, 512], F32)
        pp = Rot(S, "pm", 3, [128, 512], F32, psum=True)
        S.dma("sp", cT[:], d["cT"], [], ["cT"])
        S.act(lambda e: e.activation(sil[:], cT[:], AF.Silu), ["cT"], ["sil"])
        S.dve(lambda e: e.memset(onesf[:], 1.0), [], ["onesf"])
        for k in range(8):
            for s in range(ns + 1):
                S.dve(lambda e, k=k, s=s: e.tensor_scalar(srep[:, k, s, :], onesf[:], sil[:, k, s:s + 1], None, ALU.mult),
                      ["sil", "onesf"], ["srep"])
        for l in range(2):
            for cb in range(12):
                sec, half = cb // 2, cb % 2
                w, wt = wb.next()
                load_cast(S, stg, w[:], d["w_ada"][l, :, :, cb * 512:(cb + 1) * 512], 8, 512, [wt], "pool")
                b, bt = bb.next()
                S.dma("sp", b[:], d["b_ada"][l, :, cb * 512:(cb + 1) * 512], [], [bt])
                g, gt = gb.next()
                if sec in (1, 2, 4, 5):
                    gi = {1: 0, 2: 1, 4: 2, 5: 3}[sec]
                    S.dma("sp", g[:], d["g4"][l, gi, :, half * 512:(half + 1) * 512], [], [gt])
                for s in range(ns + 1):
                    p, pt = pp.next()
                    for k in range(8):
                        S.pe(lambda e, p=p, k=k, s=s, w=w: e.matmul(p[:], srep[:, k, s, :], w[:, k, :], start=(k == 0), stop=(k == 7)),
                             ["srep", wt], [pt])
                    o, ot = ob.next()
                    if sec in (0, 3):
                        S.dve(lambda e, o=o, p=p, b=b: e.tensor_tensor(o[:], p[:], b[:], ALU.add), [pt, bt], [ot])
                    elif sec in (1, 4):
                        S.dve(lambda e, o=o, p=p, b=b: e.scalar_tensor_tensor(o[:], p[:], 1.0, b[:], ALU.add, ALU.add), [pt, bt], [ot])
                        S.dve(lambda e, o=o, g=g: e.tensor_tensor(o[:], o[:], g[:], ALU.mult), [ot, gt], [ot])
                    else:
                        S.dve(lambda e, o=o, p=p, b=b: e.tensor_tensor(o[:], p[:], b[:], ALU.add), [pt, bt], [ot])
                        S.dve(lambda e, o=o, g=g: e.tensor_tensor(o[:], o[:], g[:], ALU.mult), [ot, gt], [ot])
                    S.dma("sp", d["modrep"][l, s, :, cb * 512:(cb + 1) * 512], o[:], [ot], [])
        S.flush()


def phase_norm(S, nc, d, l, b, sub, src, tiles, hT, ident, ns):
    with ExitStack() as es:
        S.es = es
        vs = S.sb("vs", [128, 2, 2, D], F32)
        for who, s in ((0, b), (1, ns)):
            if who == 1 and 0 not in tiles:
                continue
            S.dma("sp", vs[:, who, 0, :], d["modrep"][l, s, :, (3 * sub) * D:(3 * sub + 1) * D], [], ["vs"])
            S.dma("sp", vs[:, who, 1, :], d["modrep"][l, s, :, (3 * sub + 1) * D:(3 * sub + 2) * D], [], ["vs"])
        xt = Rot(S, "xt", 3, [128, D], F32)
        junk = S.sb("junk", [128, D], F32)
        ss = Rot(S, "ss", 2, [128, 1], F32)
        rs = Rot(S, "rs", 2, [128, 1], F32)
        h1 = Rot(S, "h1", 3, [128, D], F32)
        hb = Rot(S, "hb", 3, [128, D], BF16)
        pt_ = Rot(S, "ptr", 2, [128, 8, 128], BF16, psum=True)
        for i in tiles:
            who = 1 if i < 2 else 0
            x, xtok = xt.next()
            S.dma("sp", x[:], src(i), [], [xtok])
            s1, s1t = ss.next()
            S.act(lambda e, x=x, s1=s1: e.activation(junk[:], x[:], AF.Square, accum_out=s1[:]), [xtok], ["junk", s1t])
            r1, r1t = rs.next()
            rstd_ops(S, r1[:], s1[:], 1.0 / D, [s1t], [r1t])
            h, ht = h1.next()
            S.dve(lambda e, h=h, x=x, r1=r1, who=who: e.scalar_tensor_tensor(h[:], x[:], r1[:], vs[:, who, 1, :], ALU.mult, ALU.mult),
                  [xtok, r1t, "vs"], [ht])
            hh, hht = hb.next()
            S.dve(lambda e, hh=hh, h=h, who=who: e.tensor_tensor(hh[:], h[:], vs[:, who, 0, :], ALU.add), [ht, "vs"], [hht])
            p, ptk = pt_.next()
            for k in range(8):
                S.pe(lambda e, p=p, hh=hh, k=k: e.transpose(p[:, k, :], hh[:, k * 128:(k + 1) * 128], ident[:]), [hht, "ident"], [ptk])
            S.act(lambda e, p=p, i=i: e.copy(hT[:, :, i * 128:(i + 1) * 128], p[:]), [ptk], [("hT", i)])
        S.flush()


def proj_fm(S, pp, w, wtok, hT, ncols_off, t0, n):
    p, pt = pp.next()
    for k in range(8):
        S.pe(lambda e, p=p, k=k: e.matmul(p[:, 0:n], w[:, k, ncols_off:ncols_off + 128], hT[:, k, t0:t0 + n],
                                          start=(k == 0), stop=(k == 7)),
             [wtok] + [("hT", i) for i in range(t0 // 128, (t0 + n) // 128)], [pt])
    return p, pt


def phase_hgrn(S, nc, d, l, hT, cst):
    ident, ones, hgmask, rowmask, resetmask = cst["ident"], cst["ones"], cst["hgmask"], cst["rowmask"], cst["resetmask"]
    with ExitStack() as es:
        S.es = es
        stg = Rot(S, "stg", 1, [128, 2048], F32)
        wq = S.sb("wq", [128, 8, 512], BF16)
        wz = S.sb("wz", [128, 8, 512], BF16)
        wv = wz
        vtok = S.sb("vtok", [128, NT, 512], BF16)
        kglT = S.sb("kglT", [128, 4, T], BF16)
        oT = S.sb("oT", [128, 4, T], F32)
        qeT = S.sb("qeT", [128, 4, T], BF16)
        keT = S.sb("keT", [128, 4, T], BF16)
        gl = S.sb("gl", [128, 4, T // 32], F32)
        lbv = S.sb("lbv", [128, 2, 2, 4], F32)
        lo = S.sb("lo", [128, 2, 4], F32)
        oml = S.sb("oml", [128, 2, 4], F32)
        gn = S.sb("gn", [128, 2], F32)
        Sf = S.sb("Sf", [128, 4, 128], F32)
        Sb = S.sb("Sb", [128, 4, 128], BF16)
        pp = Rot(S, "pp", 2, [128, 512], F32, psum=True)
        S.dma("sp", lbv[:], d["lbT"], [], ["lbv"])
        S.dma("sp", gn[:], d["hgn"], [], ["gn"])
        if l == 0:
            S.dve(lambda e: e.memset(lo[:], 0.0), [], ["lo"])
        else:
            S.dve(lambda e: e.tensor_tensor(lo[:], lbv[:, 1, :, :], lbv[:, 0, :, :], ALU.subtract), ["lbv"], ["lo"])
            S.act(lambda e: e.activation(lo[:], lo[:], AF.Sigmoid), ["lo"], ["lo"])
        S.dve(lambda e: e.tensor_scalar(oml[:], lo[:], -1.0, 1.0, ALU.mult, ALU.add), ["lo"], ["oml"])
        S.dve(lambda e: e.memset(oT[:], 0.0), [], ["oT"])
        load_cast(S, stg, wq[:], d["w_in"][l, :, :, C_HQ:C_HQ + 512], 8, 512, ["wq"], "pool")
        load_cast(S, stg, wv[:], d["w_in"][l, :, :, C_HV:C_HV + 512], 8, 512, ["wz"], "pool")
        for i in range(NT):
            p, pt = pp.next()
            for k in range(8):
                S.pe(lambda e, p=p, k=k, i=i: e.matmul(p[:], hT[:, k, i * 128:(i + 1) * 128], wv[:, k, :], start=(k == 0), stop=(k == 7)),
                     ["wz", ("hT", i)], [pt])
            S.act(lambda e, p=p, i=i: e.copy(vtok[:, i, :], p[:]), [pt], [("vtok", i)])
        tA = Rot(S, "tA", 2, [128, 512], F32)
        tB = Rot(S, "tB", 2, [128, 512], F32)
        tC = Rot(S, "tC", 2, [128, 512], F32)
        tK = Rot(S, "tK", 1, [128, 512], F32)
        tD = Rot(S, "tD", 1, [128, 512], F32)
        pkt = Rot(S, "pkt", 1, [128, 4, 128], BF16, psum=True)
        pA = Rot(S, "pA", 1, [128, 4, 128], F32, psum=True)
        po = Rot(S, "po", 2, [128, 4, 128], F32, psum=True)
        pu4 = Rot(S, "pu4", 2, [128, 4, 128], F32, psum=True)
        ktm = S.sb("ktm", [128, 4, 512], BF16)
        AT = S.sb("AT", [128, 4, 128], BF16)
        if l == 0 and not getattr(S, "_rep_h", False):
            S._rep_h = True
            print("hgrn sbuf remaining", nc.sbuf_bytes_remaining)
        for dr in range(2):
            load_cast(S, stg, wz[:], d["w_in"][l, :, :, (C_ZF, C_ZB)[dr]:(C_ZF, C_ZB)[dr] + 512], 8, 512, ["wz"], "pool")
            for h in range(4):
                for (t0, n) in TG:
                    nch = n // 32
                    p, pt = proj_fm(S, pp, wz, "wz", hT, h * 128, t0, n)
                    a, at = tA.next()
                    S.act(lambda e, a=a, p=p, n=n: e.activation(a[:, 0:n], p[:, 0:n], AF.Sigmoid), [pt], [at])
                    S.dve(lambda e, a=a, n=n, h=h, dr=dr: e.tensor_scalar(a[:, 0:n], a[:, 0:n], oml[:, dr, h:h + 1], lo[:, dr, h:h + 1], ALU.mult, ALU.add),
                          [at, "oml", "lo"], [at])
                    kk, kt = tK.next()
                    S.pool(lambda e, kk=kk, a=a, n=n: e.tensor_scalar(kk[:, 0:n], a[:, 0:n], -1.0, 1.0, ALU.mult, ALU.add), [at], [kt])
                    b_, bt = tB.next()
                    S.act(lambda e, b_=b_, a=a, n=n: e.activation(b_[:, 0:n], a[:, 0:n], AF.Ln), [at], [bt])
                    c_, ct = tC.next()
                    S.dve(lambda e, c_=c_, b_=b_, n=n: e.tensor_tensor_scan(c_[:, 0:n], resetmask[:, 0:n], b_[:, 0:n], 0.0, ALU.mult, ALU.add),
                          [bt, "resetmask"], [ct])
                    if dr == 1:
                        c3 = c_[:, 0:n].rearrange("p (c t) -> p c t", t=32)
                        b3 = b_[:, 0:n].rearrange("p (c t) -> p c t", t=32)
                        S.pool(lambda e, b_=b_, c_=c_, n=n: e.tensor_tensor(b_[:, 0:n], b_[:, 0:n], c_[:, 0:n], ALU.subtract), [bt, ct], [bt])
                        g_, gt_ = tD.next()
                        g3 = g_[:, 0:n].rearrange("p (c t) -> p c t", t=32)
                        S.dve(lambda e, c3=c3, b3=b3, g3=g3, nch=nch: e.tensor_tensor(g3, b3, bc(c3[:, :, 31:32], [128, nch, 32]), ALU.add), [bt, ct], [gt_])
                        c_, ct = g_, gt_
                    S.act(lambda e, a=a, c_=c_, n=n: e.activation(a[:, 0:n], c_[:, 0:n], AF.Exp), [ct, at], [at])
                    a3 = a[:, 0:n].rearrange("p (c t) -> p c t", t=32)
                    pos = 31 if dr == 0 else 0
                    S.pool(lambda e, a3=a3, h=h, t0=t0, nch=nch, pos=pos: e.tensor_copy(gl[:, h, t0 // 32:t0 // 32 + nch], a3[:, :, pos]), [at], ["gl"])
                    q, qt = proj_fm(S, pp, wq, "wq", hT, h * 128, t0, n)
                    S.dve(lambda e, q=q, a=a, h=h, t0=t0, n=n: e.tensor_tensor(qeT[:, h, t0:t0 + n], q[:, 0:n], a[:, 0:n], ALU.mult),
                          [qt, at], [("qeT", h)])
                    S.act(lambda e, b_=b_, c_=c_, n=n: e.activation(b_[:, 0:n], c_[:, 0:n], AF.Exp, scale=-1.0), [ct, bt], [bt])
                    S.pool(lambda e, b_=b_, kk=kk, h=h, t0=t0, n=n: e.tensor_tensor(keT[:, h, t0:t0 + n], kk[:, 0:n], b_[:, 0:n], ALU.mult),
                           [bt, kt], [("keT", h)])
                    S.dve(lambda e, h=h, t0=t0, n=n, nch=nch: e.tensor_tensor(kglT[:, h, t0:t0 + n].rearrange("p (c t) -> p c t", t=32),
                                                                             keT[:, h, t0:t0 + n].rearrange("p (c t) -> p c t", t=32),
                                                                             bc(gl[:, h, t0 // 32:t0 // 32 + nch].unsqueeze(2), [128, nch, 32]), ALU.mult),
                          [("keT", h), "gl"], [("kglT", h)])
            S.dve(lambda e: e.memset(Sf[:], 0.0), [], [("Sf", h) for h in range(4)])
            S.pool(lambda e: e.memset(Sb[:], 0.0), [], [("Sb", h) for h in range(4)])
            order = list(range(NT)) if dr == 0 else [1, 0] + list(range(NT - 1, 1, -1))
            jord = [0, 1, 2, 3] if dr == 0 else [3, 2, 1, 0]
            for i in order:
                tk = slice(i * 128, (i + 1) * 128)
                pk, pkt_ = pkt.next()
                for h in range(4):
                    S.pe(lambda e, pk=pk, h=h, tk=tk: e.transpose(pk[:, h, :], kglT[:, h, tk], ident[:]), [("kglT", h), "ident"], [pkt_])
                for j in range(4):
                    eng = S.act if j % 2 == 0 else S.dve
                    if j % 2 == 0:
                        S.dve(lambda e, pk=pk, j=j: e.tensor_scalar(ktm[:, j, :], pk[:].rearrange("p h k -> p (h k)"), rowmask[:, j:j + 1], None, ALU.mult),
                              [pkt_, "rowmask"], [("ktm", j)])
                    else:
                        S.dve(lambda e, pk=pk, j=j: e.tensor_scalar(ktm[:, j, :], pk[:].rearrange("p h k -> p (h k)"), rowmask[:, j:j + 1], None, ALU.mult),
                              [pkt_, "rowmask"], [("ktm", j)])
                pa, pat = pA.next()
                for h in range(4):
                    S.pe(lambda e, pa=pa, h=h, tk=tk: e.matmul(pa[:, h, :], keT[:, h, tk], qeT[:, h, tk], start=True, stop=True),
                         [("keT", h), ("qeT", h)], [pat])
                S.dve(lambda e, pa=pa, dr=dr: e.tensor_tensor(AT[:].rearrange("p h t -> p (h t)"), pa[:].rearrange("p h t -> p (h t)"), hgmask[:, dr, :], ALU.mult),
                      [pat, "hgmask"], ["AT"])
                o_, ot = po.next()
                for h in range(4):
                    S.pe(lambda e, o_=o_, h=h, i=i: e.matmul(o_[:, h, :], vtok[:, i, h * 128:(h + 1) * 128], AT[:, h, :], start=(h == 0), stop=False,
                                                             skip_group_check=True),
                         [("vtok", i), "AT"], [ot])
                for jj, j in enumerate(jord):
                    c0 = i * 128 + 32 * j
                    ch = c0 // 32
                    u4, u4n = pu4.next()
                    for h in range(4):
                        S.pe(lambda e, o_=o_, h=h, j=j, c0=c0, jj=jj: e.matmul(o_[:, h, 32 * j:32 * j + 32], Sb[:, h, :], qeT[:, h, c0:c0 + 32],
                                                                               start=False, stop=(jj == 3), skip_group_check=True),
                             [("Sb", h), ("qeT", h)], [ot])
                        S.pe(lambda e, u4=u4, h=h, j=j, i=i: e.matmul(u4[:, h, :], ktm[:, j, h * 128:(h + 1) * 128], vtok[:, i, h * 128:(h + 1) * 128], start=True, stop=True),
                             [("ktm", j), ("vtok", i)], [u4n])
                    for h in range(4):
                        S.dve(lambda e, u4=u4, h=h, ch=ch: e.scalar_tensor_tensor(Sf[:, h, :], Sf[:, h, :], gl[:, h, ch:ch + 1], u4[:, h, :], ALU.mult, ALU.add),
                              [u4n, ("Sf", h), "gl"], [("Sf", h)])
                        S.act(lambda e, h=h: e.copy(Sb[:, h, :], Sf[:, h, :]), [("Sf", h)], [("Sb", h)])
                S.dve(lambda e, o_=o_, tk=tk: e.tensor_tensor(oT[:, :, tk], oT[:, :, tk], o_[:], ALU.add), [ot, "oT"], ["oT"])
        load_cast(S, stg, wz[:], d["w_in"][l, :, :, C_ZG:C_ZG + 512], 8, 512, ["wz"], "pool")
        sq = Rot(S, "sq", 2, [128, 512], BF16)
        aTb = Rot(S, "aTb", 2, [128, 512], BF16)
        for h in range(4):
            for (t0, n) in TG:
                s_, st = sq.next()
                S.act(lambda e, s_=s_, h=h, t0=t0, n=n: e.activation(s_[:, 0:n], oT[:, h, t0:t0 + n], AF.Square), ["oT"], [st])
                p, pt = pp.next()
                S.pe(lambda e, p=p, s_=s_, n=n: e.matmul(p[:, 0:n], ones[:], s_[:, 0:n], start=True, stop=True), [st, "ones"], [pt])
                a, at = tA.next()
                rstd_ops(S, a[:, 0:n], p[:, 0:n], 1.0 / 128, [pt], [at])
                z, zt = proj_fm(S, pp, wz, "wz", hT, h * 128, t0, n)
                b_, bt = tB.next()
                S.act(lambda e, b_=b_, z=z, n=n: e.activation(b_[:, 0:n], z[:, 0:n], AF.Silu), [zt], [bt])
                c_, ct = tC.next()
                S.dve(lambda e, c_=c_, a=a, h=h, t0=t0, n=n: e.scalar_tensor_tensor(c_[:, 0:n], oT[:, h, t0:t0 + n], gn[:, l:l + 1], a[:, 0:n], ALU.mult, ALU.mult),
                      ["oT", at, "gn"], [ct])
                o2, o2t = aTb.next()
                S.pool(lambda e, o2=o2, c_=c_, b_=b_, n=n: e.tensor_tensor(o2[:, 0:n], c_[:, 0:n], b_[:, 0:n], ALU.mult), [ct, bt], [o2t])
                S.dma("sp", d["aTs"][:, h, t0:t0 + n], o2[:, 0:n], [o2t], [])
        S.flush()


def phase_attn(S, nc, d, l, hT, cst, need_ctx):
    ident = cst["ident"]
    with ExitStack() as es:
        S.es = es
        stg = Rot(S, "stg", 2, [128, 2048], F32)
        wqkv = S.sb("wqkv", [128, 8, 768], BF16)
        qT = S.sb("qT", [128, 4, T], BF16)
        kT = S.sb("kT", [128, T], BF16)
        vaug = S.sb("vaug", [128, NT, 2, 65], BF16)
        btok = S.sb("btok", [128, NT, 512], BF16)
        gq = S.sb("gq", [128, 512], F32)
        gk = S.sb("gk", [128, 128], F32)
        S.dma("sp", gq[:], d["gq"][l], [], ["gq"])
        S.dma("sp", gk[:], d["gk"][l], [], ["gk"])
        S.dve(lambda e: e.memset(vaug[:], 1.0), [], ["vaug"])
        for (c0, c1, k0) in ((0, 512, 0), (512, 768, 512)):
            load_cast(S, stg, wqkv[:, :, c0:c1], d["w_in"][l, :, :, C_AQ + c0:C_AQ + c1], 8, c1 - c0, ["wqkv"], "pool")
        with ExitStack() as es2:
            S.es = es2
            pq = Rot(S, "pq", 3, [128, 512], F32, psum=True)
            pkv = Rot(S, "pkv", 2, [128, 256], F32, psum=True)
            ptq = Rot(S, "ptq", 2, [128, 4, 128], BF16, psum=True)
            ptk = Rot(S, "ptk", 1, [128, 128], BF16, psum=True)
            sqb = Rot(S, "sqb", 3, [128, 640], F32)
            ssb = Rot(S, "ssb", 3, [128, 10], F32)
            qn = Rot(S, "qn", 3, [128, 640], F32)
            qr = Rot(S, "qr", 3, [128, 640], BF16)
            qrp = Rot(S, "qrp", 3, [128, 512], BF16)
            cs = Rot(S, "cs", 3, [128, 2, 256], F32)
            tmp1 = Rot(S, "tmp1", 4, [128, 320], F32)
            tmp2 = Rot(S, "tmp2", 4, [128, 320], F32)
            for i in range(NT):
                p, pt = pq.next()
                for k in range(8):
                    S.pe(lambda e, p=p, k=k, i=i: e.matmul(p[:], hT[:, k, i * 128:(i + 1) * 128], wqkv[:, k, 0:512], start=(k == 0), stop=(k == 7)),
                         ["wqkv", ("hT", i)], [pt])
                p2, p2t = pkv.next()
                for k in range(8):
                    S.pe(lambda e, p2=p2, k=k, i=i: e.matmul(p2[:], hT[:, k, i * 128:(i + 1) * 128], wqkv[:, k, 512:768], start=(k == 0), stop=(k == 7)),
                         ["wqkv", ("hT", i)], [p2t])
                S.act(lambda e, p2=p2, i=i: e.copy(vaug[:, i, :, 0:64], p2[:, 128:256].rearrange("p (g d) -> p g d", g=2)), [p2t, "vaug"], [("vaug", i)])
                sq_, sqt = sqb.next()
                S.act(lambda e, sq_=sq_, p=p: e.activation(sq_[:, 0:512], p[:], AF.Square), [pt], [sqt])
                S.act(lambda e, sq_=sq_, p2=p2: e.activation(sq_[:, 512:640], p2[:, 0:128], AF.Square), [p2t], [sqt])
                s_, st = ssb.next()
                S.dve(lambda e, s_=s_, sq_=sq_: e.tensor_reduce(s_[:], sq_[:].rearrange("p (h d) -> p h d", d=64), AX.X, ALU.add), [sqt], [st])
                rstd_ops(S, s_[:], s_[:], 1.0 / 64, [st], [st])
                q_, qnt = qn.next()
                S.dve(lambda e, q_=q_, p=p, s_=s_: e.tensor_tensor(q_[:, 0:512].rearrange("p (h d) -> p h d", d=64), p[:].rearrange("p (h d) -> p h d", d=64),
                                                                   bc(s_[:, 0:8].unsqueeze(2), [128, 8, 64]), ALU.mult), [pt, st], [qnt])
                S.dve(lambda e, q_=q_, p2=p2, s_=s_: e.tensor_tensor(q_[:, 512:640].rearrange("p (h d) -> p h d", d=64), p2[:, 0:128].rearrange("p (h d) -> p h d", d=64),
                                                                     bc(s_[:, 8:10].unsqueeze(2), [128, 2, 64]), ALU.mult), [p2t, st], [qnt])
                S.dve(lambda e, q_=q_: e.tensor_tensor(q_[:, 0:512], q_[:, 0:512], gq[:], ALU.mult), [qnt, "gq"], [qnt])
                S.dve(lambda e, q_=q_: e.tensor_tensor(q_[:, 512:640], q_[:, 512:640], gk[:], ALU.mult), [qnt, "gk"], [qnt])
                r_, rt = qr.next()
                if i < 2:
                    S.dve(lambda e, r_=r_, q_=q_: e.tensor_copy(r_[:], q_[:]), [qnt], [rt])
                else:
                    c_, ct = cs.next()
                    r0 = (i - 2) * 128
                    S.dma("sp", c_[:, 0, :], d["cos8"][r0:r0 + 128, :], [], [ct])
                    S.dma("sp", c_[:, 1, :], d["sin8"][r0:r0 + 128, :], [], [ct])
                    x1 = q_[:].rearrange("p (n two) -> p n two", two=2)[:, :, 0]
                    x2 = q_[:].rearrange("p (n two) -> p n two", two=2)[:, :, 1]
                    o1 = r_[:].rearrange("p (n two) -> p n two", two=2)[:, :, 0]
                    o2 = r_[:].rearrange("p (n two) -> p n two", two=2)[:, :, 1]
                    t1, t1t = tmp1.next()
                    t2, t2t = tmp2.next()
                    for (a0, a1, cc0) in ((0, 256, 0), (256, 320, 0)):
                        w_ = a1 - a0
                        S.dve(lambda e, t1=t1, x1=x1, c_=c_, a0=a0, a1=a1, w_=w_: e.tensor_tensor(t1[:, a0:a1], x1[:, a0:a1], c_[:, 0, 0:w_], ALU.mult), [qnt, ct], [t1t])
                        S.pool(lambda e, t2=t2, x2=x2, c_=c_, a0=a0, a1=a1, w_=w_: e.tensor_tensor(t2[:, a0:a1], x2[:, a0:a1], c_[:, 1, 0:w_], ALU.mult), [qnt, ct], [t2t])
                    S.dve(lambda e, o1=o1, t1=t1, t2=t2: e.tensor_tensor(o1, t1[:], t2[:], ALU.subtract), [t1t, t2t], [rt])
                    t3, t3t = tmp1.next()
                    t4, t4t = tmp2.next()
                    for (a0, a1, cc0) in ((0, 256, 0), (256, 320, 0)):
                        w_ = a1 - a0
                        S.dve(lambda e, t3=t3, x1=x1, c_=c_, a0=a0, a1=a1, w_=w_: e.tensor_tensor(t3[:, a0:a1], x1[:, a0:a1], c_[:, 1, 0:w_], ALU.mult), [qnt, ct], [t3t])
                        S.pool(lambda e, t4=t4, x2=x2, c_=c_, a0=a0, a1=a1, w_=w_: e.tensor_tensor(t4[:, a0:a1], x2[:, a0:a1], c_[:, 0, 0:w_], ALU.mult), [qnt, ct], [t4t])
                    S.dve(lambda e, o2=o2, t3=t3, t4=t4: e.tensor_tensor(o2, t3[:], t4[:], ALU.add), [t3t, t4t, rt], [rt])
                tq, tqt = ptq.next()
                rp, rpt = qrp.next()
                S.dve(lambda e, rp=rp, r_=r_: e.tensor_copy(rp[:].rearrange("p (h g d) -> p h g d", h=4, g=2),
                                                             r_[:, 0:512].rearrange("p (g h d) -> p h g d", g=2, h=4)), [rt], [rpt])
                for pr in range(4):
                    S.pe(lambda e, tq=tq, rp=rp, pr=pr: e.transpose(tq[:, pr, :], rp[:, pr * 128:(pr + 1) * 128], ident[:]), [rpt, "ident"], [tqt])
                S.act(lambda e, tq=tq, i=i: e.copy(qT[:, :, i * 128:(i + 1) * 128], tq[:]), [tqt], [("qT", i)])
                tk_, tkt = ptk.next()
                S.pe(lambda e, tk_=tk_, r_=r_: e.transpose(tk_[:], r_[:, 512:640], ident[:]), [rt, "ident"], [tkt])
                S.dve(lambda e, tk_=tk_, i=i: e.tensor_copy(kT[:, i * 128:(i + 1) * 128], tk_[:]), [tkt], [("kT", i)])
            S.flush()
        S.es = es
        with ExitStack() as es3:
            S.es = es3
            ps_s = Rot(S, "ps_s", 2, [128, 2, 512], F32, psum=True)
            ps_o = Rot(S, "ps_o", 4, [128, 4, 128], F32, psum=True)
            PT = Rot(S, "PT", 3, [128, 2, 512], BF16)
            rc = Rot(S, "rc", 4, [128, 4], F32)
            groups = [(256 + 512 * g, 512, list(range(NT))) for g in range(4)]
            if need_ctx:
                groups.append((0, 256, [0, 1]))
            for (q0, nq, ktiles) in groups:
                nqt = nq // 128
                for pr in range(4):
                    obufs = [ps_o.next() for g in range(2)]
                    for ki, kc in enumerate(ktiles):
                        s_, st = ps_s.next()
                        for g in range(2):
                            ps = slice(64 * g, 64 * g + 64)
                            S.pe(lambda e, s_=s_, ps=ps, kc=kc, pr=pr, q0=q0, nq=nq, g=g: e.matmul(s_[:, g, 0:nq], kT[ps, kc * 128:(kc + 1) * 128], qT[ps, pr, q0:q0 + nq], start=True, stop=True),
                                 [("kT", kc)] + [("qT", q0 // 128 + t) for t in range(nqt)], [st])
                        p_, ptok = PT.next()
                        S.act(lambda e, p_=p_, s_=s_, nq=nq: e.activation(p_[:, :, 0:nq], s_[:, :, 0:nq], AF.Exp, scale=0.125), [st], [ptok])
                        for g in range(2):
                            o_, ot = obufs[g]
                            for t in range(nqt):
                                S.pe(lambda e, o_=o_, p_=p_, t=t, kc=kc, g=g, ki=ki, nk=len(ktiles): e.matmul(o_[:, t, 0:65], p_[:, g, t * 128:(t + 1) * 128], vaug[:, kc, g, :],
                                                                                                            start=(ki == 0 and t == 0), stop=(ki == nk - 1), skip_group_check=True),
                                     [ptok, ("vaug", kc)], [ot])
                    for g in range(2):
                        hd = pr + 4 * g
                        o_, ot = obufs[g]
                        r_, rt = rc.next()
                        S.dve(lambda e, r_=r_, o_=o_, nqt=nqt: e.reciprocal(r_[:, 0:nqt], o_[:, 0:nqt, 64]), [ot], [rt])
                        S.dve(lambda e, r_=r_, o_=o_, nqt=nqt, q0=q0, hd=hd: e.tensor_tensor(btok[:, q0 // 128:q0 // 128 + nqt, hd * 64:(hd + 1) * 64], o_[:, 0:nqt, 0:64],
                                                                                            bc(r_[:, 0:nqt].unsqueeze(2), [128, nqt, 64]), ALU.mult),
                              [ot, rt], [("btok", q0 // 128 + t) for t in range(nqt)])
            S.flush()
        S.es = es
        with ExitStack() as es4:
            S.es = es4
            ptb = Rot(S, "ptb", 2, [128, 4, 128], BF16, psum=True)
            bo = Rot(S, "bo", 2, [128, 4, 128], BF16)
            for i in range(0 if need_ctx else 2, NT):
                p, pt = ptb.next()
                for c in range(4):
                    S.pe(lambda e, p=p, c=c, i=i: e.transpose(p[:, c, :], btok[:, i, c * 128:(c + 1) * 128], ident[:]), [("btok", i), "ident"], [pt])
                o, ot = bo.next()
                S.act(lambda e, o=o, p=p: e.copy(o[:], p[:]), [pt], [ot])
                S.dma("sp", d["bTs"][:, :, i * 128:(i + 1) * 128], o[:], [ot], [])
            S.flush()


def sin_act(S, out, arg_ps, bcol, fcol, n, rt, wt, tmp):
    S.dve(lambda e: e.tensor_scalar(tmp[:, 0:n], arg_ps, bcol, fcol, ALU.add, ALU.mult), rt, ["sin_t"])
    MAGIC = 12582912.0
    kk = S.sin_kk
    S.dve(lambda e: e.tensor_scalar(kk[0:tmp.shape[0], 0:n], tmp[:, 0:n], 1.0 / (2.0 * math.pi), MAGIC, ALU.mult, ALU.add), ["sin_t"], ["sin_k"])
    S.dve(lambda e: e.tensor_scalar(kk[0:tmp.shape[0], 0:n], kk[0:tmp.shape[0], 0:n], -MAGIC, -2.0 * math.pi, ALU.add, ALU.mult), ["sin_k"], ["sin_k"])
    S.dve(lambda e: e.tensor_tensor(tmp[:, 0:n], tmp[:, 0:n], kk[0:tmp.shape[0], 0:n], ALU.add), ["sin_t", "sin_k"], ["sin_t"])
    S.dve(lambda e: e.tensor_scalar(tmp[:, 0:n], tmp[:, 0:n], -3.1415925, 3.1415925, ALU.max, ALU.min), ["sin_t"], ["sin_t"])
    S.act(lambda e: e.activation(out, tmp[:, 0:n], AF.Sin), ["sin_t"], [wt])


def phase_filters(S, nc, d, l, which):
    Ls, tag = (L, "x") if which == "x" else (LC, "c")
    nj = Ls // 128
    with ExitStack() as es:
        S.es = es
        w1 = S.sb("w1", [33, 64], F32)
        wi = S.sb("wi", [64, 2, 64], F32)
        wl = S.sb("wl", [64, 1024], F32)
        b1 = S.sb("b1", [64, 2], F32)
        bi = S.sb("bi", [64, 2, 2], F32)
        fq = S.sb("fq", [64, 2], F32)
        zT = S.sb("zT", [33, Ls], F32)
        h3 = S.sb("h3", [64, Ls], F32)
        Af = S.sb("Af", [128, nj, 512], BF16)
        Bf = S.sb("Bf", [128, nj, 512], BF16)
        S.dma("sp", w1[:], d["hyw1"][l], [], ["w"])
        S.dma("sp", wi[:], d["hywi"][l].rearrange("j i o -> i j o"), [], ["w"])
        S.dma("sp", wl[:], d["hywl"][l], [], ["w"])
        S.dma("sp", b1[:], d["hyb1"], [], ["w"])
        S.dma("sp", bi[:], d["hybi"], [], ["w"])
        S.dma("sp", fq[:], d["hyfreq"], [], ["w"])
        S.dma("sp", zT[:], d["zT" + tag], [], ["zT"])
        pm = Rot(S, "pm", 2, [64, 512], F32, psum=True)
        pf = Rot(S, "pf", 2, [128, 512], F32, psum=True)
        ha = Rot(S, "ha", 2, [64, 512], F32)
        hb = Rot(S, "hb", 2, [64, 512], F32)
        tmp = S.sb("tmpf", [64, 512], F32)
        S.sin_kk = S.sb("sinkk", [64, 512], F32)
        for t0 in range(0, Ls, 512):
            n = min(512, Ls - t0)
            p, pt = pm.next()
            S.pe(lambda e, p=p, t0=t0, n=n: e.matmul(p[:, 0:n], w1[:], zT[:, t0:t0 + n], start=True, stop=True), ["w", "zT"], [pt])
            a, at = ha.next()
            sin_act(S, a[:, 0:n], p[:, 0:n], b1[:, l:l + 1], fq[:, l:l + 1], n, [pt, "w"], at, tmp)
            p, pt = pm.next()
            S.pe(lambda e, p=p, a=a, n=n: e.matmul(p[:, 0:n], wi[:, 0, :], a[:, 0:n], start=True, stop=True), ["w", at], [pt])
            b_, bt = hb.next()
            sin_act(S, b_[:, 0:n], p[:, 0:n], bi[:, l, 0:1], fq[:, l:l + 1], n, [pt, "w"], bt, tmp)
            p, pt = pm.next()
            S.pe(lambda e, p=p, b_=b_, n=n: e.matmul(p[:, 0:n], wi[:, 1, :], b_[:, 0:n], start=True, stop=True), ["w", bt], [pt])
            sin_act(S, h3[:, t0:t0 + n], p[:, 0:n], bi[:, l, 1:2], fq[:, l:l + 1], n, [pt, "w"], "h3", tmp)
        dec = Rot(S, "dec", 2, [128, 512], F32)
        hf = Rot(S, "hf", 2, [128, 512], F32)
        hbk = Rot(S, "hbk", 2, [128, 512], F32)
        for j in range(nj):
            dc, dct = dec.next()
            S.dma("sp", dc[:], d["decay" + tag][j * 128:(j + 1) * 128, :], [], [dct])
            p1, p1t = pf.next()
            S.pe(lambda e, p1=p1, j=j: e.matmul(p1[:], h3[:, j * 128:(j + 1) * 128], wl[:, 0:512], start=True, stop=True), ["h3", "w"], [p1t])
            p2, p2t = pf.next()
            S.pe(lambda e, p2=p2, j=j: e.matmul(p2[:], h3[:, j * 128:(j + 1) * 128], wl[:, 512:1024], start=True, stop=True), ["h3", "w"], [p2t])
            f_, ft = hf.next()
            S.dve(lambda e, f_=f_, p1=p1, dc=dc: e.tensor_tensor(f_[:], p1[:], dc[:], ALU.mult), [p1t, dct], [ft])
            g_, gt = hbk.next()
            S.dve(lambda e, g_=g_, p2=p2, dc=dc: e.tensor_tensor(g_[:], p2[:], dc[:], ALU.mult), [p2t, dct], [gt])
            S.pool(lambda e, f_=f_, g_=g_, j=j: e.tensor_tensor(Af[:, j, :], f_[:], g_[:], ALU.add), [ft, gt], ["Af"])
            S.pool(lambda e, f_=f_, g_=g_, j=j: e.tensor_tensor(Bf[:, j, :], f_[:], g_[:], ALU.subtract), [ft, gt], ["Bf"])
        fw = Rot(S, "fw", 2, [128, nj, 128], BF16)
        ko = Rot(S, "ko", 2, [128, 512], F32)
        nf = Ls // 128
        for part, src in ((0, Af), (1, Bf)):
            for fc in range(nf):
                fcg = part * nf + fc
                w_, wt = fw.next()
                S.dma("sp", w_[:], d["fwd" + tag][:, :, fcg * 128:(fcg + 1) * 128], [], [wt])
                p, pt = pf.next()
                for j in range(nj):
                    S.pe(lambda e, p=p, w_=w_, j=j, src=src: e.matmul(p[:], w_[:, j, :], src[:, j, :], start=(j == 0), stop=(j == nj - 1)),
                         [wt, "Af", "Bf"], [pt])
                o, ot = ko.next()
                S.act(lambda e, o=o, p=p: e.copy(o[:], p[:]), [pt], [ot])
                dst = d["kspx"][l, :, fcg, :] if which == "x" else d["kspc"][:, fcg, :]
                S.dma("sp", dst, o[:], [ot], [])
        S.flush()


def phase_hyena(S, nc, d, l, hT, cst, need_ctx):
    ident = cst["ident"]
    with ExitStack() as es:
        S.es = es
        x0T = S.sb("x0T", [128, 4, T], BF16)
        uT = S.sb("uT", [128, 4, T], BF16)
        cw = S.sb("cw", [128, 12, 3], F32)
        cb = S.sb("cb", [128, 12], F32)
        hbias = S.sb("hbias", [128, 4], F32)
        S.dma("sp", cw[:], d["hycw"][:, l], [], ["cw"])
        S.dma("sp", cb[:], d["hycb"][:, l], [], ["cw"])
        S.dma("sp", hbias[:], d["hybias"][:, l], [], ["cw"])
        utok = S.sb("utok", [128, NT, 512], BF16)
        with ExitStack() as es2:
            S.es = es2
            stg = Rot(S, "stg", 2, [128, 2048], F32)
            wh = Rot(S, "wh", 2, [128, 8, 128], BF16)
            pp = Rot(S, "pp", 3, [128, 512], F32, psum=True)
            pz = Rot(S, "pz", 2, [128, T], F32)
            zc = Rot(S, "zc", 2, [128, T], F32)
            x1c = S.sb("x1c", [128, 4, T], BF16)
            segs = [(0, LC), (LC, T)]
            for ch in range(12):
                w_, wt = wh.next()
                load_cast(S, stg, w_[:], d["w_in"][l, :, :, C_HY + ch * 128:C_HY + (ch + 1) * 128], 8, 128, [wt], "pool")
                z_, zt = pz.next()
                for (t0, n) in TG:
                    p, pt = proj_fm(S, pp, w_, wt, hT, 0, t0, n)
                    S.act(lambda e, z_=z_, p=p, t0=t0, n=n: e.copy(z_[:, t0:t0 + n], p[:, 0:n]), [pt], [zt])
                c_, ct = zc.next()
                S.dve(lambda e, c_=c_, z_=z_, ch=ch: e.tensor_scalar(c_[:], z_[:], cw[:, ch, 1:2], cb[:, ch:ch + 1], ALU.mult, ALU.add), [zt, "cw"], [ct])
                for (a0, a1) in segs:
                    S.dve(lambda e, c_=c_, z_=z_, ch=ch, a0=a0, a1=a1: e.scalar_tensor_tensor(c_[:, a0 + 1:a1], z_[:, a0:a1 - 1], cw[:, ch, 0:1], c_[:, a0 + 1:a1], ALU.mult, ALU.add),
                          [zt, ct, "cw"], [ct])
                    S.dve(lambda e, c_=c_, z_=z_, ch=ch, a0=a0, a1=a1: e.scalar_tensor_tensor(c_[:, a0:a1 - 1], z_[:, a0 + 1:a1], cw[:, ch, 2:3], c_[:, a0:a1 - 1], ALU.mult, ALU.add),
                          [zt, ct, "cw"], [ct])
                sec, cc = ch // 4, ch % 4
                if sec == 0:
                    S.pool(lambda e, c_=c_, cc=cc: e.tensor_copy(x0T[:, cc, :], c_[:]), [ct], [("x0T", cc)])
                elif sec == 1:
                    S.pool(lambda e, c_=c_, cc=cc: e.tensor_copy(x1c[:, cc, :], c_[:]), [ct], [("x1c", cc)])
                else:
                    S.pool(lambda e, c_=c_, cc=cc: e.tensor_tensor(uT[:, cc, :], c_[:], x1c[:, cc, :], ALU.mult), [ct, ("x1c", cc)], [("uT", cc)])
            ptu = Rot(S, "ptu", 2, [128, 4, 128], BF16, psum=True)
            for i in range(NT):
                p, pt = ptu.next()
                for cc in range(4):
                    S.pe(lambda e, p=p, cc=cc, i=i: e.transpose(p[:, cc, :], uT[:, cc, i * 128:(i + 1) * 128], ident[:]), [("uT", cc), "ident"], [pt])
                S.act(lambda e, p=p, i=i: e.copy(utok[:, i, :], p[:].rearrange("p c t -> p (c t)")), [pt], [("utok", i)])
            S.flush()
        for (which, tb, Ls) in ((("c", 0, LC),) if need_ctx else ()) + (("x", LC, L),):
            nj = Ls // 128
            nf = Ls // 128
            i0 = tb // 128
            with ExitStack() as es3:
                S.es = es3
                YT = S.sb("YT", [128, 2 * nf, 512], BF16)
                fw = Rot(S, "fw", 3, [128, nj, 128], BF16)
                kr = Rot(S, "kr", 2, [128, 2, 512], F32)
                pu = Rot(S, "pu", 4, [128, 512], F32, psum=True)
                ur = Rot(S, "ur", 2, [128, 2, 512], F32)
                t1 = Rot(S, "t1", 2, [128, 512], F32)
                t2 = Rot(S, "t2", 2, [128, 512], F32)
                for fc in range(nf):
                    pr_ = []
                    for part in range(2):
                        fcg = part * nf + fc
                        w_, wt = fw.next()
                        S.dma("sp", w_[:], d["fwd" + which][:, :, fcg * 128:(fcg + 1) * 128], [], [wt])
                        p, pt = pu.next()
                        for j in range(nj):
                            S.pe(lambda e, p=p, w_=w_, j=j: e.matmul(p[:], w_[:, j, :], utok[:, i0 + j, :], start=(j == 0), stop=(j == nj - 1)),
                                 [wt, ("utok", i0 + j)], [pt])
                        pr_.append((p, pt))
                    k_, kt = kr.next()
                    for part in range(2):
                        src = d["kspx"][l, :, part * nf + fc, :] if which == "x" else d["kspc"][:, part * nf + fc, :]
                        S.dma("sp", k_[:, part, :], src, [], [kt])
                    u_, ut = ur.next()
                    S.act(lambda e, u_=u_, p=pr_[0][0]: e.copy(u_[:, 0, :], p[:]), [pr_[0][1]], [ut])
                    S.act(lambda e, u_=u_, p=pr_[1][0]: e.copy(u_[:, 1, :], p[:]), [pr_[1][1]], [ut])
                    a, at = t1.next()
                    b_, bt = t2.next()
                    S.dve(lambda e, a=a, u_=u_, k_=k_: e.tensor_tensor(a[:], u_[:, 0, :], k_[:, 0, :], ALU.mult), [ut, kt], [at])
                    S.pool(lambda e, b_=b_, u_=u_, k_=k_: e.tensor_tensor(b_[:], u_[:, 1, :], k_[:, 1, :], ALU.mult), [ut, kt], [bt])
                    S.dve(lambda e, a=a, b_=b_, fc=fc: e.tensor_tensor(YT[:, fc, :], a[:], b_[:], ALU.subtract), [at, bt], [("YT", fc)])
                    a2, a2t = t1.next()
                    b2, b2t = t2.next()
                    S.dve(lambda e, a2=a2, u_=u_, k_=k_: e.tensor_tensor(a2[:], u_[:, 0, :], k_[:, 1, :], ALU.mult), [ut, kt], [a2t])
                    S.pool(lambda e, b2=b2, u_=u_, k_=k_: e.tensor_tensor(b2[:], u_[:, 1, :], k_[:, 0, :], ALU.mult), [ut, kt], [b2t])
                    S.pool(lambda e, a2=a2, b2=b2, fc=fc, nf=nf: e.tensor_tensor(YT[:, nf + fc, :], a2[:], b2[:], ALU.add), [a2t, b2t], [("YT", nf + fc)])
                iv = Rot(S, "iv", 2, [128, 2 * nf, 256], BF16)
                py = Rot(S, "py", 3, [128, 256], F32, psum=True)
                y1 = Rot(S, "y1", 2, [128, 256], F32)
                co = Rot(S, "co", 2, [128, 256], BF16)
                for tg in range(Ls // 256):
                    v_, vt = iv.next()
                    S.dma("sp", v_[:], d["inv" + which][:, :, tg * 256:(tg + 1) * 256], [], [vt])
                    ta = tb + tg * 256
                    for cc in range(4):
                        p, pt = py.next()
                        for f in range(2 * nf):
                            S.pe(lambda e, p=p, v_=v_, f=f, cc=cc: e.matmul(p[:], YT[:, f, cc * 128:(cc + 1) * 128], v_[:, f, :], start=(f == 0), stop=(f == 2 * nf - 1)),
                                 [("YT", f), vt], [pt])
                        y_, yt = y1.next()
                        S.dve(lambda e, y_=y_, p=p, cc=cc, ta=ta: e.scalar_tensor_tensor(y_[:], uT[:, cc, ta:ta + 256], hbias[:, cc:cc + 1], p[:], ALU.mult, ALU.add),
                              [pt, ("uT", cc), "cw"], [yt])
                        o, ot = co.next()
                        S.pool(lambda e, o=o, y_=y_, cc=cc, ta=ta: e.tensor_tensor(o[:], y_[:], x0T[:, cc, ta:ta + 256], ALU.mult), [yt, ("x0T", cc)], [ot])
                        S.dma("sp", d["cTs"][:, cc, ta:ta + 256], o[:], [ot], [])
                S.flush()
            S.es = es


def post_residual(S, d, l, b, sub, ns, i, ypair, ytoks, src, dst, bufs):
    xt, junk, ss, rs, tmpb, al = bufs
    who = 1 if i < 2 else 0
    x, xtok = xt.next()
    S.dma("sp", x[:], src(i), [], [xtok])
    s1, s1t = ss.next()
    for hlf in range(2):
        S.act(lambda e, hlf=hlf, s1=s1: e.activation(junk[:], ypair[hlf][:], AF.Square, accum_out=s1[:, hlf:hlf + 1]), [ytoks[hlf]], ["junk", s1t])
    r1, r1t = rs.next()
    S.dve(lambda e, r1=r1, s1=s1: e.tensor_tensor(r1[:], s1[:, 0:1], s1[:, 1:2], ALU.add), [s1t], [r1t])
    rstd_ops(S, r1[:], r1[:], 1.0 / D, [r1t], [r1t])
    t_, tt = tmpb.next()
    for hlf in range(2):
        S.dve(lambda e, hlf=hlf, t_=t_, r1=r1, who=who: e.scalar_tensor_tensor(t_[:, hlf * 512:(hlf + 1) * 512], ypair[hlf][:], r1[:], al[:, who, hlf * 512:(hlf + 1) * 512], ALU.mult, ALU.mult),
              [ytoks[hlf], r1t, "al"], [tt])
    S.dve(lambda e, t_=t_, x=x: e.tensor_tensor(t_[:], t_[:], x[:], ALU.add), [tt, xtok], [tt])
    S.dma("sp", dst(i), t_[:], [tt], [])


def phase_merge(S, nc, d, l, b, ns, hT, tiles, src, dst):
    with ExitStack() as es:
        S.es = es
        stg = Rot(S, "stg", 4, [128, 2048], F32)
        wo = S.sb("wo", [128, 3, 4, 1024], BF16)
        wout = S.sb("wout", [128, 8, 1024], BF16)
        for bi_, nm in enumerate(("w_oa", "w_ob", "w_oc")):
            load_cast(S, stg, wo[:, bi_, :, :], d[nm][l], 4, 1024, ["wo"], "pool")
        load_cast(S, stg, wout[:], d["w_out"][l], 8, 1024, ["wout"], "pool")
        al = S.sb("al", [128, 2, D], F32)
        S.dma("sp", al[:, 0, :], d["modrep"][l, b, :, 2 * D:3 * D], [], ["al"])
        S.dma("sp", al[:, 1, :], d["modrep"][l, ns, :, 2 * D:3 * D], [], ["al"])
        wg = Rot(S, "wg", 4, [128, 8, 128], BF16)
        br = Rot(S, "br", 2, [128, 3, 4, 512], BF16)
        mT = Rot(S, "mT", 2, [128, 8, 512], BF16)
        pg = Rot(S, "pg", 2, [128, 512], F32, psum=True)
        py = Rot(S, "py", 2, [128, 512], F32, psum=True)
        po = Rot(S, "po", 4, [128, 512], F32, psum=True)
        sg = Rot(S, "sg", 2, [128, 512], F32)
        acc = Rot(S, "acc", 2, [128, 512], F32)
        tm = Rot(S, "tm", 2, [128, 512], F32)
        bufs = (Rot(S, "xt", 2, [128, D], F32), S.sb("junk", [128, 512], F32), Rot(S, "ss", 2, [128, 2], F32),
                Rot(S, "rs", 2, [128, 1], F32), Rot(S, "tmpb", 2, [128, D], F32), al)
        t_lo = min(tiles) * 128
        groups = [(t0, n) for (t0, n) in TG if t0 + n > t_lo]
        if t_lo > 0:
            groups = [(256, 256)] + [(512, 512), (1024, 512), (1536, 512), (2048, 256)]
        for (t0, n) in groups:
            b_, bt = br.next()
            for bi_, nm in enumerate(("aTs", "bTs", "cTs")):
                S.dma("sp", b_[:, bi_, :, 0:n], d[nm][:, :, t0:t0 + n], [], [bt])
            m_, mt = mT.next()
            for oc in range(8):
                a_, at = acc.next()
                for bi_ in range(3):
                    w_, wt = wg.next()
                    c0 = C_GT + bi_ * D + oc * 128
                    load_cast(S, stg, w_[:], d["w_in"][l, :, :, c0:c0 + 128], 8, 128, [wt], "pool")
                    g_, gt = proj_fm(S, pg, w_, wt, hT, 0, t0, n)
                    s_, st = sg.next()
                    S.act(lambda e, s_=s_, g_=g_, n=n: e.activation(s_[:, 0:n], g_[:, 0:n], AF.Sigmoid), [gt], [st])
                    y_, yt = py.next()
                    for k in range(4):
                        S.pe(lambda e, y_=y_, k=k, bi_=bi_, oc=oc, b_=b_, n=n: e.matmul(y_[:, 0:n], wo[:, bi_, k, oc * 128:(oc + 1) * 128], b_[:, bi_, k, 0:n], start=(k == 0), stop=(k == 3)),
                             ["wo", bt], [yt])
                    if bi_ == 0:
                        S.dve(lambda e, a_=a_, s_=s_, y_=y_, n=n: e.tensor_tensor(a_[:, 0:n], s_[:, 0:n], y_[:, 0:n], ALU.mult), [st, yt], [at])
                    else:
                        t_, tt = tm.next()
                        S.dve(lambda e, t_=t_, s_=s_, y_=y_, n=n: e.tensor_tensor(t_[:, 0:n], s_[:, 0:n], y_[:, 0:n], ALU.mult), [st, yt], [tt])
                        S.pool(lambda e, a_=a_, t_=t_, n=n: e.tensor_tensor(a_[:, 0:n], a_[:, 0:n], t_[:, 0:n], ALU.add), [at, tt], [at])
                S.act(lambda e, m_=m_, a_=a_, oc=oc, n=n: e.copy(m_[:, oc, 0:n], a_[:, 0:n]), [at], [mt])
            for ti in range(n // 128):
                i = t0 // 128 + ti
                ys, yts = [], []
                for hlf in range(2):
                    o_, ot = po.next()
                    for k in range(8):
                        S.pe(lambda e, o_=o_, k=k, ti=ti, hlf=hlf, m_=m_: e.matmul(o_[:], m_[:, k, ti * 128:(ti + 1) * 128], wout[:, k, hlf * 512:(hlf + 1) * 512], start=(k == 0), stop=(k == 7)),
                             [mt, "wout"], [ot])
                    ys.append(o_)
                    yts.append(ot)
                post_residual(S, d, l, b, 0, ns, i, ys, yts, src, dst, bufs)
        S.flush()


def phase_ffn(S, nc, d, l, b, ns, norm_fn, tiles, src, dst):
    t_lo = min(tiles) * 128
    segs = [(0, LC), (LC, T)] if t_lo == 0 else [(LC, T)]
    groups = [(t0, n) for (t0, n) in TG] if t_lo == 0 else [(256, 256), (512, 512), (1024, 512), (1536, 512), (2048, 256)]
    with ExitStack() as es:
        S.es = es
        gT = S.sb("gT", [128, 22, T], BF16)
        with ExitStack() as es2:
            S.es = es2
            hT = S.sb("hT2", [128, 8, T], BF16)
            norm_fn(hT)
            S.es = es2
            stg = Rot(S, "stg", 2, [128, 2048], F32)
            wu = Rot(S, "wu", 4, [128, 8, 128], BF16)
            pp = Rot(S, "pp", 3, [128, 512], F32, psum=True)
            ua = Rot(S, "ua", 2, [128, T], F32)
            uc = Rot(S, "uc", 2, [128, T], F32)
            fcw = S.sb("fcw", [128, 44, 3], F32)
            fcb = S.sb("fcb", [128, 44], F32)
            S.dma("sp", fcw[:], d["ffcw"][:, l], [], ["fcw"])
            S.dma("sp", fcb[:], d["ffcb"][:, l], [], ["fcw"])
            for j in range(22):
                res = []
                for half in range(2):
                    ch = half * 22 + j
                    w_, wt = wu.next()
                    load_cast(S, stg, w_[:], d["w_up"][l, :, :, ch * 128:(ch + 1) * 128], 8, 128, [wt], "pool")
                    z_, zt = ua.next()
                    for (t0, n) in groups:
                        p, pt = proj_fm(S, pp, w_, wt, hT, 0, t0, n)
                        S.act(lambda e, z_=z_, p=p, t0=t0, n=n: e.copy(z_[:, t0:t0 + n], p[:, 0:n]), [pt], [zt])
                    c_, ct = uc.next()
                    S.dve(lambda e, c_=c_, z_=z_, ch=ch: e.tensor_scalar(c_[:, t_lo:T], z_[:, t_lo:T], fcw[:, ch, 1:2], fcb[:, ch:ch + 1], ALU.mult, ALU.add), [zt, "fcw"], [ct])
                    for (a0, a1) in segs:
                        S.dve(lambda e, c_=c_, z_=z_, ch=ch, a0=a0, a1=a1: e.scalar_tensor_tensor(c_[:, a0 + 1:a1], z_[:, a0:a1 - 1], fcw[:, ch, 0:1], c_[:, a0 + 1:a1], ALU.mult, ALU.add),
                              [zt, ct, "fcw"], [ct])
                        S.dve(lambda e, c_=c_, z_=z_, ch=ch, a0=a0, a1=a1: e.scalar_tensor_tensor(c_[:, a0:a1 - 1], z_[:, a0 + 1:a1], fcw[:, ch, 2:3], c_[:, a0:a1 - 1], ALU.mult, ALU.add),
                              [zt, ct, "fcw"], [ct])
                    res.append((c_, ct))
                (ca, cat), (cb_, cbt) = res
                S.act(lambda e, ca=ca: e.activation(ca[:, t_lo:T], ca[:, t_lo:T], AF.Silu), [cat], [cat])
                S.pool(lambda e, ca=ca, cb_=cb_, j=j: e.tensor_tensor(gT[:, j, t_lo:T], ca[:, t_lo:T], cb_[:, t_lo:T], ALU.mult), [cat, cbt], [("gT", j)])
            S.flush()
        S.es = es
        with ExitStack() as es3:
            S.es = es3
            stg = Rot(S, "stg", 2, [128, 2048], F32)
            wd = S.sb("wd", [128, 22, 1024], BF16)
            load_cast(S, stg, wd[:], d["w_down"][l], 22, 1024, ["wd"], "pool")
            al = S.sb("al", [128, 2, D], F32)
            S.dma("sp", al[:, 0, :], d["modrep"][l, b, :, 5 * D:6 * D], [], ["al"])
            S.dma("sp", al[:, 1, :], d["modrep"][l, ns, :, 5 * D:6 * D], [], ["al"])
            po = Rot(S, "po", 4, [128, 512], F32, psum=True)
            bufs = (Rot(S, "xt", 2, [128, D], F32), S.sb("junk", [128, 512], F32), Rot(S, "ss", 2, [128, 2], F32),
                    Rot(S, "rs", 2, [128, 1], F32), Rot(S, "tmpb", 2, [128, D], F32), al)
            for i in tiles:
                ys, yts = [], []
                for hlf in range(2):
                    o_, ot = po.next()
                    for k in range(22):
                        S.pe(lambda e, o_=o_, k=k, i=i, hlf=hlf: e.matmul(o_[:], gT[:, k, i * 128:(i + 1) * 128], wd[:, k, hlf * 512:(hlf + 1) * 512], start=(k == 0), stop=(k == 21)),
                             ["wd"], [ot])
                    ys.append(o_)
                    yts.append(ot)
                post_residual(S, d, l, b, 1, ns, i, ys, yts, src, dst, bufs)
            S.flush()


def host_layout_shared(inp):
    m = host_layout(inp, 1, 0)
    m.pop("xin")
    m.pop("cT")
    return m


def host_layout_core(inp, ns, core):
    b0 = core * ns
    m = {}
    m["xin"] = np.ascontiguousarray(np.concatenate([inp["ctx"][b0:b0 + ns], inp["x"][b0:b0 + ns]], axis=1))
    cc = np.concatenate([inp["c"][b0:b0 + ns], inp["c_ctx"][None, :]], axis=0)
    m["cT"] = np.ascontiguousarray(cc.reshape(ns + 1, 8, 128).transpose(2, 1, 0))
    return m


def build_program(ns, consts, layout, stop_after=None):
    nc = bass.Bass("TRN2", target_bir_lowering=False)
    d = declare(nc, ns, consts, layout)
    with ExitStack() as es0:
        S = Sched(nc, es0)
        S.es = es0
        cst = {}
        cst["ident"] = S.sb("ident", [128, 128], BF16)
        cst["ones"] = S.sb("ones", [128, 128], BF16)
        cst["hgmask"] = S.sb("hgmask", [128, 2, 512], BF16)
        cst["rowmask"] = S.sb("rowmask", [128, 4], F32)
        cst["resetmask"] = S.sb("resetmask", [128, 512], F32)
        S.epsT = S.sb("epsT", [128, 1], F32)
        S.mpi = S.sb("mpi", [128, 1], F32)
        S.dma("sp", cst["ident"][:], d["ident"], [], [])
        S.dma("sp", cst["ones"][:], d["ones"], [], [])
        S.dma("sp", cst["hgmask"][:], d["hgmask"].rearrange("d p n -> p d n"), [], [])
        S.dma("sp", cst["rowmask"][:], d["rowmask"], [], [])
        S.dma("sp", cst["resetmask"][:], d["resetmask"], [], [])
        S.dve(lambda e: e.memset(S.epsT[:], EPS), [], [])
        S.dve(lambda e: e.memset(S.mpi[:], -math.pi), [], [])
        S.flush()
        phase_mod(S, nc, d, ns)
        for l in range(2):
            phase_filters(S, nc, d, l, "x")
        phase_filters(S, nc, d, 0, "c")
        import os
        STOP = int(os.environ.get("KSTOP", "99"))
        for l in range(2):
            need_ctx = (l == 0)
            if STOP < 2:
                break
            for b in range(ns):
                src0 = (lambda i, b=b: d["xin"][b, i * 128:(i + 1) * 128, :]) if l == 0 else (lambda i, b=b: d["xs"][b, i * 128:(i + 1) * 128, :])
                xs_t = lambda i, b=b: d["xs"][b, i * 128:(i + 1) * 128, :]
                tiles = list(range(NT)) if need_ctx else list(range(2, NT))
                with ExitStack() as esh:
                    S.es = esh
                    hT = S.sb("hT", [128, 8, T], BF16)
                    phase_norm(S, nc, d, l, b, 0, src0, list(range(NT)), hT, cst["ident"], ns)
                    if STOP >= 3:
                        phase_hgrn(S, nc, d, l, hT, cst)
                    if STOP >= 4:
                        phase_attn(S, nc, d, l, hT, cst, need_ctx)
                    if STOP >= 5:
                        phase_hyena(S, nc, d, l, hT, cst, need_ctx)
                    if STOP >= 6:
                        phase_merge(S, nc, d, l, b, ns, hT, tiles, src0, xs_t)
                if STOP < 7:
                    continue
                if l == 0:
                    dst = xs_t
                else:
                    dst = lambda i, b=b: d["out"][b, (i - 2) * 128:(i - 1) * 128, :]
                norm_fn = lambda hT2, l=l, b=b, tiles=tiles, xs_t=xs_t: phase_norm(S, nc, d, l, b, 1, xs_t, tiles, hT2, cst["ident"], ns)
                phase_ffn(S, nc, d, l, b, ns, norm_fn, tiles, xs_t, dst)
        S.es = es0
        S.flush()
    return nc


NS = 4


def kernel(**inputs):
    inp = {k: np.asarray(v, dtype=np.float32) for k, v in inputs.items()}
    consts = host_constants()
    shared = host_layout_shared(inp)
    cores = [host_layout_core(inp, NS, c) for c in range(NCORES)]
    layout = dict(shared)
    layout.update(cores[0])
    nc = build_program(NS, consts, layout)
    in_maps = []
    for c in range(NCORES):
        m = dict(consts)
        m.update(shared)
        m.update(cores[c])
        in_maps.append(m)
    res = run_bass_kernel_spmd(nc, in_maps, core_ids=list(range(NCORES)))
    out = np.concatenate([np.asarray(r["out"], dtype=np.float32) for r in res.results], axis=0)
    return out
```
